# Optimizing a Trainium2 kernel written in Bass

```python
import math
import jax, jax.numpy as jnp
from jax import lax
import numpy as np

D_MODEL = 4096
BATCH = 2
SEQ = 4096
DEPTH = 2

N_A_LAYERS = DEPTH // 2
N_B_LAYERS = DEPTH - N_A_LAYERS
SSM_W = D_MODEL // 2
SSM_GROUP = 16
SSM_GROUPS = SSM_W // SSM_GROUP
SSM_STATE = 64
SB_HEAD_DIM = 128
SB_HEADS = SSM_W // SB_HEAD_DIM
SB_W = SB_HEADS * SB_HEAD_DIM
MEM_TOKENS = 256
MEM_HEADS = 4
MEM_HEAD_DIM = D_MODEL // 16
MEM_W = MEM_HEADS * MEM_HEAD_DIM
MIX_W = SSM_W + MEM_W
D_FF = 256 * (-(-8 * D_MODEL // (3 * 256)))
CONV_W = 3
Q_BLOCK = 128
EPS = 1e-6

kernel_name = "yoco_s5_stickbreaking_memory_hybrid"


def _rms(x, g):
    x32 = x.astype(jnp.float32)
    y = x32 * lax.rsqrt(jnp.mean(x32 * x32, axis=-1, keepdims=True) + EPS)
    return (y * g.astype(jnp.float32)).astype(x.dtype)


def s5_mixer(u, lam_re, lam_im, log_step, b_re, b_im, c_re, c_im, d_skip, w_glu):
    f32 = jnp.float32
    bsz, seq, _ = u.shape
    u32 = u.astype(f32).reshape(bsz, seq, SSM_GROUPS, SSM_GROUP)
    lr = jnp.minimum(lam_re.astype(f32), -1e-4)
    li = lam_im.astype(f32)
    step = jnp.exp(log_step.astype(f32))[:, None]
    mag = jnp.exp(lr * step)
    ab_re = mag * jnp.cos(li * step)
    ab_im = mag * jnp.sin(li * step)
    p_re = ab_re - 1.0
    den = lr * lr + li * li
    f_re = (p_re * lr + ab_im * li) / den
    f_im = (ab_im * lr - p_re * li) / den
    br = b_re.astype(f32)
    bi = b_im.astype(f32)
    bb_re = f_re[..., None] * br - f_im[..., None] * bi
    bb_im = f_re[..., None] * bi + f_im[..., None] * br
    bu_re = jnp.einsum("bsgh,gph->bsgp", u32, bb_re)
    bu_im = jnp.einsum("bsgh,gph->bsgp", u32, bb_im)
    a_re = jnp.broadcast_to(ab_re, bu_re.shape)
    a_im = jnp.broadcast_to(ab_im, bu_im.shape)

    def combine(left, right):
        a1r, a1i, b1r, b1i = left
        a2r, a2i, b2r, b2i = right
        return (a2r * a1r - a2i * a1i,
                a2r * a1i + a2i * a1r,
                a2r * b1r - a2i * b1i + b2r,
                a2r * b1i + a2i * b1r + b2i)

    _, _, xr, xi = lax.associative_scan(combine, (a_re, a_im, bu_re, bu_im), axis=1)
    y = (jnp.einsum("bsgp,ghp->bsgh", xr, c_re.astype(f32))
         - jnp.einsum("bsgp,ghp->bsgh", xi, c_im.astype(f32))
         + d_skip.astype(f32) * u32)
    g = jax.nn.gelu(y.reshape(bsz, seq, SSM_W))
    out = g * jax.nn.sigmoid(g @ w_glu.astype(f32))
    return out.astype(u.dtype)


def stick_breaking(q, k, v):
    bsz, seq, nh, hd = q.shape
    scale = 1.0 / math.sqrt(hd)
    outs = []
    for i in range(seq // Q_BLOCK):
        t0 = i * Q_BLOCK
        t1 = t0 + Q_BLOCK
        qb = q[:, t0:t1]
        kb = k[:, :t1]
        vb = v[:, :t1]
        z = jnp.einsum("bqhd,bkhd->bhqk", qb, kb).astype(jnp.float32) * scale
        tpos = t0 + jnp.arange(Q_BLOCK)[:, None]
        spos = jnp.arange(t1)[None, :]
        causal = spos < tpos
        log_1mb = jnp.where(causal, jax.nn.log_sigmoid(-z), 0.0)
        log_rem = lax.cumsum(log_1mb, axis=3, reverse=True) - log_1mb
        w = jnp.where(causal, jnp.exp(jax.nn.log_sigmoid(z) + log_rem), 0.0)
        outs.append(jnp.einsum("bhqk,bkhd->bqhd", w.astype(v.dtype), vb))
    return jnp.concatenate(outs, axis=1).reshape(bsz, seq, nh * hd)


def mem_attention(qm, mem_n, w_kv, gq, gk):
    bsz, seq, _ = qm.shape
    q = _rms(qm.reshape(bsz, seq, MEM_HEADS, MEM_HEAD_DIM), gq)
    kv = mem_n @ w_kv
    k, v = jnp.split(kv, 2, axis=-1)
    k = _rms(k.reshape(bsz, -1, MEM_HEADS, MEM_HEAD_DIM), gk)
    v = v.reshape(bsz, -1, MEM_HEADS, MEM_HEAD_DIM)
    logits = jnp.einsum("bshd,bmhd->bhsm", q, k).astype(jnp.float32) / math.sqrt(MEM_HEAD_DIM)
    p = jax.nn.softmax(logits, axis=-1)
    o = jnp.einsum("bhsm,bmhd->bshd", p.astype(v.dtype), v)
    return o.reshape(bsz, seq, MEM_W)


def conv_ffn(h, w_up, conv_w, conv_b, w_down):
    up = h @ w_up
    up = lax.conv_general_dilated(
        up, conv_w[:, None, :], window_strides=(1,), padding=[(CONV_W - 1, 0)],
        dimension_numbers=("NWC", "WIO", "NWC"), feature_group_count=up.shape[-1]) + conv_b
    a, b = jnp.split(up, 2, axis=-1)
    return (jax.nn.silu(a) * b) @ w_down


def setup_inputs(seed: int = 0) -> dict:
    key = jax.random.key(seed)
    ks = jax.random.split(key, 26)
    f32 = jnp.float32

    def nrm(k, shape, scale):
        return jax.random.normal(k, shape, f32) * scale

    def gain(k, shape):
        return 1.0 + 0.02 * jax.random.normal(k, shape, f32)

    n_idx = jnp.arange(SSM_STATE, dtype=f32)
    sshape = (N_A_LAYERS, SSM_GROUPS, SSM_STATE)
    return {
        "x": nrm(ks[0], (BATCH, SEQ, D_MODEL), 1.0),
        "mem": nrm(ks[1], (BATCH, MEM_TOKENS, D_MODEL), 1.0),
        "norm_mix_g": gain(ks[2], (DEPTH, D_MODEL)),
        "w_in": nrm(ks[3], (DEPTH, D_MODEL, MIX_W), D_MODEL ** -0.5),
        "w_out": nrm(ks[4], (DEPTH, MIX_W, D_MODEL), MIX_W ** -0.5),
        "mem_norm_g": gain(ks[5], (DEPTH, D_MODEL)),
        "w_mem_kv": nrm(ks[6], (DEPTH, D_MODEL, 2 * MEM_W), D_MODEL ** -0.5),
        "mem_q_norm_g": gain(ks[7], (DEPTH, MEM_HEAD_DIM)),
        "mem_k_norm_g": gain(ks[8], (DEPTH, MEM_HEAD_DIM)),
        "s5_lam_re": -0.5 + 0.01 * jax.random.normal(ks[9], sshape, f32),
        "s5_lam_im": math.pi * n_idx + 0.01 * jax.random.normal(ks[10], sshape, f32),
        "s5_log_step": jax.random.uniform(ks[11], (N_A_LAYERS, SSM_GROUPS), f32,
                                          math.log(1e-3), math.log(1e-1)),
        "s5_b_re": nrm(ks[12], (N_A_LAYERS, SSM_GROUPS, SSM_STATE, SSM_GROUP), (2 * SSM_GROUP) ** -0.5),
        "s5_b_im": nrm(ks[13], (N_A_LAYERS, SSM_GROUPS, SSM_STATE, SSM_GROUP), (2 * SSM_GROUP) ** -0.5),
        "s5_c_re": nrm(ks[14], (N_A_LAYERS, SSM_GROUPS, SSM_GROUP, SSM_STATE), (2 * SSM_STATE) ** -0.5),
        "s5_c_im": nrm(ks[15], (N_A_LAYERS, SSM_GROUPS, SSM_GROUP, SSM_STATE), (2 * SSM_STATE) ** -0.5),
        "s5_d": nrm(ks[16], (N_A_LAYERS, SSM_GROUPS, SSM_GROUP), 1.0),
        "s5_w_glu": nrm(ks[17], (N_A_LAYERS, SSM_W, SSM_W), SSM_W ** -0.5),
        "kv_norm_g": gain(ks[18], (D_MODEL,)),
        "w_kv_shared": nrm(ks[19], (D_MODEL, 2 * SB_W), D_MODEL ** -0.5),
        "norm_ffn_g": gain(ks[20], (DEPTH, D_MODEL)),
        "w_ffn_up": nrm(ks[21], (DEPTH, D_MODEL, 2 * D_FF), D_MODEL ** -0.5),
        "ffn_conv_w": nrm(ks[22], (DEPTH, CONV_W, 2 * D_FF), CONV_W ** -0.5),
        "ffn_conv_b": nrm(ks[23], (DEPTH, 2 * D_FF), 0.02),
        "w_ffn_down": nrm(ks[24], (DEPTH, D_FF, D_MODEL), D_FF ** -0.5),
    }


def reference(x, mem, norm_mix_g, w_in, w_out, mem_norm_g, w_mem_kv, mem_q_norm_g,
              mem_k_norm_g, s5_lam_re, s5_lam_im, s5_log_step, s5_b_re, s5_b_im,
              s5_c_re, s5_c_im, s5_d, s5_w_glu, kv_norm_g, w_kv_shared, norm_ffn_g,
              w_ffn_up, ffn_conv_w, ffn_conv_b, w_ffn_down):
    bsz, seq, _ = x.shape
    h = x
    k_sh = None
    v_sh = None
    for i in range(DEPTH):
        proj = _rms(h, norm_mix_g[i]) @ w_in[i]
        prim = proj[..., :SSM_W]
        qm = proj[..., SSM_W:]
        m_out = mem_attention(qm, _rms(mem, mem_norm_g[i]), w_mem_kv[i],
                              mem_q_norm_g[i], mem_k_norm_g[i])
        if i < N_A_LAYERS:
            p_out = s5_mixer(prim, s5_lam_re[i], s5_lam_im[i], s5_log_step[i],
                             s5_b_re[i], s5_b_im[i], s5_c_re[i], s5_c_im[i],
                             s5_d[i], s5_w_glu[i])
        else:
            q = prim.reshape(bsz, seq, SB_HEADS, SB_HEAD_DIM)
            p_out = stick_breaking(q, k_sh, v_sh)
        h = h + jnp.concatenate([p_out, m_out], axis=-1) @ w_out[i]
        h = h + conv_ffn(_rms(h, norm_ffn_g[i]), w_ffn_up[i], ffn_conv_w[i],
                         ffn_conv_b[i], w_ffn_down[i])
        if i == N_A_LAYERS - 1:
            kv = _rms(h, kv_norm_g) @ w_kv_shared
            k_sh, v_sh = jnp.split(kv, 2, axis=-1)
            k_sh = k_sh.reshape(bsz, seq, SB_HEADS, SB_HEAD_DIM)
            v_sh = v_sh.reshape(bsz, seq, SB_HEADS, SB_HEAD_DIM)
    return h
```

```python
import math
import numpy as np
import ml_dtypes
import concourse.bass as bass
import concourse.mybir as mybir
from concourse.bass_utils import run_bass_kernel_spmd

F32 = mybir.dt.float32
BF16 = mybir.dt.bfloat16
I32 = mybir.dt.int32
AF = mybir.ActivationFunctionType
ALU = mybir.AluOpType

SEM_LIMIT = 16000
TT = 512


class Cfg:
    def __init__(self, D=4096, SEQ=4096, B=2, SSM_W=2048, MEM_HEADS=4, MEM_TOKENS=256, D_FF=11008):
        self.D = D
        self.SEQ = SEQ
        self.B = B
        self.SSM_W = SSM_W
        self.G = SSM_W // 16
        self.NPAIR = self.G // 2
        self.SB_HEADS = SSM_W // 128
        self.MEM_HEADS = MEM_HEADS
        self.MEM_W = MEM_HEADS * 256
        self.MIX_W = SSM_W + self.MEM_W
        self.MEM_TOKENS = MEM_TOKENS
        self.D_FF = D_FF
        self.KC = D // 128
        self.NT = SEQ // TT
        self.FC = D_FF // 128


FULL = Cfg()


class Sem:
    def __init__(self, nc, name):
        self.h = nc.alloc_semaphore(name)
        self.v = 0


class Buf:
    __slots__ = ("name", "w", "r", "dsem")

    def __init__(self, name=""):
        self.name = name
        self.w = []
        self.r = {}
        self.dsem = None


class Stream:
    def __init__(self, P, name, eng):
        self.P = P
        self.name = name
        self.eng = eng
        self.sems = []
        self.sem = None
        self.waited = {}
        self.insts = []

    def cur_sem(self):
        if self.sem is None or self.sem.v >= SEM_LIMIT:
            self.sem = Sem(self.P.nc, "e%s%d" % (self.name, len(self.sems)))
            self.sems.append(self.sem)
        return self.sem


class Prog:
    def __init__(self, nc):
        self.nc = nc
        self.st = {
            "pe": Stream(self, "pe", nc.tensor),
            "act": Stream(self, "act", nc.scalar),
            "dve": Stream(self, "dve", nc.vector),
            "pool": Stream(self, "pool", nc.gpsimd),
            "sp": Stream(self, "sp", nc.sync),
        }
        self.dma_pool = []
        self.dma_live = []
        self.ndsem = 0
        self.rr = 0

    def _deps(self, reads, writes):
        d = {}
        for b in reads:
            for (s, v) in b.w:
                if d.get(s, 0) < v:
                    d[s] = v
        for b in writes:
            for (s, v) in b.w:
                if d.get(s, 0) < v:
                    d[s] = v
            for s, v in b.r.items():
                if d.get(s, 0) < v:
                    d[s] = v
        return d

    def _waits(self, S, d):
        waits = []
        for s, v in d.items():
            if S.name == "pe" and s in S.sems:
                continue
            if S.waited.get(s, 0) >= v:
                continue
            S.waited[s] = v
            waits.append((s, v))
        return waits

    def _mark(self, tok, reads, writes):
        s, v = tok
        for b in reads:
            if b.r.get(s, 0) < v:
                b.r[s] = v
        for b in writes:
            b.w = [tok]
            b.r = {}

    def op(self, st, fn, reads=(), writes=()):
        S = self.st[st]
        waits = self._waits(S, self._deps(reads, writes))
        sem = S.cur_sem()
        sem.v += 1
        tok = (sem, sem.v)
        S.insts.append((waits, fn, sem, 1))
        self._mark(tok, reads, writes)
        return tok

    def _dsem(self, owner):
        if owner.dsem is None or owner.dsem.v >= SEM_LIMIT:
            if self.dma_pool:
                owner.dsem = self.dma_pool.pop()
            else:
                owner.dsem = Sem(self.nc, "d%d" % self.ndsem)
                self.ndsem += 1
            self.dma_live.append(owner.dsem)
        return owner.dsem

    def dma(self, out, in_, reads, writes, owner, st="sp"):
        S = self.st[st]
        waits = self._waits(S, self._deps(reads, writes))
        sem = self._dsem(owner)
        sem.v += 16
        tok = (sem, sem.v)

        def fn(e, out=out, in_=in_):
            return e.dma_start(out=out, in_=in_)

        S.insts.append((waits, fn, sem, 16))
        self._mark(tok, reads, writes)
        return tok

    def barrier(self):
        toks = {}
        for S in self.st.values():
            if S.sem is not None and S.sem.v > 0:
                toks[S.sem] = S.sem.v
        for s in self.dma_live:
            if s.v > 0:
                toks[s] = s.v
        for S in self.st.values():
            waits = self._waits(S, dict(toks))
            if waits:
                S.insts.append((waits, None, None, 0))
        for s in self.dma_live:
            if s.v < SEM_LIMIT and s not in self.dma_pool:
                self.dma_pool.append(s)
        self.dma_live = []

    def emit(self):
        nc = self.nc
        with nc.Block() as block:
            def run(S):
                def body(e):
                    for (waits, fn, sem, inc) in S.insts:
                        for (s, v) in waits:
                            e.wait_ge(s.h, v)
                        if fn is not None:
                            ins = fn(e)
                            ins.then_inc(sem.h, inc)
                return body
            block.tensor(run(self.st["pe"]))
            block.scalar(run(self.st["act"]))
            block.vector(run(self.st["dve"]))
            block.gpsimd(run(self.st["pool"]))
            block.sync(run(self.st["sp"]))


class Alloc:
    def __init__(self, t, ncols):
        self.t = t
        self.n = ncols
        self.off = 0

    def take(self, units):
        a = self.off
        al = (units + 15) // 16 * 16
        assert a + al <= self.n, ("sbuf pool overflow", a, units, self.n)
        self.off += al
        return a


class Pool2:
    def __init__(self, al, bf):
        self.al = al
        self.bf = bf

    @property
    def off(self):
        return self.al.off

    @off.setter
    def off(self, v):
        self.al.off = v

    def alloc(self, ncols):
        if self.bf:
            units = (ncols + 1) // 2
            a = self.al.take(units)
            return self.al.t[:, a:a + units].bitcast(BF16)[:, 0:ncols]
        a = self.al.take(ncols)
        return self.al.t[:, a:a + ncols]


class K:
    def __init__(self, cfg):
        self.cfg = cfg
        self.nc = bass.Bass("TRN2", target_bir_lowering=False)
        self.P = Prog(self.nc)
        self.dr = {}
        self.cast_rr = 0

    def din(self, name, shape, dt=F32):
        t = self.nc.dram_tensor(name, list(shape), dt, kind="ExternalInput").ap()
        self.dr[name] = t
        return t

    def dscr(self, name, shape, dt=F32):
        t = self.nc.dram_tensor(name, list(shape), dt, kind="Internal").ap()
        self.dr[name] = t
        return t

    def dout(self, name, shape, dt=F32):
        t = self.nc.dram_tensor(name, list(shape), dt, kind="ExternalOutput").ap()
        self.dr[name] = t
        return t

    def act(self, out, in_, func, reads, writes, scale=None, bias=None):
        kw = {}
        if scale is not None:
            kw["scale"] = scale
        if bias is not None:
            kw["bias"] = bias
        return self.P.op("act", lambda e: e.activation(out=out, in_=in_, func=func, **kw), reads, writes)

    def tt(self, st, out, in0, in1, op, reads, writes):
        return self.P.op(st, lambda e: e.tensor_tensor(out=out, in0=in0, in1=in1, op=op), reads, writes)

    def ts(self, st, out, in0, s1, op0, reads, writes, s2=None, op1=None):
        if op1 is None:
            return self.P.op(st, lambda e: e.tensor_scalar(out=out, in0=in0, scalar1=s1, scalar2=None, op0=op0),
                             reads, writes)
        return self.P.op(st, lambda e: e.tensor_scalar(out=out, in0=in0, scalar1=s1, scalar2=s2, op0=op0, op1=op1),
                         reads, writes)

    def stt(self, out, in0, scalar, in1, op0, op1, reads, writes):
        return self.P.op("dve", lambda e: e.scalar_tensor_tensor(out=out, in0=in0, scalar=scalar, in1=in1,
                                                                 op0=op0, op1=op1), reads, writes)

    def copy(self, st, out, in_, reads, writes):
        if st == "act":
            return self.P.op("act", lambda e: e.activation(out=out, in_=in_, func=AF.Copy), reads, writes)
        return self.P.op(st, lambda e: e.tensor_copy(out=out, in_=in_), reads, writes)

    def cast_any(self, out, in_, reads, writes, engines=("act", "dve", "pool")):
        st = engines[self.cast_rr % len(engines)]
        self.cast_rr += 1
        return self.copy(st, out, in_, reads, writes)

    def mm(self, items, reads, writes):
        def fn(e):
            ins = None
            for (o, l, r, st, sp) in items:
                ins = e.matmul(o, l, r, start=st, stop=sp)
            return ins
        return self.P.op("pe", fn, reads, writes)


def build(cfg, debug_outs=()):
    kb = K(cfg)
    nc = kb.nc
    P = kb.P
    D, SEQ, KC, NT, FC = cfg.D, cfg.SEQ, cfg.KC, cfg.NT, cfg.FC
    SSM_W, MEM_W, MIX_W = cfg.SSM_W, cfg.MEM_W, cfg.MIX_W
    SC = SSM_W // 128
    MC = MEM_W // 128
    XC = MIX_W // 128
    MT = cfg.MEM_TOKENS
    NPAIR = cfg.NPAIR
    H = cfg.SB_HEADS

    xT = kb.din("xT", [D, SEQ])
    memT = kb.din("memT", [D, MT])
    gains = kb.din("gains", [128, 7 * KC])
    qkg = kb.din("qkg", [128, 8])
    convp = kb.din("convp", [128, 2 * 2 * FC * 4])
    s5rep = kb.din("s5rep", [128, 3 * NPAIR * 128])
    s5par = kb.din("s5par", [128, 3 * NPAIR])
    s5B = kb.din("s5B", [128, 2 * NPAIR * 128])
    s5C = kb.din("s5C", [128, 2 * NPAIR * 128])
    s5D = kb.din("s5D", [128, SC])
    consts = kb.din("consts", [128, 128 + 128 + 4 * TT])

    def wspec(name, Kd, ntiles, nblk):
        kc = Kd // 128
        w32 = kb.din(name, [ntiles * 128, kc * nblk])
        w16 = kb.dscr(name + "_bf", [ntiles * 128, kc * nblk], BF16)
        return dict(name=name, w32=w32, w16=w16, kc=kc, ntiles=ntiles, nblk=nblk)

    W = {}
    for l in range(2):
        W["w_in%d" % l] = wspec("w_in%d" % l, D, MIX_W // 256, 256)
        W["w_out%d" % l] = wspec("w_out%d" % l, MIX_W, D // 256, 256)
        W["w_mkv%d" % l] = wspec("w_mkv%d" % l, D, 2 * MEM_W // 256, 256)
        W["w_up%d" % l] = wspec("w_up%d" % l, D, FC, 256)
        W["w_dn%d" % l] = wspec("w_dn%d" % l, cfg.D_FF, D // 128, 128)
    W["w_glu"] = wspec("w_glu", SSM_W, SSM_W // 256, 256)
    W["w_kv"] = wspec("w_kv", D, 2 * SSM_W // 256, 256)

    HA = kb.dscr("HA", [D, SEQ])
    HB = kb.dscr("HB", [D, SEQ])
    PROJ = kb.dscr("PROJ", [MIX_W, SEQ])
    MIX = kb.dscr("MIX", [MIX_W, SEQ])
    GS5 = kb.dscr("GS5", [SSM_W, SEQ])
    HID = kb.dscr("HID", [cfg.D_FF, SEQ], BF16)
    KT = kb.dscr("KTs", [SSM_W, SEQ], BF16)
    VTM = kb.dscr("VTM", [SEQ, SSM_W], BF16)
    yT = kb.dout("yT", [D, SEQ])
    dbg = {}
    for nm in debug_outs:
        src = kb.dr[nm]
        dbg[nm] = kb.dout("dbg_" + nm, list(src.shape), src.dtype)

    import contextlib
    es = contextlib.ExitStack()
    NALL = 53000
    big = es.enter_context(nc.sbuf_tensor("big", [128, NALL], F32))
    al = Alloc(big, NALL)
    p32 = Pool2(al, False)
    p16 = Pool2(al, True)
    banks = [es.enter_context(nc.psum_tensor("bank%d" % i, [128, 512], F32))[:, :] for i in range(8)]
    bankbuf = [Buf("bank%d" % i) for i in range(8)]

    cst32 = p32.alloc(128 + 128 + 4 * TT)
    cstb = Buf("cst")
    P.dma(cst32, consts, [], [cstb], cstb)
    TRI32 = cst32[:, 0:128]
    ONES32 = cst32[:, 128:256]
    MASK = [cst32[:, 256 + i * TT:256 + (i + 1) * TT] for i in range(4)]
    ones16 = p16.alloc(128)
    kb.copy("dve", ones16, ONES32, [cstb], [cstb])
    TRI16 = p16.alloc(128)
    kb.copy("dve", TRI16, TRI32, [cstb], [cstb])
    gn32 = p32.alloc(7 * KC)
    P.dma(gn32, gains, [], [cstb], cstb)
    qk32 = p32.alloc(8)
    P.dma(qk32, qkg, [], [cstb], cstb)
    base32 = p32.off

    def new_phase():
        P.barrier()
        p32.off = base32
        for b in bankbuf:
            b.w = []
            b.r = {}

    CAST_FIRST = ("w_in0", "w_mkv0")
    BG_ORDER = ["w_glu", "w_out0", "w_up0", "w_dn0", "w_kv", "w_in1", "w_mkv1", "w_out1", "w_up1", "w_dn1"]
    BGP = 1024
    bgq = []
    for name in BG_ORDER:
        w = W[name]
        rows = w["ntiles"] * 128
        cols = w["kc"] * w["nblk"]
        for r0 in range(0, rows, 128):
            for c0 in range(0, cols, BGP):
                bgq.append((w, r0, c0, min(cols, c0 + BGP)))
    bgstate = dict(i=0, bufs=None)

    def bg_setup():
        NBG = 3
        bgstate["bufs"] = ([p32.alloc(BGP) for _ in range(NBG)], [p16.alloc(BGP) for _ in range(NBG)],
                           [Buf() for _ in range(NBG)], [Buf() for _ in range(NBG)])

    def bg_step(n):
        st32, st16, b32, b16 = bgstate["bufs"]
        for _ in range(n):
            if not bgq:
                return
            w, r0, c0, c1 = bgq.pop(0)
            k = bgstate["i"] % len(st32)
            bgstate["i"] += 1
            m = c1 - c0
            P.dma(st32[k][:, 0:m], w["w32"][r0:r0 + 128, c0:c1], [], [b32[k]], b32[k], st="sp")
            kb.copy("act", st16[k][:, 0:m], st32[k][:, 0:m], [b32[k]], [b16[k]])
            P.dma(w["w16"][r0:r0 + 128, c0:c1], st16[k][:, 0:m], [b16[k]], [], b16[k], st="act")

    def cast_weights():
        new_phase()
        PIECE = 4096
        NB = 6
        st32 = [p32.alloc(PIECE) for _ in range(NB)]
        st16 = [p16.alloc(PIECE) for _ in range(NB)]
        b32 = [Buf() for _ in range(NB)]
        b16 = [Buf() for _ in range(NB)]
        i = 0
        for name, w in W.items():
            if name not in CAST_FIRST:
                continue
            rows = w["ntiles"] * 128
            cols = w["kc"] * w["nblk"]
            for r0 in range(0, rows, 128):
                for c0 in range(0, cols, PIECE):
                    c1 = min(cols, c0 + PIECE)
                    n = c1 - c0
                    k = i % NB
                    i += 1
                    P.dma(st32[k][:, 0:n], w["w32"][r0:r0 + 128, c0:c1], [], [b32[k]], b32[k])
                    kb.cast_any(st16[k][:, 0:n], st32[k][:, 0:n], [b32[k]], [b16[k]], engines=("act", "dve"))
                    P.dma(w["w16"][r0:r0 + 128, c0:c1], st16[k][:, 0:n], [b16[k]], [], b16[k], st="pool")


    def load_act_tile(src, Kc, t0, tw, a32, abuf):
        v = src.rearrange("(k p) t -> p k t", p=128)
        a3 = a32.rearrange("p (k t) -> p k t", k=Kc)
        for k0 in range(0, Kc, 8):
            k1 = min(Kc, k0 + 8)
            P.dma(a3[:, k0:k1, :], v[:, k0:k1, t0:t0 + tw], [], [abuf], abuf)

    def rms_tile(a32, abuf, Kc, tw, gcol, xn, xnbuf, sq, sqbuf, rstd, rbuf, bank, eps_scale):
        items = []
        for kc in range(Kc):
            k2 = kc % 2
            kb.act(sq[k2][:, 0:tw], a32[:, kc * tw:(kc + 1) * tw], AF.Square, [abuf], [sqbuf[k2]])
            kb.mm([(banks[bank][:, 0:tw], ONES32, sq[k2][:, 0:tw], kc == 0, kc == Kc - 1)],
                  [sqbuf[k2], cstb], [bankbuf[bank]])
        kb.act(rstd[:, 0:tw], banks[bank][:, 0:tw], AF.Sqrt, [bankbuf[bank], cstb], [rbuf],
               scale=1.0 / (Kc * 128), bias=epsb[:, 0:1])
        P.op("dve", lambda e: e.reciprocal(out=rstd[:, 0:tw], in_=rstd[:, 0:tw]), [rbuf], [rbuf])
        for kc in range(Kc):
            kb.stt(xn[:, kc * tw:(kc + 1) * tw], a32[:, kc * tw:(kc + 1) * tw], gn32[:, gcol + kc:gcol + kc + 1],
                   rstd[:, 0:tw], ALU.mult, ALU.mult, [abuf, rbuf, cstb], [xnbuf])

    epsb = p32.alloc(1)
    P.op("dve", lambda e: e.memset(epsb, 1e-6), [], [cstb])
    base32 = p32.off

    class WStream:
        def __init__(self, maxcols):
            self.bufs = [p16.alloc(maxcols) for _ in range(3)]
            self.bb = [Buf() for _ in range(3)]
            self.i = 0
            self.q = []

        def prefetch(self, w, ti):
            k = self.i % 3
            self.i += 1
            cols = w["kc"] * w["nblk"]
            P.dma(self.bufs[k][:, 0:cols], w["w16"][ti * 128:(ti + 1) * 128, :], [], [self.bb[k]], self.bb[k])
            self.q.append((self.bufs[k], self.bb[k]))

        def pop(self):
            return self.q.pop(0)

    def gemm_pass(w, xn, xnbuf, tw, ws, evac, tiles=None, pre=2):
        tiles = list(range(w["ntiles"])) if tiles is None else tiles
        kc_n = w["kc"]
        nb = w["nblk"]
        for j in range(min(pre, len(tiles))):
            ws.prefetch(w, tiles[j])
        bi = 0
        for idx, ti in enumerate(tiles):
            if idx + pre < len(tiles):
                ws.prefetch(w, tiles[idx + pre])
            wt, wb = ws.pop()
            for c in range(nb // 128):
                bank = gemm_pass.bank_rr % 4
                gemm_pass.bank_rr += 1
                items = []
                for kc in range(kc_n):
                    items.append((banks[bank][:, 0:tw], wt[:, kc * nb + c * 128: kc * nb + (c + 1) * 128],
                                  xn[:, kc * tw:(kc + 1) * tw], kc == 0, kc == kc_n - 1))
                kb.mm(items, [wb, xnbuf], [bankbuf[bank]])
                evac(ti, c, bank)
    gemm_pass.bank_rr = 0

    def store_tile(dst, r0, t0, tw, src, sbuf, st="sp"):
        P.dma(dst[r0:r0 + 128, t0:t0 + tw], src, [sbuf], [], sbuf, st=st)

    def phase_inproj(l, hsrc):
        new_phase()
        w = W["w_in%d" % l]
        a32 = p32.alloc(KC * TT)
        abuf = Buf()
        xn = p16.alloc(KC * TT)
        xnbuf = Buf()
        sq = [p32.alloc(TT) for _ in range(2)]
        sqb = [Buf() for _ in range(2)]
        rstd = p32.alloc(TT)
        rb = Buf()
        ws = WStream(w["kc"] * w["nblk"])
        ev = [p32.alloc(TT) for _ in range(4)]
        evb = [Buf() for _ in range(4)]
        cnt = [0]
        for tt in range(NT):
            t0 = tt * TT
            load_act_tile(hsrc, KC, t0, TT, a32, abuf)
            rms_tile(a32, abuf, KC, TT, l * KC, xn, xnbuf, sq, sqb, rstd, rb, 7, None)

            def evac(ti, c, bank, t0=t0):
                k = cnt[0] % 4
                cnt[0] += 1
                kb.copy("act" if k % 2 else "dve", ev[k], banks[bank][:, 0:TT], [bankbuf[bank]], [evb[k]])
                store_tile(PROJ, (ti * 2 + c) * 128, t0, TT, ev[k], evb[k])
            gemm_pass(w, xn, xnbuf, TT, ws, evac)

    def phase_memattn(l):
        new_phase()
        w = W["w_mkv%d" % l]
        m32 = p32.alloc(KC * MT)
        mb = Buf()
        mn = p16.alloc(KC * MT)
        mnb = Buf()
        sq = [p32.alloc(TT) for _ in range(2)]
        sqb = [Buf() for _ in range(2)]
        rstd = p32.alloc(TT)
        rb = Buf()
        load_act_tile(memT, KC, 0, MT, m32, mb)
        rms_tile(m32, mb, KC, MT, (2 + l) * KC, mn, mnb, sq, sqb, rstd, rb, 7, None)
        ws = WStream(w["kc"] * w["nblk"])
        k32 = p32.alloc(MC * MT)
        k32b = Buf()
        kn = p16.alloc(MC * MT)
        knb = Buf()
        MCH = MT // 128
        vtm = p16.alloc(MCH * MEM_W)
        vtb = Buf()
        nkt = MEM_W // 256
        ws_i = 0
        for j in range(min(2, w["ntiles"])):
            ws.prefetch(w, j)
        for ti in range(w["ntiles"]):
            if ti + 2 < w["ntiles"]:
                ws.prefetch(w, ti + 2)
            wt, wb = ws.pop()
            if ti < nkt:
                for c in range(2):
                    bank = c
                    items = [(banks[bank][:, 0:MT], wt[:, kc * 256 + c * 128: kc * 256 + (c + 1) * 128],
                              mn[:, kc * MT:(kc + 1) * MT], kc == 0, kc == KC - 1) for kc in range(KC)]
                    kb.mm(items, [wb, mnb], [bankbuf[bank]])
                    ch = ti * 2 + c
                    kb.copy("act", k32[:, ch * MT:(ch + 1) * MT], banks[bank][:, 0:MT], [bankbuf[bank]], [k32b])
            else:
                d0 = (ti - nkt) * 256
                for mc in range(MCH):
                    bank = 2 + mc % 2
                    items = [(banks[bank][:, 0:256], mn[:, kc * MT + mc * 128: kc * MT + (mc + 1) * 128],
                              wt[:, kc * 256:(kc + 1) * 256], kc == 0, kc == KC - 1) for kc in range(KC)]
                    kb.mm(items, [wb, mnb], [bankbuf[bank]])
                    kb.copy("dve", vtm[:, mc * MEM_W + d0: mc * MEM_W + d0 + 256], banks[bank][:, 0:256],
                            [bankbuf[bank]], [vtb])
        for h in range(cfg.MEM_HEADS):
            for dc in range(2):
                ch = 2 * h + dc
                kb.act(sq[dc][:, 0:MT], k32[:, ch * MT:(ch + 1) * MT], AF.Square, [k32b], [sqb[dc]])
                kb.mm([(banks[7][:, 0:MT], ONES32, sq[dc][:, 0:MT], dc == 0, dc == 1)], [sqb[dc], cstb], [bankbuf[7]])
            kb.act(rstd[:, 0:MT], banks[7][:, 0:MT], AF.Sqrt, [bankbuf[7], cstb], [rb], scale=1.0 / 256, bias=epsb[:, 0:1])
            P.op("dve", lambda e: e.reciprocal(out=rstd[:, 0:MT], in_=rstd[:, 0:MT]), [rb], [rb])
            for dc in range(2):
                ch = 2 * h + dc
                kb.stt(kn[:, ch * MT:(ch + 1) * MT], k32[:, ch * MT:(ch + 1) * MT],
                       qk32[:, 4 * l + 2 + dc:4 * l + 3 + dc], rstd[:, 0:MT], ALU.mult, ALU.mult,
                       [k32b, rb, cstb], [knb])
        q32 = p32.alloc(MC * TT)
        qb = Buf()
        qn = p16.alloc(MC * TT)
        qnb = Buf()
        pT = p16.alloc(MCH * TT)
        pTb = Buf()
        rden = p32.alloc(TT)
        rdb = Buf()
        ev = [p32.alloc(TT) for _ in range(2)]
        evb = [Buf() for _ in range(2)]
        cnt = 0
        for tt in range(NT):
            t0 = tt * TT
            v = PROJ.rearrange("(k p) t -> p k t", p=128)
            P.dma(q32.rearrange("p (k t) -> p k t", k=MC), v[:, SC:SC + MC, t0:t0 + TT], [], [qb], qb)
            for h in range(cfg.MEM_HEADS):
                for dc in range(2):
                    ch = 2 * h + dc
                    kb.act(sq[dc], q32[:, ch * TT:(ch + 1) * TT], AF.Square, [qb], [sqb[dc]])
                    kb.mm([(banks[7], ONES32, sq[dc], dc == 0, dc == 1)], [sqb[dc], cstb], [bankbuf[7]])
                kb.act(rstd, banks[7], AF.Sqrt, [bankbuf[7], cstb], [rb], scale=1.0 / 256, bias=epsb[:, 0:1])
                P.op("dve", lambda e: e.reciprocal(out=rstd, in_=rstd), [rb], [rb])
                for dc in range(2):
                    ch = 2 * h + dc
                    kb.stt(qn[:, ch * TT:(ch + 1) * TT], q32[:, ch * TT:(ch + 1) * TT],
                           qk32[:, 4 * l + dc:4 * l + dc + 1], rstd, ALU.mult, ALU.mult, [qb, rb, cstb], [qnb])
                for mc in range(MCH):
                    bank = mc % 2
                    items = [(banks[bank], kn[:, (2 * h + dc) * MT + mc * 128:(2 * h + dc) * MT + (mc + 1) * 128],
                              qn[:, (2 * h + dc) * TT:(2 * h + dc + 1) * TT], dc == 0, dc == 1) for dc in range(2)]
                    kb.mm(items, [knb, qnb], [bankbuf[bank]])
                    kb.act(pT[:, mc * TT:(mc + 1) * TT], banks[bank], AF.Exp, [bankbuf[bank]], [pTb], scale=1.0 / 16.0)
                items = [(banks[2], ones16, pT[:, mc * TT:(mc + 1) * TT], mc == 0, mc == MCH - 1) for mc in range(MCH)]
                kb.mm(items, [pTb, cstb], [bankbuf[2]])
                P.op("dve", lambda e: e.reciprocal(out=rden, in_=banks[2]), [bankbuf[2]], [rdb])
                for dc in range(2):
                    bank = 3 + dc
                    items = [(banks[bank], vtm[:, mc * MEM_W + h * 256 + dc * 128: mc * MEM_W + h * 256 + (dc + 1) * 128],
                              pT[:, mc * TT:(mc + 1) * TT], mc == 0, mc == MCH - 1) for mc in range(MCH)]
                    kb.mm(items, [vtb, pTb], [bankbuf[bank]])
                    k = cnt % 2
                    cnt += 1
                    kb.tt("dve", ev[k], banks[bank], rden, ALU.mult, [bankbuf[bank], rdb], [evb[k]])
                    store_tile(MIX, SSM_W + h * 256 + dc * 128, t0, TT, ev[k], evb[k])

    def phase_s5():
        new_phase()
        NQ = NPAIR * 128
        NP_ = NPAIR
        LRE = p16.alloc(NQ)
        LIM = p16.alloc(NQ)
        CRE = p16.alloc(NQ)
        CIMN = p16.alloc(NQ)
        Lb = Buf()
        par = p32.alloc(3 * NP_)
        pbuf = Buf()
        P.dma(par, s5par, [], [pbuf], pbuf)
        T2 = [p32.alloc(NP_) for _ in range(6)]
        t2b = Buf()
        scr2 = p32.alloc(NP_)
        scr2i = p32.alloc(NP_).bitcast(I32)
        NLEV = 10
        CLx = [p32.alloc(NP_) for _ in range(NLEV - 1)]
        SLx = [p32.alloc(NP_) for _ in range(NLEV - 1)]
        d32 = p32.alloc(SC)
        mark = p32.off

        def prep(lr_in, li_in, ls_in, T, bufs_in, ob):
            step, lr, mag, th, cs, sn = T
            kb.act(step, ls_in, AF.Exp, bufs_in, [ob])
            kb.ts("dve", lr, lr_in, -1e-4, ALU.min, bufs_in, [ob])
            kb.tt("dve", mag, lr, step, ALU.mult, [ob], [ob])
            kb.act(mag, mag, AF.Exp, [ob], [ob])
            kb.tt("dve", th, li_in, step, ALU.mult, bufs_in + [ob], [ob])
            return step, lr, mag, th, cs, sn

        def sincos(th, cs, sn, ob, scr, scri):
            for (dst, shift) in ((sn, 0.0), (cs, math.pi / 2)):
                kb.ts("dve", scr, th, 1.0 / (2 * math.pi), ALU.mult, [ob], [ob], s2=shift / (2 * math.pi) + 0.5, op1=ALU.add)
                kb.copy("dve", scri, scr, [ob], [ob])
                kb.copy("dve", scr, scri, [ob], [ob])
                kb.ts("dve", scr, scr, -2 * math.pi, ALU.mult, [ob], [ob], s2=shift, op1=ALU.add)
                kb.tt("dve", dst, th, scr, ALU.add, [ob], [ob])
                kb.ts("dve", scr, dst, math.pi, ALU.is_gt, [ob], [ob], s2=-2 * math.pi, op1=ALU.mult)
                kb.tt("dve", dst, dst, scr, ALU.add, [ob], [ob])
                kb.ts("dve", scr, dst, -math.pi, ALU.is_lt, [ob], [ob], s2=2 * math.pi, op1=ALU.mult)
                kb.tt("dve", dst, dst, scr, ALU.add, [ob], [ob])
                kb.act(dst, dst, AF.Sin, [ob], [ob])

        PB = min(8, NPAIR)
        BQ = PB * 128
        rep = p32.alloc(3 * BQ)
        rbuf_ = Buf()
        tmp = [p32.alloc(BQ) for _ in range(6)]
        tb = Buf()
        scr = p32.alloc(BQ)
        scri = p32.alloc(BQ).bitcast(I32)
        Bp = p32.alloc(2 * BQ)
        Bb = Buf()
        Cp = p32.alloc(2 * BQ)
        Cb = Buf()
        rep3 = s5rep.rearrange("p (w q) -> p w q", w=3)
        B3 = s5B.rearrange("p (w q) -> p w q", w=2)
        C3 = s5C.rearrange("p (w q) -> p w q", w=2)
        for blk in range(NPAIR // PB):
            q0 = blk * BQ
            P.dma(rep.rearrange("p (w q) -> p w q", w=3), rep3[:, :, q0:q0 + BQ], [], [rbuf_], rbuf_)
            P.dma(Bp.rearrange("p (w q) -> p w q", w=2), B3[:, :, q0:q0 + BQ], [], [Bb], Bb)
            P.dma(Cp.rearrange("p (w q) -> p w q", w=2), C3[:, :, q0:q0 + BQ], [], [Cb], Cb)
            li = rep[:, BQ:2 * BQ]
            step, lr, mag, th, cs, sn = prep(rep[:, 0:BQ], li, rep[:, 2 * BQ:3 * BQ], tmp, [rbuf_], tb)
            sincos(th, cs, sn, tb, scr, scri)
            kb.tt("dve", cs, cs, mag, ALU.mult, [tb], [tb])
            kb.tt("dve", sn, sn, mag, ALU.mult, [tb], [tb])
            kb.tt("dve", step, lr, lr, ALU.mult, [tb], [tb])
            kb.tt("dve", scr, li, li, ALU.mult, [rbuf_, tb], [tb])
            kb.tt("dve", step, step, scr, ALU.add, [tb], [tb])
            P.op("dve", lambda e, a=step: e.reciprocal(out=a, in_=a), [tb], [tb])
            kb.ts("dve", mag, cs, -1.0, ALU.add, [tb], [tb])
            kb.tt("dve", th, mag, lr, ALU.mult, [tb], [tb])
            kb.tt("dve", scr, sn, li, ALU.mult, [rbuf_, tb], [tb])
            kb.tt("dve", th, th, scr, ALU.add, [tb], [tb])
            kb.tt("dve", th, th, step, ALU.mult, [tb], [tb])
            kb.tt("dve", cs, sn, lr, ALU.mult, [tb], [tb])
            kb.tt("dve", scr, mag, li, ALU.mult, [rbuf_, tb], [tb])
            kb.tt("dve", cs, cs, scr, ALU.subtract, [tb], [tb])
            kb.tt("dve", cs, cs, step, ALU.mult, [tb], [tb])
            f_re, f_im = th, cs
            br, bi = Bp[:, 0:BQ], Bp[:, BQ:2 * BQ]
            kb.tt("dve", mag, f_re, br, ALU.mult, [tb, Bb], [tb])
            kb.tt("dve", scr, f_im, bi, ALU.mult, [tb, Bb], [tb])
            kb.tt("dve", LRE[:, q0:q0 + BQ], mag, scr, ALU.subtract, [tb], [Lb])
            kb.tt("dve", mag, f_re, bi, ALU.mult, [tb, Bb], [tb])
            kb.tt("dve", scr, f_im, br, ALU.mult, [tb, Bb], [tb])
            kb.tt("dve", LIM[:, q0:q0 + BQ], mag, scr, ALU.add, [tb], [Lb])
            kb.copy("dve", CRE[:, q0:q0 + BQ], Cp[:, 0:BQ], [Cb], [Lb])
            kb.ts("dve", CIMN[:, q0:q0 + BQ], Cp[:, BQ:2 * BQ], -1.0, ALU.mult, [Cb], [Lb])
        step2, lr2, rho, th2, c0, s0 = prep(par[:, 0:NP_], par[:, NP_:2 * NP_], par[:, 2 * NP_:3 * NP_], T2, [pbuf], t2b)
        sincos(th2, c0, s0, t2b, scr2, scr2i)
        CL = [c0] + CLx
        SL = [s0] + SLx
        for m in range(1, NLEV):
            kb.tt("dve", CL[m], CL[m - 1], CL[m - 1], ALU.mult, [t2b], [t2b])
            kb.tt("dve", scr2, SL[m - 1], SL[m - 1], ALU.mult, [t2b], [t2b])
            kb.tt("dve", CL[m], CL[m], scr2, ALU.subtract, [t2b], [t2b])
            kb.tt("dve", SL[m], CL[m - 1], SL[m - 1], ALU.mult, [t2b], [t2b])
            kb.ts("dve", SL[m], SL[m], 2.0, ALU.mult, [t2b], [t2b])
        P.dma(d32, s5D, [], [t2b], t2b)
        P.barrier()
        p32.off = mark

        NPC = 4
        Ec = [p32.alloc(TT) for _ in range(NPC)]
        Es = [p32.alloc(TT) for _ in range(NPC)]
        Eb = [Buf() for _ in range(NPC)]
        rhoT = [p32.alloc(TT) for _ in range(NPC)]
        init = [p32.alloc(2) for _ in range(NPC)]
        inb = [Buf() for _ in range(NPC)]
        u32 = [p32.alloc(TT) for _ in range(2)]
        u32b = [Buf() for _ in range(2)]
        u16 = [p16.alloc(TT) for _ in range(2)]
        u16b = [Buf() for _ in range(2)]
        NW = NPC
        NTL = 8
        wk = [[p32.alloc(TT) for _ in range(NTL)] for _ in range(NW)]
        wb2 = [[Buf() for _ in range(NTL)] for _ in range(NW)]
        x16 = [[wk[i_][j_].bitcast(BF16)[:, 0:TT] for j_ in range(2)] for i_ in range(NW)]
        x16b = [[wb2[i_][j_] for j_ in range(2)] for i_ in range(NW)]
        tiny = [p32.alloc(4) for _ in range(NW)]
        tinyb = [Buf() for _ in range(NW)]
        gout = [p32.alloc(TT) for _ in range(2)]
        goutb = [Buf() for _ in range(2)]
        bg_setup()
        nsteps = SC * NT
        bg_per = (len(bgq) + nsteps - 1) // nsteps
        uc = 0
        gc = 0
        bc = 0
        for c in range(SC):
            for pi in range(NPC):
                kb.ts("dve", Ec[pi][:, 0:1], ONES32[:, 0:1], 1.0, ALU.mult, [cstb], [Eb[pi]])
                kb.ts("dve", Es[pi][:, 0:1], ONES32[:, 0:1], 0.0, ALU.mult, [cstb], [Eb[pi]])
            for m in range(NLEV - 1):
                s = 1 << m
                for pi in range(NPC):
                    q = c * NPC + pi
                    sm = SL[m][:, q:q + 1]
                    eng = "pool" if (s >= 64 and pi % 2) else "dve"
                    kb.ts(eng, wk[pi][2][:, 0:s], Es[pi][:, 0:s], sm, ALU.mult, [Eb[pi], t2b], [wb2[pi][2]])
                    kb.ts(eng, wk[pi][3][:, 0:s], Ec[pi][:, 0:s], sm, ALU.mult, [Eb[pi], t2b], [wb2[pi][3]])
                for pi in range(NPC):
                    q = c * NPC + pi
                    cm = CL[m][:, q:q + 1]
                    kb.stt(Ec[pi][:, s:2 * s], Ec[pi][:, 0:s], cm, wk[pi][2][:, 0:s], ALU.mult, ALU.subtract,
                           [Eb[pi], wb2[pi][2], t2b], [Eb[pi]])
                    kb.stt(Es[pi][:, s:2 * s], Es[pi][:, 0:s], cm, wk[pi][3][:, 0:s], ALU.mult, ALU.add,
                           [Eb[pi], wb2[pi][3], t2b], [Eb[pi]])
            for pi in range(NPC):
                q = c * NPC + pi
                kb.act(rhoT[pi], ONES_T, AF.Copy, [cstb, t2b], [Eb[pi]], scale=rho[:, q:q + 1])
                P.op("dve", lambda e, a=init[pi]: e.memset(a, 0.0), [], [inb[pi]])
            for tt in range(NT):
                t0 = tt * TT
                k = uc % 2
                uc += 1
                P.dma(u32[k], PROJ[c * 128:(c + 1) * 128, t0:t0 + TT], [], [u32b[k]], u32b[k], st="act")
                kb.copy("act", u16[k], u32[k], [u32b[k]], [u16b[k]])
                bg_step(bg_per)
                ybank = 6 + (gc % 2)
                R = range(NPC)
                for pi in R:
                    q = c * NPC + pi
                    ba = 2 * (bc % 3)
                    bc += 1
                    kb.mm([(banks[ba], LRE[:, q * 128:(q + 1) * 128], u16[k], True, True)], [Lb, u16b[k]], [bankbuf[ba]])
                    kb.mm([(banks[ba + 1], LIM[:, q * 128:(q + 1) * 128], u16[k], True, True)], [Lb, u16b[k]], [bankbuf[ba + 1]])
                    kb.copy("act", wk[pi][0], banks[ba], [bankbuf[ba]], [wb2[pi][0]])
                    kb.copy("act", wk[pi][1], banks[ba + 1], [bankbuf[ba + 1]], [wb2[pi][1]])
                for pi in R:
                    br_, bi_, t1, t2, t3, t4, wr, wim = wk[pi]
                    Bbr, Bbi, B1, B2, B3, B4, Bwr, Bwi = wb2[pi]
                    kb.tt("dve", t1, Ec[pi], br_, ALU.mult, [Eb[pi], Bbr], [B1])
                    kb.tt("pool", t2, Es[pi], bi_, ALU.mult, [Eb[pi], Bbi], [B2])
                    kb.tt("pool", t3, Ec[pi], bi_, ALU.mult, [Eb[pi], Bbi], [B3])
                    kb.tt("dve", t4, Es[pi], br_, ALU.mult, [Eb[pi], Bbr], [B4])
                for pi in R:
                    br_, bi_, t1, t2, t3, t4, wr, wim = wk[pi]
                    Bbr, Bbi, B1, B2, B3, B4, Bwr, Bwi = wb2[pi]
                    kb.tt("dve", t1, t1, t2, ALU.add, [B1, B2], [B1])
                    kb.tt("pool", t3, t3, t4, ALU.subtract, [B3, B4], [B3])
                for pi in R:
                    br_, bi_, t1, t2, t3, t4, wr, wim = wk[pi]
                    Bbr, Bbi, B1, B2, B3, B4, Bwr, Bwi = wb2[pi]
                    P.op("dve", lambda e, o=wr, a=rhoT[pi], b=t1, i0=init[pi][:, 0:1]:
                         e.tensor_tensor_scan(out=o, data0=a, data1=b, initial=i0, op0=ALU.mult, op1=ALU.add),
                         [Eb[pi], B1, inb[pi]], [Bwr])
                    P.op("dve", lambda e, o=wim, a=rhoT[pi], b=t3, i0=init[pi][:, 1:2]:
                         e.tensor_tensor_scan(out=o, data0=a, data1=b, initial=i0, op0=ALU.mult, op1=ALU.add),
                         [Eb[pi], B3, inb[pi]], [Bwi])
                for pi in R:
                    q = c * NPC + pi
                    br_, bi_, t1, t2, t3, t4, wr, wim = wk[pi]
                    Bbr, Bbi, B1, B2, B3, B4, Bwr, Bwi = wb2[pi]
                    c9 = CL[NLEV - 1][:, q:q + 1]
                    s9 = SL[NLEV - 1][:, q:q + 1]
                    tn = tiny[pi]
                    kb.ts("dve", tn[:, 0:1], wim[:, TT - 1:TT], s9, ALU.mult, [Bwi, t2b], [tinyb[pi]])
                    kb.ts("dve", tn[:, 1:2], wr[:, TT - 1:TT], s9, ALU.mult, [Bwr, t2b], [tinyb[pi]])
                    kb.stt(init[pi][:, 0:1], wr[:, TT - 1:TT], c9, tn[:, 0:1], ALU.mult, ALU.subtract, [Bwr, tinyb[pi], t2b], [inb[pi]])
                    kb.stt(init[pi][:, 1:2], wim[:, TT - 1:TT], c9, tn[:, 1:2], ALU.mult, ALU.add, [Bwi, tinyb[pi], t2b], [inb[pi]])
                for pi in R:
                    br_, bi_, t1, t2, t3, t4, wr, wim = wk[pi]
                    Bbr, Bbi, B1, B2, B3, B4, Bwr, Bwi = wb2[pi]
                    kb.tt("pool", t1, Ec[pi], wr, ALU.mult, [Eb[pi], Bwr], [B1])
                    kb.tt("dve", t2, Es[pi], wim, ALU.mult, [Eb[pi], Bwi], [B2])
                    kb.tt("pool", t4, Ec[pi], wim, ALU.mult, [Eb[pi], Bwi], [B4])
                    kb.tt("dve", t3, Es[pi], wr, ALU.mult, [Eb[pi], Bwr], [B3])
                for pi in R:
                    br_, bi_, t1, t2, t3, t4, wr, wim = wk[pi]
                    Bbr, Bbi, B1, B2, B3, B4, Bwr, Bwi = wb2[pi]
                    kb.tt("pool", x16[pi][0], t1, t2, ALU.subtract, [B1, B2], [x16b[pi][0]])
                    kb.tt("dve", x16[pi][1], t3, t4, ALU.add, [B3, B4], [x16b[pi][1]])
                for pi in R:
                    q = c * NPC + pi
                    kb.mm([(banks[ybank], CRE[:, q * 128:(q + 1) * 128], x16[pi][0], pi == 0, False),
                           (banks[ybank], CIMN[:, q * 128:(q + 1) * 128], x16[pi][1], False, pi == NPC - 1)],
                          [Lb, x16b[pi][0], x16b[pi][1]], [bankbuf[ybank]])
                g = gc % 2
                gc += 1
                kb.stt(gout[g], u32[k], d32[:, c:c + 1], banks[ybank], ALU.mult, ALU.add, [u32b[k], t2b, bankbuf[ybank]], [goutb[g]])
                kb.act(gout[g], gout[g], AF.Gelu, [goutb[g]], [goutb[g]])
                store_tile(GS5, c * 128, t0, TT, gout[g], goutb[g], st="act")
        bg_step(len(bgq))


    ONES_T = None

    def phase_glu():
        new_phase()
        w = W["w_glu"]
        a32 = p32.alloc(SC * TT)
        abuf = Buf()
        xn = p16.alloc(SC * TT)
        xnbuf = Buf()
        ws = WStream(w["kc"] * w["nblk"])
        ev = [p32.alloc(TT) for _ in range(4)]
        evb = [Buf() for _ in range(4)]
        cnt = [0]
        for tt in range(NT):
            t0 = tt * TT
            load_act_tile(GS5, SC, t0, TT, a32, abuf)
            for kc in range(SC):
                kb.cast_any(xn[:, kc * TT:(kc + 1) * TT], a32[:, kc * TT:(kc + 1) * TT], [abuf], [xnbuf], engines=("dve", "pool"))

            def evac(ti, c, bank, t0=t0):
                k = cnt[0] % 4
                cnt[0] += 1
                ch = ti * 2 + c
                kb.act(ev[k], banks[bank], AF.Sigmoid, [bankbuf[bank]], [evb[k]])
                kb.tt("dve", ev[k], ev[k], a32[:, ch * TT:(ch + 1) * TT], ALU.mult, [evb[k], abuf], [evb[k]])
                store_tile(MIX, ch * 128, t0, TT, ev[k], evb[k])
            gemm_pass(w, xn, xnbuf, TT, ws, evac)

    def phase_resid(w, src, src_dt, Kc, hold, hnew):
        new_phase()
        if src_dt == F32:
            a32 = p32.alloc(Kc * TT)
            abuf = Buf()
        xn = p16.alloc(Kc * TT)
        xnbuf = Buf()
        ws = WStream(w["kc"] * w["nblk"])
        hin = [p32.alloc(TT) for _ in range(4)]
        hinb = [Buf() for _ in range(4)]
        cnt = [0]
        nchunks = w["ntiles"] * (w["nblk"] // 128)
        for tt in range(NT):
            t0 = tt * TT
            if src_dt == F32:
                load_act_tile(src, Kc, t0, TT, a32, abuf)
                for kc in range(Kc):
                    kb.cast_any(xn[:, kc * TT:(kc + 1) * TT], a32[:, kc * TT:(kc + 1) * TT], [abuf], [xnbuf])
            else:
                v = src.rearrange("(k p) t -> p k t", p=128)
                x3 = xn.rearrange("p (k t) -> p k t", k=Kc)
                for k0 in range(0, Kc, 8):
                    k1 = min(Kc, k0 + 8)
                    P.dma(x3[:, k0:k1, :], v[:, k0:k1, t0:t0 + TT], [], [xnbuf], xnbuf)
            pend = []

            def evac(ti, c, bank, t0=t0):
                k = cnt[0] % 4
                cnt[0] += 1
                ch = ti * (w["nblk"] // 128) + c
                P.dma(hin[k], hold[ch * 128:(ch + 1) * 128, t0:t0 + TT], [], [hinb[k]], hinb[k])
                kb.tt("dve", hin[k], banks[bank], hin[k], ALU.add, [bankbuf[bank], hinb[k]], [hinb[k]])
                store_tile(hnew, ch * 128, t0, TT, hin[k], hinb[k])
            gemm_pass(w, xn, xnbuf, TT, ws, evac)

    def phase_ffn_up(l, hsrc):
        new_phase()
        w = W["w_up%d" % l]
        a32 = p32.alloc(KC * TT)
        abuf = Buf()
        xn = p16.alloc(KC * TT)
        xnbuf = Buf()
        sq = [p32.alloc(TT) for _ in range(2)]
        sqb = [Buf() for _ in range(2)]
        rstd = p32.alloc(TT)
        rb = Buf()
        cp = p32.alloc(2 * FC * 4)
        cpb = Buf()
        P.dma(cp, convp[:, l * 2 * FC * 4:(l + 1) * 2 * FC * 4], [], [cpb], cpb)
        tails = p32.alloc(2 * FC * 2)
        tlb = Buf()
        P.op("dve", lambda e: e.memset(tails, 0.0), [], [tlb])
        ws = WStream(w["kc"] * w["nblk"])
        NB_ = 2
        ua = [[p32.alloc(TT + 2) for _ in range(2)] for _ in range(NB_)]
        uab = [Buf() for _ in range(NB_)]
        cv = [[p32.alloc(TT) for _ in range(2)] for _ in range(NB_)]
        cvb = [Buf() for _ in range(NB_)]
        hid = [p16.alloc(TT) for _ in range(NB_)]
        hidb = [Buf() for _ in range(NB_)]
        cnt = [0]
        for tt in range(NT):
            t0 = tt * TT
            load_act_tile(hsrc, KC, t0, TT, a32, abuf)
            rms_tile(a32, abuf, KC, TT, (4 + l) * KC, xn, xnbuf, sq, sqb, rstd, rb, 7, None)

            def evac(ti, c, bank, t0=t0):
                k = (cnt[0] // 2) % NB_
                cnt[0] += 1
                u = ua[k][c]
                pc = cp[:, (c * FC + ti) * 4:(c * FC + ti) * 4 + 4]
                tl = tails[:, (c * FC + ti) * 2:(c * FC + ti) * 2 + 2]
                kb.copy("pool", u[:, 0:2], tl, [tlb], [uab[k]])
                kb.copy("act" if c == 0 else "dve", u[:, 2:TT + 2], banks[bank], [bankbuf[bank]], [uab[k]])
                kb.copy("pool", tl, u[:, TT:TT + 2], [uab[k]], [tlb])
                o = cv[k][c]
                kb.act(o, u[:, 2:TT + 2], AF.Identity, [uab[k], cpb], [cvb[k]], scale=pc[:, 2:3], bias=pc[:, 3:4])
                kb.stt(o, u[:, 1:TT + 1], pc[:, 1:2], o, ALU.mult, ALU.add, [uab[k], cpb, cvb[k]], [cvb[k]])
                kb.stt(o, u[:, 0:TT], pc[:, 0:1], o, ALU.mult, ALU.add, [uab[k], cpb, cvb[k]], [cvb[k]])
                if c == 1:
                    kb.act(cv[k][0], cv[k][0], AF.Silu, [cvb[k]], [cvb[k]])
                    kb.tt("dve", hid[k], cv[k][0], cv[k][1], ALU.mult, [cvb[k]], [hidb[k]])
                    store_tile(HID, ti * 128, t0, TT, hid[k], hidb[k])
            gemm_pass(w, xn, xnbuf, TT, ws, evac)

    def phase_kv(hsrc):
        new_phase()
        w = W["w_kv"]
        a32 = p32.alloc(KC * TT)
        abuf = Buf()
        xn = p16.alloc(KC * TT)
        xnbuf = Buf()
        sq = [p32.alloc(TT) for _ in range(2)]
        sqb = [Buf() for _ in range(2)]
        rstd = p32.alloc(TT)
        rb = Buf()
        ws = WStream(w["kc"] * w["nblk"])
        ev = [p16.alloc(TT) for _ in range(4)]
        evb = [Buf() for _ in range(4)]
        cnt = [0]
        nkt = SSM_W // 256
        for tt in range(NT):
            t0 = tt * TT
            load_act_tile(hsrc, KC, t0, TT, a32, abuf)
            rms_tile(a32, abuf, KC, TT, 6 * KC, xn, xnbuf, sq, sqb, rstd, rb, 7, None)

            def evac(ti, c, bank, t0=t0):
                k = cnt[0] % 4
                cnt[0] += 1
                kb.copy("act" if k % 2 else "dve", ev[k], banks[bank], [bankbuf[bank]], [evb[k]])
                store_tile(KT, (ti * 2 + c) * 128, t0, TT, ev[k], evb[k])
            gemm_pass(w, xn, xnbuf, TT, ws, evac, tiles=list(range(nkt)))
            vt = list(range(nkt, 2 * nkt))
            for j in range(min(2, len(vt))):
                ws.prefetch(w, vt[j])
            for idx, ti in enumerate(vt):
                if idx + 2 < len(vt):
                    ws.prefetch(w, vt[idx + 2])
                wt, wb = ws.pop()
                for tc in range(TT // 128):
                    bank = gemm_pass.bank_rr % 4
                    gemm_pass.bank_rr += 1
                    items = [(banks[bank][:, 0:256], xn[:, kc * TT + tc * 128: kc * TT + (tc + 1) * 128],
                              wt[:, kc * 256:(kc + 1) * 256], kc == 0, kc == KC - 1) for kc in range(KC)]
                    kb.mm(items, [wb, xnbuf], [bankbuf[bank]])
                    k = cnt[0] % 4
                    cnt[0] += 1
                    kb.copy("act" if k % 2 else "dve", ev[k][:, 0:256], banks[bank][:, 0:256], [bankbuf[bank]], [evb[k]])
                    P.dma(VTM[t0 + tc * 128: t0 + (tc + 1) * 128, (ti - nkt) * 256:(ti - nkt + 1) * 256], ev[k][:, 0:256],
                          [evb[k]], [], evb[k])

    def phase_sb():
        new_phase()
        NKC = SEQ // 128
        scale = 1.0 / math.sqrt(128.0)
        kT = [p16.alloc(SEQ) for _ in range(2)]
        kTb = [Buf() for _ in range(2)]
        vt = [p16.alloc(NKC * 128) for _ in range(2)]
        vtb = [Buf() for _ in range(2)]
        q32 = [p32.alloc(TT) for _ in range(2)]
        q32b = [Buf() for _ in range(2)]
        q16 = [p16.alloc(TT) for _ in range(3)]
        q16b = [Buf() for _ in range(3)]
        NW = 3
        E = [p32.alloc(TT) for _ in range(NW)]
        Eb_ = [Buf() for _ in range(NW)]
        SP_ = [p32.alloc(TT) for _ in range(NW)]
        SPb = [Buf() for _ in range(NW)]
        X = [p32.alloc(TT) for _ in range(NW)]
        Xb = [Buf() for _ in range(NW)]
        sacc = [p32.alloc(TT) for _ in range(3)]
        saccb = [Buf() for _ in range(3)]
        shi = [p16.alloc(TT) for _ in range(3)]
        shib = [Buf() for _ in range(3)]
        slo = [p16.alloc(TT) for _ in range(3)]
        slob = [Buf() for _ in range(3)]
        hi16 = [p16.alloc(TT) for _ in range(NW)]
        hi16b = [Buf() for _ in range(NW)]
        lo16 = [p16.alloc(TT) for _ in range(NW)]
        lo16b = [Buf() for _ in range(NW)]
        W16 = [p16.alloc(TT) for _ in range(NW)]
        W16b = [Buf() for _ in range(NW)]
        ev = [p32.alloc(TT) for _ in range(2)]
        evb = [Buf() for _ in range(2)]
        tasks = []
        ti_ = 0
        for h in range(H):
            for tt in range(NT):
                nk = 4 * tt + 4
                for kc in range(nk - 1, -1, -1):
                    tasks.append(dict(h=h, tt=tt, kc=kc, first=(kc == nk - 1), last=(kc == 0), dj=kc - 4 * tt,
                                      tile=ti_, w=len(tasks) % NW))
                ti_ += 1
        sidx = [0]

        def S1(t):
            h, tt, kc, w_ = t["h"], t["tt"], t["kc"], t["w"]
            hk = h % 2
            if t["first"]:
                if tt == 0:
                    P.dma(kT[hk], KT[h * 128:(h + 1) * 128, :], [], [kTb[hk]], kTb[hk])
                    v3 = vt[hk].rearrange("p (k d) -> p k d", k=NKC)
                    vs = VTM.rearrange("(k p) d -> p k d", p=128)
                    for k0 in range(0, NKC, 8):
                        k1 = min(NKC, k0 + 8)
                        P.dma(v3[:, k0:k1, :], vs[:, k0:k1, h * 128:(h + 1) * 128], [], [vtb[hk]], vtb[hk])
                k = t["tile"] % 2
                k3 = t["tile"] % 3
                P.dma(q32[k], PROJ[h * 128:(h + 1) * 128, tt * TT:(tt + 1) * TT], [], [q32b[k]], q32b[k])
                kb.copy("pool", q16[k3], q32[k], [q32b[k]], [q16b[k3]])
            k3 = t["tile"] % 3
            zb = w_
            kb.mm([(banks[zb], kT[hk][:, kc * 128:(kc + 1) * 128], q16[k3], True, True)], [kTb[hk], q16b[k3]], [bankbuf[zb]])
            kb.act(E[w_], banks[zb], AF.Exp, [bankbuf[zb]], [Eb_[w_]], scale=scale)
            kb.act(SP_[w_], E[w_], AF.Ln, [Eb_[w_], cstb], [SPb[w_]], bias=onesb[:, 0:1])
            if t["dj"] >= 0:
                kb.tt("dve", SP_[w_], SP_[w_], MASK[t["dj"]], ALU.mult, [SPb[w_], cstb], [SPb[w_]])
            kb.copy("act", hi16[w_], SP_[w_], [SPb[w_]], [hi16b[w_]])
            kb.tt("dve", lo16[w_], SP_[w_], hi16[w_], ALU.subtract, [SPb[w_], hi16b[w_]], [lo16b[w_]])

        def S2(t):
            w_ = t["w"]
            cbk = 3 + w_
            so, sob = sacc[sidx[0] % 3], saccb[sidx[0] % 3]
            sn_, snb = sacc[(sidx[0] + 1) % 3], saccb[(sidx[0] + 1) % 3]
            i0 = sidx[0] % 3
            i1 = (sidx[0] + 1) % 3
            if t["first"]:
                kb.mm([(banks[cbk], TRI16, hi16[w_], True, False), (banks[cbk], TRI16, lo16[w_], False, True)],
                      [hi16b[w_], lo16b[w_], cstb], [bankbuf[cbk]])
            else:
                kb.mm([(banks[cbk], TRI16, hi16[w_], True, False), (banks[cbk], TRI16, lo16[w_], False, False),
                       (banks[cbk], ones16, shi[i0], False, False), (banks[cbk], ones16, slo[i0], False, True)],
                      [hi16b[w_], lo16b[w_], cstb, shib[i0], slob[i0]], [bankbuf[cbk]])
            if not t["last"]:
                if t["first"]:
                    kb.copy("pool", sn_, SP_[w_], [SPb[w_]], [snb])
                else:
                    kb.tt("pool", sn_, so, SP_[w_], ALU.add, [SPb[w_], sob], [snb])
                kb.copy("pool", shi[i1], sn_, [snb], [shib[i1]])
                kb.tt("dve", slo[i1], sn_, shi[i1], ALU.subtract, [snb, shib[i1]], [slob[i1]])
                sidx[0] += 1

        def S3(t):
            h, tt, kc, w_ = t["h"], t["tt"], t["kc"], t["w"]
            hk = h % 2
            cbk = 3 + w_
            obank = 6 + (t["tile"] % 2)
            kb.act(X[w_], banks[cbk], AF.Exp, [bankbuf[cbk]], [Xb[w_]], scale=-1.0)
            if t["dj"] >= 0:
                kb.tt("dve", X[w_], X[w_], MASK[t["dj"]], ALU.mult, [Xb[w_], cstb], [Xb[w_]])
            kb.tt("dve", W16[w_], E[w_], X[w_], ALU.mult, [Eb_[w_], Xb[w_]], [W16b[w_]])
            kb.mm([(banks[obank], vt[hk][:, kc * 128:(kc + 1) * 128], W16[w_], t["first"], t["last"])],
                  [vtb[hk], W16b[w_]], [bankbuf[obank]])
            if t["last"]:
                e_ = t["tile"] % 2
                kb.copy("act", ev[e_], banks[obank], [bankbuf[obank]], [evb[e_]])
                store_tile(MIX, h * 128, tt * TT, TT, ev[e_], evb[e_])

        n = len(tasks)
        for i in range(n + 2):
            if i < n:
                S1(tasks[i])
            if 0 <= i - 1 < n:
                S2(tasks[i - 1])
            if 0 <= i - 2 < n:
                S3(tasks[i - 2])

    def dump(nm):
        new_phase()
        src = kb.dr[nm]
        b = Buf()
        rows = src.shape[0]
        for r0 in range(0, rows, 128):
            P.dma(dbg[nm][r0:r0 + 128, :], src[r0:r0 + 128, :], [], [b], b)

    ONES_T = p32.alloc(TT)
    P.op("dve", lambda e: e.memset(ONES_T, 1.0), [], [cstb])
    onesb = p32.alloc(1)
    P.op("dve", lambda e: e.memset(onesb, 1.0), [], [cstb])
    base32 = p32.off

    stages = cfg.stages if hasattr(cfg, "stages") else None

    def want(s):
        return stages is None or s in stages

    if want("cast"):
        cast_weights()
    with nc.allow_low_precision("bf16 matmuls with fp32 accumulation, as the reference tolerance assumes"):
        if want("in0"):
            phase_inproj(0, xT)
        if want("mem0"):
            phase_memattn(0)
        if want("s5"):
            phase_s5()
        if want("glu"):
            phase_glu()
        if want("out0"):
            phase_resid(W["w_out0"], MIX, F32, XC, xT, HA)
        if want("up0"):
            phase_ffn_up(0, HA)
        if want("dn0"):
            phase_resid(W["w_dn0"], HID, BF16, FC, HA, HB)
        if want("kv"):
            phase_kv(HB)
        if want("in1"):
            phase_inproj(1, HB)
        if want("mem1"):
            phase_memattn(1)
        if want("sb"):
            phase_sb()
        if want("out1"):
            phase_resid(W["w_out1"], MIX, F32, XC, HB, HA)
        if want("up1"):
            phase_ffn_up(1, HA)
        if want("dn1"):
            phase_resid(W["w_dn1"], HID, BF16, FC, HA, yT)
        for nm in debug_outs:
            dump(nm)
        P.barrier()
        P.emit()
    es.close()
    return nc


def tile_w(Wm, nblk, col_groups=None):
    Kd, N = Wm.shape
    kc = Kd // 128
    if col_groups is None:
        nt = N // nblk
        x = Wm.reshape(kc, 128, nt, nblk).transpose(2, 1, 0, 3)
    else:
        x = Wm[:, col_groups.reshape(-1)].reshape(kc, 128, col_groups.shape[0], nblk).transpose(2, 1, 0, 3)
        nt = col_groups.shape[0]
    return np.ascontiguousarray(x).reshape(nt * 128, kc * nblk)


def chunkvec(v):
    return np.ascontiguousarray(v.reshape(-1, 128).T)


def host_layout(cfg, inp):
    f = np.float32
    D, KC, FC, NPAIR, SC = cfg.D, cfg.KC, cfg.FC, cfg.NPAIR, cfg.SSM_W // 128
    shared = {}
    gains = [chunkvec(inp["norm_mix_g"][0]), chunkvec(inp["norm_mix_g"][1]), chunkvec(inp["mem_norm_g"][0]),
             chunkvec(inp["mem_norm_g"][1]), chunkvec(inp["norm_ffn_g"][0]), chunkvec(inp["norm_ffn_g"][1]),
             chunkvec(inp["kv_norm_g"])]
    shared["gains"] = np.ascontiguousarray(np.concatenate(gains, axis=1), dtype=f)
    qk = []
    for l in range(2):
        qk += [chunkvec(inp["mem_q_norm_g"][l]), chunkvec(inp["mem_k_norm_g"][l])]
    shared["qkg"] = np.ascontiguousarray(np.concatenate(qk, axis=1), dtype=f)
    cw = inp["ffn_conv_w"]
    cb = inp["ffn_conv_b"]
    cp = np.zeros((128, 2, 2, FC, 4), f)
    for l in range(2):
        for ab in range(2):
            sl = slice(ab * cfg.D_FF, (ab + 1) * cfg.D_FF)
            for i in range(3):
                cp[:, l, ab, :, i] = chunkvec(cw[l, i, sl])
            cp[:, l, ab, :, 3] = chunkvec(cb[l, sl])
    shared["convp"] = cp.reshape(128, -1)
    lam_re, lam_im, ls = inp["s5_lam_re"][0], inp["s5_lam_im"][0], inp["s5_log_step"][0]
    G = cfg.G

    def pairlay(a):
        return np.ascontiguousarray(a.reshape(NPAIR, 2, 64).transpose(1, 2, 0).reshape(128, NPAIR))
    lsb = np.repeat(ls[:, None], 64, axis=1)
    par = np.stack([pairlay(lam_re), pairlay(lam_im), pairlay(lsb)], axis=1)
    shared["s5par"] = np.ascontiguousarray(par.reshape(128, -1), dtype=f)
    rep = np.stack([a.reshape(NPAIR * 128) for a in (lam_re.reshape(NPAIR, 128), lam_im.reshape(NPAIR, 128),
                                                       lsb.reshape(NPAIR, 128))], axis=0)
    shared["s5rep"] = np.ascontiguousarray(np.broadcast_to(rep.reshape(1, -1), (128, 3 * NPAIR * 128)), dtype=f)
    Bp = np.zeros((128, 2, NPAIR, 2, 64), f)
    Cp = np.zeros((128, 2, NPAIR, 128), f)
    for which, (bsrc, csrc) in enumerate(((inp["s5_b_re"][0], inp["s5_c_re"][0]), (inp["s5_b_im"][0], inp["s5_c_im"][0]))):
        for q in range(NPAIR):
            for e in range(2):
                g = 2 * q + e
                r0 = (q % 4) * 32 + e * 16
                Bp[r0:r0 + 16, which, q, e, :] = bsrc[g].T
                Cp[e * 64:(e + 1) * 64, which, q, r0:r0 + 16] = csrc[g].T
    shared["s5B"] = Bp.reshape(128, -1)
    shared["s5C"] = Cp.reshape(128, -1)
    shared["s5D"] = chunkvec(inp["s5_d"][0].reshape(-1)).astype(f)
    tri = (np.arange(128)[:, None] >= np.arange(128)[None, :]).astype(f)
    ones = np.ones((128, 128), f)
    masks = []
    for dj in range(4):
        s = np.arange(128)[:, None] + 128 * dj
        t = np.arange(TT)[None, :]
        masks.append((s < t).astype(f))
    shared["consts"] = np.concatenate([tri, ones] + masks, axis=1)
    for l in range(2):
        shared["w_in%d" % l] = tile_w(inp["w_in"][l], 256)
        shared["w_out%d" % l] = tile_w(inp["w_out"][l], 256)
        shared["w_mkv%d" % l] = tile_w(inp["w_mem_kv"][l], 256)
        cg = np.stack([np.concatenate([np.arange(j * 128, (j + 1) * 128), cfg.D_FF + np.arange(j * 128, (j + 1) * 128)])
                       for j in range(FC)])
        shared["w_up%d" % l] = tile_w(inp["w_ffn_up"][l], 256, cg)
        shared["w_dn%d" % l] = tile_w(inp["w_ffn_down"][l], 128)
    shared["w_glu"] = tile_w(inp["s5_w_glu"][0], 256)
    shared["w_kv"] = tile_w(inp["w_kv_shared"], 256)
    in_maps = []
    for b in range(cfg.B):
        m = dict(shared)
        m["xT"] = np.ascontiguousarray(inp["x"][b].T)
        m["memT"] = np.ascontiguousarray(inp["mem"][b].T)
        in_maps.append(m)
    return in_maps


_CACHE = {}


def run(cfg, inputs, debug_outs=(), trace=False):
    inp = {k: np.asarray(v) for k, v in inputs.items()}
    in_maps = host_layout(cfg, inp)
    key = (id(cfg), tuple(debug_outs))
    nc = build(cfg, debug_outs)
    res = run_bass_kernel_spmd(nc, in_maps, core_ids=list(range(cfg.B)))
    return res


def kernel(**inputs):
    cfg = FULL
    res = run(cfg, inputs)
    out = np.stack([np.ascontiguousarray(res.results[b]["yT"].T) for b in range(cfg.B)], axis=0)
    return out.astype(np.float32)
```

```python
import math
import numpy as np
import ml_dtypes
import concourse.bass as bass
import concourse.mybir as mybir
from concourse.bass_utils import run_bass_kernel_spmd

F32 = mybir.dt.float32
BF16 = mybir.dt.bfloat16
I32 = mybir.dt.int32
AF = mybir.ActivationFunctionType
ALU = mybir.AluOpType

SEM_LIMIT = 16000
TT = 512


class Cfg:
    def __init__(self, D=4096, SEQ=4096, B=2, SSM_W=2048, MEM_HEADS=4, MEM_TOKENS=256, D_FF=11008):
        self.D = D
        self.SEQ = SEQ
        self.B = B
        self.SSM_W = SSM_W
        self.G = SSM_W // 16
        self.NPAIR = self.G // 2
        self.SB_HEADS = SSM_W // 128
        self.MEM_HEADS = MEM_HEADS
        self.MEM_W = MEM_HEADS * 256
        self.MIX_W = SSM_W + self.MEM_W
        self.MEM_TOKENS = MEM_TOKENS
        self.D_FF = D_FF
        self.KC = D // 128
        self.NT = SEQ // TT
        self.FC = D_FF // 128


FULL = Cfg()


class Sem:
    def __init__(self, nc, name):
        self.h = nc.alloc_semaphore(name)
        self.v = 0


class Buf:
    __slots__ = ("name", "w", "r", "dsem")

    def __init__(self, name=""):
        self.name = name
        self.w = []
        self.r = {}
        self.dsem = None


class Stream:
    def __init__(self, P, name, eng):
        self.P = P
        self.name = name
        self.eng = eng
        self.sems = []
        self.sem = None
        self.waited = {}
        self.insts = []

    def cur_sem(self):
        if self.sem is None or self.sem.v >= SEM_LIMIT:
            self.sem = Sem(self.P.nc, "e%s%d" % (self.name, len(self.sems)))
            self.sems.append(self.sem)
        return self.sem


class Prog:
    def __init__(self, nc):
        self.nc = nc
        self.st = {
            "pe": Stream(self, "pe", nc.tensor),
            "act": Stream(self, "act", nc.scalar),
            "dve": Stream(self, "dve", nc.vector),
            "pool": Stream(self, "pool", nc.gpsimd),
            "sp": Stream(self, "sp", nc.sync),
        }
        self.dma_pool = []
        self.dma_live = []
        self.ndsem = 0
        self.rr = 0

    def _deps(self, reads, writes):
        d = {}
        for b in reads:
            for (s, v) in b.w:
                if d.get(s, 0) < v:
                    d[s] = v
        for b in writes:
            for (s, v) in b.w:
                if d.get(s, 0) < v:
                    d[s] = v
            for s, v in b.r.items():
                if d.get(s, 0) < v:
                    d[s] = v
        return d

    def _waits(self, S, d):
        waits = []
        for s, v in d.items():
            if S.name == "pe" and s in S.sems:
                continue
            if S.waited.get(s, 0) >= v:
                continue
            S.waited[s] = v
            waits.append((s, v))
        return waits

    def _mark(self, tok, reads, writes):
        s, v = tok
        for b in reads:
            if b.r.get(s, 0) < v:
                b.r[s] = v
        for b in writes:
            b.w = [tok]
            b.r = {}

    def op(self, st, fn, reads=(), writes=()):
        S = self.st[st]
        waits = self._waits(S, self._deps(reads, writes))
        sem = S.cur_sem()
        sem.v += 1
        tok = (sem, sem.v)
        S.insts.append((waits, fn, sem, 1))
        self._mark(tok, reads, writes)
        return tok

    def _dsem(self, owner):
        if owner.dsem is None or owner.dsem.v >= SEM_LIMIT:
            if self.dma_pool:
                owner.dsem = self.dma_pool.pop()
            else:
                owner.dsem = Sem(self.nc, "d%d" % self.ndsem)
                self.ndsem += 1
            self.dma_live.append(owner.dsem)
        return owner.dsem

    def dma(self, out, in_, reads, writes, owner, st="sp"):
        S = self.st[st]
        waits = self._waits(S, self._deps(reads, writes))
        sem = self._dsem(owner)
        sem.v += 16
        tok = (sem, sem.v)

        def fn(e, out=out, in_=in_):
            return e.dma_start(out=out, in_=in_)

        S.insts.append((waits, fn, sem, 16))
        self._mark(tok, reads, writes)
        return tok

    def barrier(self):
        toks = {}
        for S in self.st.values():
            if S.sem is not None and S.sem.v > 0:
                toks[S.sem] = S.sem.v
        for s in self.dma_live:
            if s.v > 0:
                toks[s] = s.v
        for S in self.st.values():
            waits = self._waits(S, dict(toks))
            if waits:
                S.insts.append((waits, None, None, 0))
        for s in self.dma_live:
            if s.v < SEM_LIMIT and s not in self.dma_pool:
                self.dma_pool.append(s)
        self.dma_live = []

    def emit(self):
        nc = self.nc
        with nc.Block() as block:
            def run(S):
                def body(e):
                    for (waits, fn, sem, inc) in S.insts:
                        for (s, v) in waits:
                            e.wait_ge(s.h, v)
                        if fn is not None:
                            ins = fn(e)
                            ins.then_inc(sem.h, inc)
                return body
            block.tensor(run(self.st["pe"]))
            block.scalar(run(self.st["act"]))
            block.vector(run(self.st["dve"]))
            block.gpsimd(run(self.st["pool"]))
            block.sync(run(self.st["sp"]))


class Alloc:
    def __init__(self, t, ncols):
        self.t = t
        self.n = ncols
        self.off = 0

    def take(self, units):
        a = self.off
        al = (units + 15) // 16 * 16
        assert a + al <= self.n, ("sbuf pool overflow", a, units, self.n)
        self.off += al
        return a


class Pool2:
    def __init__(self, al, bf):
        self.al = al
        self.bf = bf

    @property
    def off(self):
        return self.al.off

    @off.setter
    def off(self, v):
        self.al.off = v

    def alloc(self, ncols):
        if self.bf:
            units = (ncols + 1) // 2
            a = self.al.take(units)
            return self.al.t[:, a:a + units].bitcast(BF16)[:, 0:ncols]
        a = self.al.take(ncols)
        return self.al.t[:, a:a + ncols]


class K:
    def __init__(self, cfg):
        self.cfg = cfg
        self.nc = bass.Bass("TRN2", target_bir_lowering=False)
        self.P = Prog(self.nc)
        self.dr = {}
        self.cast_rr = 0

    def din(self, name, shape, dt=F32):
        t = self.nc.dram_tensor(name, list(shape), dt, kind="ExternalInput").ap()
        self.dr[name] = t
        return t

    def dscr(self, name, shape, dt=F32):
        t = self.nc.dram_tensor(name, list(shape), dt, kind="Internal").ap()
        self.dr[name] = t
        return t

    def dout(self, name, shape, dt=F32):
        t = self.nc.dram_tensor(name, list(shape), dt, kind="ExternalOutput").ap()
        self.dr[name] = t
        return t

    def act(self, out, in_, func, reads, writes, scale=None, bias=None):
        kw = {}
        if scale is not None:
            kw["scale"] = scale
        if bias is not None:
            kw["bias"] = bias
        return self.P.op("act", lambda e: e.activation(out=out, in_=in_, func=func, **kw), reads, writes)

    def tt(self, st, out, in0, in1, op, reads, writes):
        return self.P.op(st, lambda e: e.tensor_tensor(out=out, in0=in0, in1=in1, op=op), reads, writes)

    def ts(self, st, out, in0, s1, op0, reads, writes, s2=None, op1=None):
        if op1 is None:
            return self.P.op(st, lambda e: e.tensor_scalar(out=out, in0=in0, scalar1=s1, scalar2=None, op0=op0),
                             reads, writes)
        return self.P.op(st, lambda e: e.tensor_scalar(out=out, in0=in0, scalar1=s1, scalar2=s2, op0=op0, op1=op1),
                         reads, writes)

    def stt(self, out, in0, scalar, in1, op0, op1, reads, writes):
        return self.P.op("dve", lambda e: e.scalar_tensor_tensor(out=out, in0=in0, scalar=scalar, in1=in1,
                                                                 op0=op0, op1=op1), reads, writes)

    def copy(self, st, out, in_, reads, writes):
        if st == "act":
            return self.P.op("act", lambda e: e.activation(out=out, in_=in_, func=AF.Copy), reads, writes)
        return self.P.op(st, lambda e: e.tensor_copy(out=out, in_=in_), reads, writes)

    def cast_any(self, out, in_, reads, writes, engines=("act", "dve")):
        st = engines[self.cast_rr % len(engines)]
        self.cast_rr += 1
        return self.copy(st, out, in_, reads, writes)

    def mm(self, items, reads, writes):
        def fn(e):
            ins = None
            for (o, l, r, st, sp) in items:
                ins = e.matmul(o, l, r, start=st, stop=sp)
            return ins
        return self.P.op("pe", fn, reads, writes)


def build(cfg, debug_outs=()):
    kb = K(cfg)
    nc = kb.nc
    P = kb.P
    D, SEQ, KC, NT, FC = cfg.D, cfg.SEQ, cfg.KC, cfg.NT, cfg.FC
    SSM_W, MEM_W, MIX_W = cfg.SSM_W, cfg.MEM_W, cfg.MIX_W
    SC = SSM_W // 128
    MC = MEM_W // 128
    XC = MIX_W // 128
    MT = cfg.MEM_TOKENS
    NPAIR = cfg.NPAIR
    H = cfg.SB_HEADS

    xT = kb.din("xT", [D, SEQ])
    memT = kb.din("memT", [D, MT])
    gains = kb.din("gains", [128, 7 * KC])
    qkg = kb.din("qkg", [128, 8])
    convp = kb.din("convp", [128, 2 * 2 * FC * 4])
    s5rep = kb.din("s5rep", [128, 3 * NPAIR * 128])
    s5par = kb.din("s5par", [128, 3 * NPAIR])
    s5B = kb.din("s5B", [128, 2 * NPAIR * 128])
    s5C = kb.din("s5C", [128, 2 * NPAIR * 128])
    s5D = kb.din("s5D", [128, SC])
    consts = kb.din("consts", [128, 128 + 128 + 4 * TT])

    def wspec(name, Kd, ntiles, nblk):
        kc = Kd // 128
        w32 = kb.din(name, [ntiles * 128, kc * nblk])
        w16 = kb.dscr(name + "_bf", [ntiles * 128, kc * nblk], BF16)
        return dict(name=name, w32=w32, w16=w16, kc=kc, ntiles=ntiles, nblk=nblk)

    W = {}
    for l in range(2):
        W["w_in%d" % l] = wspec("w_in%d" % l, D, MIX_W // 256, 256)
        W["w_out%d" % l] = wspec("w_out%d" % l, MIX_W, D // 256, 256)
        W["w_mkv%d" % l] = wspec("w_mkv%d" % l, D, 2 * MEM_W // 256, 256)
        W["w_up%d" % l] = wspec("w_up%d" % l, D, FC, 256)
        W["w_dn%d" % l] = wspec("w_dn%d" % l, cfg.D_FF, D // 128, 128)
    W["w_glu"] = wspec("w_glu", SSM_W, SSM_W // 256, 256)
    W["w_kv"] = wspec("w_kv", D, 2 * SSM_W // 256, 256)

    HA = kb.dscr("HA", [D, SEQ])
    HB = kb.dscr("HB", [D, SEQ])
    PROJ = kb.dscr("PROJ", [MIX_W, SEQ])
    MIX = kb.dscr("MIX", [MIX_W, SEQ])
    GS5 = kb.dscr("GS5", [SSM_W, SEQ])
    HID = kb.dscr("HID", [cfg.D_FF, SEQ], BF16)
    KT = kb.dscr("KTs", [SSM_W, SEQ], BF16)
    VTM = kb.dscr("VTM", [SEQ, SSM_W], BF16)
    yT = kb.dout("yT", [D, SEQ])
    dbg = {}
    for nm in debug_outs:
        src = kb.dr[nm]
        dbg[nm] = kb.dout("dbg_" + nm, list(src.shape), src.dtype)

    import contextlib
    es = contextlib.ExitStack()
    NALL = 53000
    big = es.enter_context(nc.sbuf_tensor("big", [128, NALL], F32))
    al = Alloc(big, NALL)
    p32 = Pool2(al, False)
    p16 = Pool2(al, True)
    banks = [es.enter_context(nc.psum_tensor("bank%d" % i, [128, 512], F32))[:, :] for i in range(8)]
    bankbuf = [Buf("bank%d" % i) for i in range(8)]

    cst32 = p32.alloc(128 + 128 + 4 * TT)
    cstb = Buf("cst")
    P.dma(cst32, consts, [], [cstb], cstb)
    TRI32 = cst32[:, 0:128]
    ONES32 = cst32[:, 128:256]
    MASK = [cst32[:, 256 + i * TT:256 + (i + 1) * TT] for i in range(4)]
    ones16 = p16.alloc(128)
    kb.copy("dve", ones16, ONES32, [cstb], [cstb])
    gn32 = p32.alloc(7 * KC)
    P.dma(gn32, gains, [], [cstb], cstb)
    qk32 = p32.alloc(8)
    P.dma(qk32, qkg, [], [cstb], cstb)
    base32 = p32.off

    def new_phase():
        P.barrier()
        p32.off = base32
        for b in bankbuf:
            b.w = []
            b.r = {}

    def cast_weights():
        new_phase()
        PIECE = 4096
        NB = 6
        st32 = [p32.alloc(PIECE) for _ in range(NB)]
        st16 = [p16.alloc(PIECE) for _ in range(NB)]
        b32 = [Buf() for _ in range(NB)]
        b16 = [Buf() for _ in range(NB)]
        i = 0
        for name, w in W.items():
            rows = w["ntiles"] * 128
            cols = w["kc"] * w["nblk"]
            for r0 in range(0, rows, 128):
                for c0 in range(0, cols, PIECE):
                    c1 = min(cols, c0 + PIECE)
                    n = c1 - c0
                    k = i % NB
                    i += 1
                    P.dma(st32[k][:, 0:n], w["w32"][r0:r0 + 128, c0:c1], [], [b32[k]], b32[k])
                    kb.cast_any(st16[k][:, 0:n], st32[k][:, 0:n], [b32[k]], [b16[k]], engines=("act", "dve"))
                    P.dma(w["w16"][r0:r0 + 128, c0:c1], st16[k][:, 0:n], [b16[k]], [], b16[k], st="pool")


    def load_act_tile(src, Kc, t0, tw, a32, abuf):
        v = src.rearrange("(k p) t -> p k t", p=128)
        a3 = a32.rearrange("p (k t) -> p k t", k=Kc)
        for k0 in range(0, Kc, 8):
            k1 = min(Kc, k0 + 8)
            P.dma(a3[:, k0:k1, :], v[:, k0:k1, t0:t0 + tw], [], [abuf], abuf)

    def rms_tile(a32, abuf, Kc, tw, gcol, xn, xnbuf, sq, sqbuf, rstd, rbuf, bank, eps_scale):
        items = []
        for kc in range(Kc):
            k2 = kc % 2
            kb.act(sq[k2][:, 0:tw], a32[:, kc * tw:(kc + 1) * tw], AF.Square, [abuf], [sqbuf[k2]])
            kb.mm([(banks[bank][:, 0:tw], ONES32, sq[k2][:, 0:tw], kc == 0, kc == Kc - 1)],
                  [sqbuf[k2], cstb], [bankbuf[bank]])
        kb.act(rstd[:, 0:tw], banks[bank][:, 0:tw], AF.Sqrt, [bankbuf[bank], cstb], [rbuf],
               scale=1.0 / (Kc * 128), bias=epsb[:, 0:1])
        P.op("dve", lambda e: e.reciprocal(out=rstd[:, 0:tw], in_=rstd[:, 0:tw]), [rbuf], [rbuf])
        for kc in range(Kc):
            kb.stt(xn[:, kc * tw:(kc + 1) * tw], a32[:, kc * tw:(kc + 1) * tw], gn32[:, gcol + kc:gcol + kc + 1],
                   rstd[:, 0:tw], ALU.mult, ALU.mult, [abuf, rbuf, cstb], [xnbuf])

    epsb = p32.alloc(1)
    P.op("dve", lambda e: e.memset(epsb, 1e-6), [], [cstb])
    base32 = p32.off

    class WStream:
        def __init__(self, maxcols):
            self.bufs = [p16.alloc(maxcols) for _ in range(3)]
            self.bb = [Buf() for _ in range(3)]
            self.i = 0
            self.q = []

        def prefetch(self, w, ti):
            k = self.i % 3
            self.i += 1
            cols = w["kc"] * w["nblk"]
            P.dma(self.bufs[k][:, 0:cols], w["w16"][ti * 128:(ti + 1) * 128, :], [], [self.bb[k]], self.bb[k])
            self.q.append((self.bufs[k], self.bb[k]))

        def pop(self):
            return self.q.pop(0)

    def gemm_pass(w, xn, xnbuf, tw, ws, evac, tiles=None, pre=2):
        tiles = list(range(w["ntiles"])) if tiles is None else tiles
        kc_n = w["kc"]
        nb = w["nblk"]
        for j in range(min(pre, len(tiles))):
            ws.prefetch(w, tiles[j])
        bi = 0
        for idx, ti in enumerate(tiles):
            if idx + pre < len(tiles):
                ws.prefetch(w, tiles[idx + pre])
            wt, wb = ws.pop()
            for c in range(nb // 128):
                bank = gemm_pass.bank_rr % 4
                gemm_pass.bank_rr += 1
                items = []
                for kc in range(kc_n):
                    items.append((banks[bank][:, 0:tw], wt[:, kc * nb + c * 128: kc * nb + (c + 1) * 128],
                                  xn[:, kc * tw:(kc + 1) * tw], kc == 0, kc == kc_n - 1))
                kb.mm(items, [wb, xnbuf], [bankbuf[bank]])
                evac(ti, c, bank)
    gemm_pass.bank_rr = 0

    def store_tile(dst, r0, t0, tw, src, sbuf):
        P.dma(dst[r0:r0 + 128, t0:t0 + tw], src, [sbuf], [], sbuf)

    def phase_inproj(l, hsrc):
        new_phase()
        w = W["w_in%d" % l]
        a32 = p32.alloc(KC * TT)
        abuf = Buf()
        xn = p16.alloc(KC * TT)
        xnbuf = Buf()
        sq = [p32.alloc(TT) for _ in range(2)]
        sqb = [Buf() for _ in range(2)]
        rstd = p32.alloc(TT)
        rb = Buf()
        ws = WStream(w["kc"] * w["nblk"])
        ev = [p32.alloc(TT) for _ in range(4)]
        evb = [Buf() for _ in range(4)]
        cnt = [0]
        for tt in range(NT):
            t0 = tt * TT
            load_act_tile(hsrc, KC, t0, TT, a32, abuf)
            rms_tile(a32, abuf, KC, TT, l * KC, xn, xnbuf, sq, sqb, rstd, rb, 7, None)

            def evac(ti, c, bank, t0=t0):
                k = cnt[0] % 4
                cnt[0] += 1
                kb.copy("act" if k % 2 else "dve", ev[k], banks[bank][:, 0:TT], [bankbuf[bank]], [evb[k]])
                store_tile(PROJ, (ti * 2 + c) * 128, t0, TT, ev[k], evb[k])
            gemm_pass(w, xn, xnbuf, TT, ws, evac)

    def phase_memattn(l):
        new_phase()
        w = W["w_mkv%d" % l]
        m32 = p32.alloc(KC * MT)
        mb = Buf()
        mn = p16.alloc(KC * MT)
        mnb = Buf()
        sq = [p32.alloc(TT) for _ in range(2)]
        sqb = [Buf() for _ in range(2)]
        rstd = p32.alloc(TT)
        rb = Buf()
        load_act_tile(memT, KC, 0, MT, m32, mb)
        rms_tile(m32, mb, KC, MT, (2 + l) * KC, mn, mnb, sq, sqb, rstd, rb, 7, None)
        ws = WStream(w["kc"] * w["nblk"])
        k32 = p32.alloc(MC * MT)
        k32b = Buf()
        kn = p16.alloc(MC * MT)
        knb = Buf()
        MCH = MT // 128
        vtm = p16.alloc(MCH * MEM_W)
        vtb = Buf()
        nkt = MEM_W // 256
        ws_i = 0
        for j in range(min(2, w["ntiles"])):
            ws.prefetch(w, j)
        for ti in range(w["ntiles"]):
            if ti + 2 < w["ntiles"]:
                ws.prefetch(w, ti + 2)
            wt, wb = ws.pop()
            if ti < nkt:
                for c in range(2):
                    bank = c
                    items = [(banks[bank][:, 0:MT], wt[:, kc * 256 + c * 128: kc * 256 + (c + 1) * 128],
                              mn[:, kc * MT:(kc + 1) * MT], kc == 0, kc == KC - 1) for kc in range(KC)]
                    kb.mm(items, [wb, mnb], [bankbuf[bank]])
                    ch = ti * 2 + c
                    kb.copy("act", k32[:, ch * MT:(ch + 1) * MT], banks[bank][:, 0:MT], [bankbuf[bank]], [k32b])
            else:
                d0 = (ti - nkt) * 256
                for mc in range(MCH):
                    bank = 2 + mc % 2
                    items = [(banks[bank][:, 0:256], mn[:, kc * MT + mc * 128: kc * MT + (mc + 1) * 128],
                              wt[:, kc * 256:(kc + 1) * 256], kc == 0, kc == KC - 1) for kc in range(KC)]
                    kb.mm(items, [wb, mnb], [bankbuf[bank]])
                    kb.copy("dve", vtm[:, mc * MEM_W + d0: mc * MEM_W + d0 + 256], banks[bank][:, 0:256],
                            [bankbuf[bank]], [vtb])
        for h in range(cfg.MEM_HEADS):
            for dc in range(2):
                ch = 2 * h + dc
                kb.act(sq[dc][:, 0:MT], k32[:, ch * MT:(ch + 1) * MT], AF.Square, [k32b], [sqb[dc]])
                kb.mm([(banks[7][:, 0:MT], ONES32, sq[dc][:, 0:MT], dc == 0, dc == 1)], [sqb[dc], cstb], [bankbuf[7]])
            kb.act(rstd[:, 0:MT], banks[7][:, 0:MT], AF.Sqrt, [bankbuf[7], cstb], [rb], scale=1.0 / 256, bias=epsb[:, 0:1])
            P.op("dve", lambda e: e.reciprocal(out=rstd[:, 0:MT], in_=rstd[:, 0:MT]), [rb], [rb])
            for dc in range(2):
                ch = 2 * h + dc
                kb.stt(kn[:, ch * MT:(ch + 1) * MT], k32[:, ch * MT:(ch + 1) * MT],
                       qk32[:, 4 * l + 2 + dc:4 * l + 3 + dc], rstd[:, 0:MT], ALU.mult, ALU.mult,
                       [k32b, rb, cstb], [knb])
        q32 = p32.alloc(MC * TT)
        qb = Buf()
        qn = p16.alloc(MC * TT)
        qnb = Buf()
        pT = p16.alloc(MCH * TT)
        pTb = Buf()
        rden = p32.alloc(TT)
        rdb = Buf()
        ev = [p32.alloc(TT) for _ in range(2)]
        evb = [Buf() for _ in range(2)]
        cnt = 0
        for tt in range(NT):
            t0 = tt * TT
            v = PROJ.rearrange("(k p) t -> p k t", p=128)
            P.dma(q32.rearrange("p (k t) -> p k t", k=MC), v[:, SC:SC + MC, t0:t0 + TT], [], [qb], qb)
            for h in range(cfg.MEM_HEADS):
                for dc in range(2):
                    ch = 2 * h + dc
                    kb.act(sq[dc], q32[:, ch * TT:(ch + 1) * TT], AF.Square, [qb], [sqb[dc]])
                    kb.mm([(banks[7], ONES32, sq[dc], dc == 0, dc == 1)], [sqb[dc], cstb], [bankbuf[7]])
                kb.act(rstd, banks[7], AF.Sqrt, [bankbuf[7], cstb], [rb], scale=1.0 / 256, bias=epsb[:, 0:1])
                P.op("dve", lambda e: e.reciprocal(out=rstd, in_=rstd), [rb], [rb])
                for dc in range(2):
                    ch = 2 * h + dc
                    kb.stt(qn[:, ch * TT:(ch + 1) * TT], q32[:, ch * TT:(ch + 1) * TT],
                           qk32[:, 4 * l + dc:4 * l + dc + 1], rstd, ALU.mult, ALU.mult, [qb, rb, cstb], [qnb])
                for mc in range(MCH):
                    bank = mc % 2
                    items = [(banks[bank], kn[:, (2 * h + dc) * MT + mc * 128:(2 * h + dc) * MT + (mc + 1) * 128],
                              qn[:, (2 * h + dc) * TT:(2 * h + dc + 1) * TT], dc == 0, dc == 1) for dc in range(2)]
                    kb.mm(items, [knb, qnb], [bankbuf[bank]])
                    kb.act(pT[:, mc * TT:(mc + 1) * TT], banks[bank], AF.Exp, [bankbuf[bank]], [pTb], scale=1.0 / 16.0)
                items = [(banks[2], ones16, pT[:, mc * TT:(mc + 1) * TT], mc == 0, mc == MCH - 1) for mc in range(MCH)]
                kb.mm(items, [pTb, cstb], [bankbuf[2]])
                P.op("dve", lambda e: e.reciprocal(out=rden, in_=banks[2]), [bankbuf[2]], [rdb])
                for dc in range(2):
                    bank = 3 + dc
                    items = [(banks[bank], vtm[:, mc * MEM_W + h * 256 + dc * 128: mc * MEM_W + h * 256 + (dc + 1) * 128],
                              pT[:, mc * TT:(mc + 1) * TT], mc == 0, mc == MCH - 1) for mc in range(MCH)]
                    kb.mm(items, [vtb, pTb], [bankbuf[bank]])
                    k = cnt % 2
                    cnt += 1
                    kb.tt("dve", ev[k], banks[bank], rden, ALU.mult, [bankbuf[bank], rdb], [evb[k]])
                    store_tile(MIX, SSM_W + h * 256 + dc * 128, t0, TT, ev[k], evb[k])

    def phase_s5():
        new_phase()
        NQ = NPAIR * 128
        NP_ = NPAIR
        LRE = p16.alloc(NQ)
        LIM = p16.alloc(NQ)
        CRE = p16.alloc(NQ)
        CIMN = p16.alloc(NQ)
        Lb = Buf()
        par = p32.alloc(3 * NP_)
        pbuf = Buf()
        P.dma(par, s5par, [], [pbuf], pbuf)
        T2 = [p32.alloc(NP_) for _ in range(6)]
        t2b = Buf()
        scr2 = p32.alloc(NP_)
        scr2i = p32.alloc(NP_).bitcast(I32)
        NLEV = 10
        CLx = [p32.alloc(NP_) for _ in range(NLEV - 1)]
        SLx = [p32.alloc(NP_) for _ in range(NLEV - 1)]
        d32 = p32.alloc(SC)
        mark = p32.off

        def prep(lr_in, li_in, ls_in, T, bufs_in, ob):
            step, lr, mag, th, cs, sn = T
            kb.act(step, ls_in, AF.Exp, bufs_in, [ob])
            kb.ts("dve", lr, lr_in, -1e-4, ALU.min, bufs_in, [ob])
            kb.tt("dve", mag, lr, step, ALU.mult, [ob], [ob])
            kb.act(mag, mag, AF.Exp, [ob], [ob])
            kb.tt("dve", th, li_in, step, ALU.mult, bufs_in + [ob], [ob])
            return step, lr, mag, th, cs, sn

        def sincos(th, cs, sn, ob, scr, scri):
            for (dst, shift) in ((sn, 0.0), (cs, math.pi / 2)):
                kb.ts("dve", scr, th, 1.0 / (2 * math.pi), ALU.mult, [ob], [ob], s2=shift / (2 * math.pi) + 0.5, op1=ALU.add)
                kb.copy("dve", scri, scr, [ob], [ob])
                kb.copy("dve", scr, scri, [ob], [ob])
                kb.ts("dve", scr, scr, -2 * math.pi, ALU.mult, [ob], [ob], s2=shift, op1=ALU.add)
                kb.tt("dve", dst, th, scr, ALU.add, [ob], [ob])
                kb.ts("dve", scr, dst, math.pi, ALU.is_gt, [ob], [ob], s2=-2 * math.pi, op1=ALU.mult)
                kb.tt("dve", dst, dst, scr, ALU.add, [ob], [ob])
                kb.ts("dve", scr, dst, -math.pi, ALU.is_lt, [ob], [ob], s2=2 * math.pi, op1=ALU.mult)
                kb.tt("dve", dst, dst, scr, ALU.add, [ob], [ob])
                kb.act(dst, dst, AF.Sin, [ob], [ob])

        PB = min(8, NPAIR)
        BQ = PB * 128
        rep = p32.alloc(3 * BQ)
        rbuf_ = Buf()
        tmp = [p32.alloc(BQ) for _ in range(6)]
        tb = Buf()
        scr = p32.alloc(BQ)
        scri = p32.alloc(BQ).bitcast(I32)
        Bp = p32.alloc(2 * BQ)
        Bb = Buf()
        Cp = p32.alloc(2 * BQ)
        Cb = Buf()
        rep3 = s5rep.rearrange("p (w q) -> p w q", w=3)
        B3 = s5B.rearrange("p (w q) -> p w q", w=2)
        C3 = s5C.rearrange("p (w q) -> p w q", w=2)
        for blk in range(NPAIR // PB):
            q0 = blk * BQ
            P.dma(rep.rearrange("p (w q) -> p w q", w=3), rep3[:, :, q0:q0 + BQ], [], [rbuf_], rbuf_)
            P.dma(Bp.rearrange("p (w q) -> p w q", w=2), B3[:, :, q0:q0 + BQ], [], [Bb], Bb)
            P.dma(Cp.rearrange("p (w q) -> p w q", w=2), C3[:, :, q0:q0 + BQ], [], [Cb], Cb)
            li = rep[:, BQ:2 * BQ]
            step, lr, mag, th, cs, sn = prep(rep[:, 0:BQ], li, rep[:, 2 * BQ:3 * BQ], tmp, [rbuf_], tb)
            sincos(th, cs, sn, tb, scr, scri)
            kb.tt("dve", cs, cs, mag, ALU.mult, [tb], [tb])
            kb.tt("dve", sn, sn, mag, ALU.mult, [tb], [tb])
            kb.tt("dve", step, lr, lr, ALU.mult, [tb], [tb])
            kb.tt("dve", scr, li, li, ALU.mult, [rbuf_, tb], [tb])
            kb.tt("dve", step, step, scr, ALU.add, [tb], [tb])
            P.op("dve", lambda e, a=step: e.reciprocal(out=a, in_=a), [tb], [tb])
            kb.ts("dve", mag, cs, -1.0, ALU.add, [tb], [tb])
            kb.tt("dve", th, mag, lr, ALU.mult, [tb], [tb])
            kb.tt("dve", scr, sn, li, ALU.mult, [rbuf_, tb], [tb])
            kb.tt("dve", th, th, scr, ALU.add, [tb], [tb])
            kb.tt("dve", th, th, step, ALU.mult, [tb], [tb])
            kb.tt("dve", cs, sn, lr, ALU.mult, [tb], [tb])
            kb.tt("dve", scr, mag, li, ALU.mult, [rbuf_, tb], [tb])
            kb.tt("dve", cs, cs, scr, ALU.subtract, [tb], [tb])
            kb.tt("dve", cs, cs, step, ALU.mult, [tb], [tb])
            f_re, f_im = th, cs
            br, bi = Bp[:, 0:BQ], Bp[:, BQ:2 * BQ]
            kb.tt("dve", mag, f_re, br, ALU.mult, [tb, Bb], [tb])
            kb.tt("dve", scr, f_im, bi, ALU.mult, [tb, Bb], [tb])
            kb.tt("dve", LRE[:, q0:q0 + BQ], mag, scr, ALU.subtract, [tb], [Lb])
            kb.tt("dve", mag, f_re, bi, ALU.mult, [tb, Bb], [tb])
            kb.tt("dve", scr, f_im, br, ALU.mult, [tb, Bb], [tb])
            kb.tt("dve", LIM[:, q0:q0 + BQ], mag, scr, ALU.add, [tb], [Lb])
            kb.copy("dve", CRE[:, q0:q0 + BQ], Cp[:, 0:BQ], [Cb], [Lb])
            kb.ts("dve", CIMN[:, q0:q0 + BQ], Cp[:, BQ:2 * BQ], -1.0, ALU.mult, [Cb], [Lb])
        step2, lr2, rho, th2, c0, s0 = prep(par[:, 0:NP_], par[:, NP_:2 * NP_], par[:, 2 * NP_:3 * NP_], T2, [pbuf], t2b)
        sincos(th2, c0, s0, t2b, scr2, scr2i)
        CL = [c0] + CLx
        SL = [s0] + SLx
        for m in range(1, NLEV):
            kb.tt("dve", CL[m], CL[m - 1], CL[m - 1], ALU.mult, [t2b], [t2b])
            kb.tt("dve", scr2, SL[m - 1], SL[m - 1], ALU.mult, [t2b], [t2b])
            kb.tt("dve", CL[m], CL[m], scr2, ALU.subtract, [t2b], [t2b])
            kb.tt("dve", SL[m], CL[m - 1], SL[m - 1], ALU.mult, [t2b], [t2b])
            kb.ts("dve", SL[m], SL[m], 2.0, ALU.mult, [t2b], [t2b])
        P.dma(d32, s5D, [], [t2b], t2b)
        P.barrier()
        p32.off = mark

        NPC = 4
        Ec = [p32.alloc(TT) for _ in range(NPC)]
        Es = [p32.alloc(TT) for _ in range(NPC)]
        Eb = [Buf() for _ in range(NPC)]
        rhoT = [p32.alloc(TT) for _ in range(NPC)]
        init = [p32.alloc(2) for _ in range(NPC)]
        inb = [Buf() for _ in range(NPC)]
        u32 = [p32.alloc(TT) for _ in range(2)]
        u32b = [Buf() for _ in range(2)]
        u16 = [p16.alloc(TT) for _ in range(2)]
        u16b = [Buf() for _ in range(2)]
        NW = NPC
        NTL = 8
        wk = [[p32.alloc(TT) for _ in range(NTL)] for _ in range(NW)]
        wb2 = [[Buf() for _ in range(NTL)] for _ in range(NW)]
        x16 = [[p16.alloc(TT) for _ in range(2)] for _ in range(NW)]
        x16b = [[Buf() for _ in range(2)] for _ in range(NW)]
        tiny = [p32.alloc(4) for _ in range(NW)]
        tinyb = [Buf() for _ in range(NW)]
        gout = [p32.alloc(TT) for _ in range(2)]
        goutb = [Buf() for _ in range(2)]
        uc = 0
        gc = 0
        bc = 0
        for c in range(SC):
            for pi in range(NPC):
                kb.ts("dve", Ec[pi][:, 0:1], ONES32[:, 0:1], 1.0, ALU.mult, [cstb], [Eb[pi]])
                kb.ts("dve", Es[pi][:, 0:1], ONES32[:, 0:1], 0.0, ALU.mult, [cstb], [Eb[pi]])
            for m in range(NLEV - 1):
                s = 1 << m
                for pi in range(NPC):
                    q = c * NPC + pi
                    sm = SL[m][:, q:q + 1]
                    eng = "dve"
                    kb.ts(eng, wk[pi][2][:, 0:s], Es[pi][:, 0:s], sm, ALU.mult, [Eb[pi], t2b], [wb2[pi][2]])
                    kb.ts(eng, wk[pi][3][:, 0:s], Ec[pi][:, 0:s], sm, ALU.mult, [Eb[pi], t2b], [wb2[pi][3]])
                for pi in range(NPC):
                    q = c * NPC + pi
                    cm = CL[m][:, q:q + 1]
                    kb.stt(Ec[pi][:, s:2 * s], Ec[pi][:, 0:s], cm, wk[pi][2][:, 0:s], ALU.mult, ALU.subtract,
                           [Eb[pi], wb2[pi][2], t2b], [Eb[pi]])
                    kb.stt(Es[pi][:, s:2 * s], Es[pi][:, 0:s], cm, wk[pi][3][:, 0:s], ALU.mult, ALU.add,
                           [Eb[pi], wb2[pi][3], t2b], [Eb[pi]])
            for pi in range(NPC):
                q = c * NPC + pi
                kb.act(rhoT[pi], ONES_T, AF.Copy, [cstb, t2b], [Eb[pi]], scale=rho[:, q:q + 1])
                P.op("dve", lambda e, a=init[pi]: e.memset(a, 0.0), [], [inb[pi]])
            for tt in range(NT):
                t0 = tt * TT
                k = uc % 2
                uc += 1
                P.dma(u32[k], PROJ[c * 128:(c + 1) * 128, t0:t0 + TT], [], [u32b[k]], u32b[k])
                kb.copy("act", u16[k], u32[k], [u32b[k]], [u16b[k]])
                ybank = 6 + (gc % 2)
                R = range(NPC)
                for pi in R:
                    q = c * NPC + pi
                    ba = 2 * (bc % 3)
                    bc += 1
                    kb.mm([(banks[ba], LRE[:, q * 128:(q + 1) * 128], u16[k], True, True)], [Lb, u16b[k]], [bankbuf[ba]])
                    kb.mm([(banks[ba + 1], LIM[:, q * 128:(q + 1) * 128], u16[k], True, True)], [Lb, u16b[k]], [bankbuf[ba + 1]])
                    kb.copy("act", wk[pi][0], banks[ba], [bankbuf[ba]], [wb2[pi][0]])
                    kb.copy("act", wk[pi][1], banks[ba + 1], [bankbuf[ba + 1]], [wb2[pi][1]])
                for pi in R:
                    br_, bi_, t1, t2, t3, t4, wr, wim = wk[pi]
                    Bbr, Bbi, B1, B2, B3, B4, Bwr, Bwi = wb2[pi]
                    kb.tt("dve", t1, Ec[pi], br_, ALU.mult, [Eb[pi], Bbr], [B1])
                    kb.tt("dve", t2, Es[pi], bi_, ALU.mult, [Eb[pi], Bbi], [B2])
                    kb.tt("dve", t3, Ec[pi], bi_, ALU.mult, [Eb[pi], Bbi], [B3])
                    kb.tt("dve", t4, Es[pi], br_, ALU.mult, [Eb[pi], Bbr], [B4])
                for pi in R:
                    br_, bi_, t1, t2, t3, t4, wr, wim = wk[pi]
                    Bbr, Bbi, B1, B2, B3, B4, Bwr, Bwi = wb2[pi]
                    kb.tt("dve", t1, t1, t2, ALU.add, [B1, B2], [B1])
                    kb.tt("dve", t3, t3, t4, ALU.subtract, [B3, B4], [B3])
                for pi in R:
                    br_, bi_, t1, t2, t3, t4, wr, wim = wk[pi]
                    Bbr, Bbi, B1, B2, B3, B4, Bwr, Bwi = wb2[pi]
                    P.op("dve", lambda e, o=wr, a=rhoT[pi], b=t1, i0=init[pi][:, 0:1]:
                         e.tensor_tensor_scan(out=o, data0=a, data1=b, initial=i0, op0=ALU.mult, op1=ALU.add),
                         [Eb[pi], B1, inb[pi]], [Bwr])
                    P.op("dve", lambda e, o=wim, a=rhoT[pi], b=t3, i0=init[pi][:, 1:2]:
                         e.tensor_tensor_scan(out=o, data0=a, data1=b, initial=i0, op0=ALU.mult, op1=ALU.add),
                         [Eb[pi], B3, inb[pi]], [Bwi])
                for pi in R:
                    q = c * NPC + pi
                    br_, bi_, t1, t2, t3, t4, wr, wim = wk[pi]
                    Bbr, Bbi, B1, B2, B3, B4, Bwr, Bwi = wb2[pi]
                    c9 = CL[NLEV - 1][:, q:q + 1]
                    s9 = SL[NLEV - 1][:, q:q + 1]
                    tn = tiny[pi]
                    kb.ts("dve", tn[:, 0:1], wim[:, TT - 1:TT], s9, ALU.mult, [Bwi, t2b], [tinyb[pi]])
                    kb.ts("dve", tn[:, 1:2], wr[:, TT - 1:TT], s9, ALU.mult, [Bwr, t2b], [tinyb[pi]])
                    kb.stt(init[pi][:, 0:1], wr[:, TT - 1:TT], c9, tn[:, 0:1], ALU.mult, ALU.subtract, [Bwr, tinyb[pi], t2b], [inb[pi]])
                    kb.stt(init[pi][:, 1:2], wim[:, TT - 1:TT], c9, tn[:, 1:2], ALU.mult, ALU.add, [Bwi, tinyb[pi], t2b], [inb[pi]])
                for pi in R:
                    br_, bi_, t1, t2, t3, t4, wr, wim = wk[pi]
                    Bbr, Bbi, B1, B2, B3, B4, Bwr, Bwi = wb2[pi]
                    kb.tt("dve", t1, Ec[pi], wr, ALU.mult, [Eb[pi], Bwr], [B1])
                    kb.tt("dve", t2, Es[pi], wim, ALU.mult, [Eb[pi], Bwi], [B2])
                    kb.tt("dve", t4, Ec[pi], wim, ALU.mult, [Eb[pi], Bwi], [B4])
                    kb.tt("dve", t3, Es[pi], wr, ALU.mult, [Eb[pi], Bwr], [B3])
                for pi in R:
                    br_, bi_, t1, t2, t3, t4, wr, wim = wk[pi]
                    Bbr, Bbi, B1, B2, B3, B4, Bwr, Bwi = wb2[pi]
                    kb.tt("dve", x16[pi][0], t1, t2, ALU.subtract, [B1, B2], [x16b[pi][0]])
                    kb.tt("dve", x16[pi][1], t3, t4, ALU.add, [B3, B4], [x16b[pi][1]])
                for pi in R:
                    q = c * NPC + pi
                    kb.mm([(banks[ybank], CRE[:, q * 128:(q + 1) * 128], x16[pi][0], pi == 0, False),
                           (banks[ybank], CIMN[:, q * 128:(q + 1) * 128], x16[pi][1], False, pi == NPC - 1)],
                          [Lb, x16b[pi][0], x16b[pi][1]], [bankbuf[ybank]])
                g = gc % 2
                gc += 1
                kb.stt(gout[g], u32[k], d32[:, c:c + 1], banks[ybank], ALU.mult, ALU.add, [u32b[k], t2b, bankbuf[ybank]], [goutb[g]])
                kb.act(gout[g], gout[g], AF.Gelu, [goutb[g]], [goutb[g]])
                store_tile(GS5, c * 128, t0, TT, gout[g], goutb[g])


    ONES_T = None

    def phase_glu():
        new_phase()
        w = W["w_glu"]
        a32 = p32.alloc(SC * TT)
        abuf = Buf()
        xn = p16.alloc(SC * TT)
        xnbuf = Buf()
        ws = WStream(w["kc"] * w["nblk"])
        ev = [p32.alloc(TT) for _ in range(4)]
        evb = [Buf() for _ in range(4)]
        cnt = [0]
        for tt in range(NT):
            t0 = tt * TT
            load_act_tile(GS5, SC, t0, TT, a32, abuf)
            for kc in range(SC):
                kb.cast_any(xn[:, kc * TT:(kc + 1) * TT], a32[:, kc * TT:(kc + 1) * TT], [abuf], [xnbuf], engines=("act", "dve"))

            def evac(ti, c, bank, t0=t0):
                k = cnt[0] % 4
                cnt[0] += 1
                ch = ti * 2 + c
                kb.act(ev[k], banks[bank], AF.Sigmoid, [bankbuf[bank]], [evb[k]])
                kb.tt("dve", ev[k], ev[k], a32[:, ch * TT:(ch + 1) * TT], ALU.mult, [evb[k], abuf], [evb[k]])
                store_tile(MIX, ch * 128, t0, TT, ev[k], evb[k])
            gemm_pass(w, xn, xnbuf, TT, ws, evac)

    def phase_resid(w, src, src_dt, Kc, hold, hnew):
        new_phase()
        if src_dt == F32:
            a32 = p32.alloc(Kc * TT)
            abuf = Buf()
        xn = p16.alloc(Kc * TT)
        xnbuf = Buf()
        ws = WStream(w["kc"] * w["nblk"])
        hin = [p32.alloc(TT) for _ in range(4)]
        hinb = [Buf() for _ in range(4)]
        cnt = [0]
        nchunks = w["ntiles"] * (w["nblk"] // 128)
        for tt in range(NT):
            t0 = tt * TT
            if src_dt == F32:
                load_act_tile(src, Kc, t0, TT, a32, abuf)
                for kc in range(Kc):
                    kb.cast_any(xn[:, kc * TT:(kc + 1) * TT], a32[:, kc * TT:(kc + 1) * TT], [abuf], [xnbuf])
            else:
                v = src.rearrange("(k p) t -> p k t", p=128)
                x3 = xn.rearrange("p (k t) -> p k t", k=Kc)
                for k0 in range(0, Kc, 8):
                    k1 = min(Kc, k0 + 8)
                    P.dma(x3[:, k0:k1, :], v[:, k0:k1, t0:t0 + TT], [], [xnbuf], xnbuf)
            pend = []

            def evac(ti, c, bank, t0=t0):
                k = cnt[0] % 4
                cnt[0] += 1
                ch = ti * (w["nblk"] // 128) + c
                P.dma(hin[k], hold[ch * 128:(ch + 1) * 128, t0:t0 + TT], [], [hinb[k]], hinb[k])
                kb.tt("dve", hin[k], banks[bank], hin[k], ALU.add, [bankbuf[bank], hinb[k]], [hinb[k]])
                store_tile(hnew, ch * 128, t0, TT, hin[k], hinb[k])
            gemm_pass(w, xn, xnbuf, TT, ws, evac)

    def phase_ffn_up(l, hsrc):
        new_phase()
        w = W["w_up%d" % l]
        a32 = p32.alloc(KC * TT)
        abuf = Buf()
        xn = p16.alloc(KC * TT)
        xnbuf = Buf()
        sq = [p32.alloc(TT) for _ in range(2)]
        sqb = [Buf() for _ in range(2)]
        rstd = p32.alloc(TT)
        rb = Buf()
        cp = p32.alloc(2 * FC * 4)
        cpb = Buf()
        P.dma(cp, convp[:, l * 2 * FC * 4:(l + 1) * 2 * FC * 4], [], [cpb], cpb)
        tails = p32.alloc(2 * FC * 2)
        tlb = Buf()
        P.op("dve", lambda e: e.memset(tails, 0.0), [], [tlb])
        ws = WStream(w["kc"] * w["nblk"])
        NB_ = 2
        ua = [[p32.alloc(TT + 2) for _ in range(2)] for _ in range(NB_)]
        uab = [Buf() for _ in range(NB_)]
        cv = [[p32.alloc(TT) for _ in range(2)] for _ in range(NB_)]
        cvb = [Buf() for _ in range(NB_)]
        hid = [p16.alloc(TT) for _ in range(NB_)]
        hidb = [Buf() for _ in range(NB_)]
        cnt = [0]
        for tt in range(NT):
            t0 = tt * TT
            load_act_tile(hsrc, KC, t0, TT, a32, abuf)
            rms_tile(a32, abuf, KC, TT, (4 + l) * KC, xn, xnbuf, sq, sqb, rstd, rb, 7, None)

            def evac(ti, c, bank, t0=t0):
                k = (cnt[0] // 2) % NB_
                cnt[0] += 1
                u = ua[k][c]
                pc = cp[:, (c * FC + ti) * 4:(c * FC + ti) * 4 + 4]
                tl = tails[:, (c * FC + ti) * 2:(c * FC + ti) * 2 + 2]
                kb.copy("act", u[:, 0:2], tl, [tlb], [uab[k]])
                kb.copy("act" if c == 0 else "dve", u[:, 2:TT + 2], banks[bank], [bankbuf[bank]], [uab[k]])
                kb.copy("act", tl, u[:, TT:TT + 2], [uab[k]], [tlb])
                o = cv[k][c]
                kb.act(o, u[:, 2:TT + 2], AF.Identity, [uab[k], cpb], [cvb[k]], scale=pc[:, 2:3], bias=pc[:, 3:4])
                kb.stt(o, u[:, 1:TT + 1], pc[:, 1:2], o, ALU.mult, ALU.add, [uab[k], cpb, cvb[k]], [cvb[k]])
                kb.stt(o, u[:, 0:TT], pc[:, 0:1], o, ALU.mult, ALU.add, [uab[k], cpb, cvb[k]], [cvb[k]])
                if c == 1:
                    kb.act(cv[k][0], cv[k][0], AF.Silu, [cvb[k]], [cvb[k]])
                    kb.tt("dve", hid[k], cv[k][0], cv[k][1], ALU.mult, [cvb[k]], [hidb[k]])
                    store_tile(HID, ti * 128, t0, TT, hid[k], hidb[k])
            gemm_pass(w, xn, xnbuf, TT, ws, evac)

    def phase_kv(hsrc):
        new_phase()
        w = W["w_kv"]
        a32 = p32.alloc(KC * TT)
        abuf = Buf()
        xn = p16.alloc(KC * TT)
        xnbuf = Buf()
        sq = [p32.alloc(TT) for _ in range(2)]
        sqb = [Buf() for _ in range(2)]
        rstd = p32.alloc(TT)
        rb = Buf()
        ws = WStream(w["kc"] * w["nblk"])
        ev = [p16.alloc(TT) for _ in range(4)]
        evb = [Buf() for _ in range(4)]
        cnt = [0]
        nkt = SSM_W // 256
        for tt in range(NT):
            t0 = tt * TT
            load_act_tile(hsrc, KC, t0, TT, a32, abuf)
            rms_tile(a32, abuf, KC, TT, 6 * KC, xn, xnbuf, sq, sqb, rstd, rb, 7, None)

            def evac(ti, c, bank, t0=t0):
                k = cnt[0] % 4
                cnt[0] += 1
                kb.copy("act" if k % 2 else "dve", ev[k], banks[bank], [bankbuf[bank]], [evb[k]])
                store_tile(KT, (ti * 2 + c) * 128, t0, TT, ev[k], evb[k])
            gemm_pass(w, xn, xnbuf, TT, ws, evac, tiles=list(range(nkt)))
            vt = list(range(nkt, 2 * nkt))
            for j in range(min(2, len(vt))):
                ws.prefetch(w, vt[j])
            for idx, ti in enumerate(vt):
                if idx + 2 < len(vt):
                    ws.prefetch(w, vt[idx + 2])
                wt, wb = ws.pop()
                for tc in range(TT // 128):
                    bank = gemm_pass.bank_rr % 4
                    gemm_pass.bank_rr += 1
                    items = [(banks[bank][:, 0:256], xn[:, kc * TT + tc * 128: kc * TT + (tc + 1) * 128],
                              wt[:, kc * 256:(kc + 1) * 256], kc == 0, kc == KC - 1) for kc in range(KC)]
                    kb.mm(items, [wb, xnbuf], [bankbuf[bank]])
                    k = cnt[0] % 4
                    cnt[0] += 1
                    kb.copy("act" if k % 2 else "dve", ev[k][:, 0:256], banks[bank][:, 0:256], [bankbuf[bank]], [evb[k]])
                    P.dma(VTM[t0 + tc * 128: t0 + (tc + 1) * 128, (ti - nkt) * 256:(ti - nkt + 1) * 256], ev[k][:, 0:256],
                          [evb[k]], [], evb[k])

    def phase_sb():
        new_phase()
        NKC = SEQ // 128
        scale = 1.0 / math.sqrt(128.0)
        kT = [p16.alloc(SEQ) for _ in range(2)]
        kTb = [Buf() for _ in range(2)]
        vt = [p16.alloc(NKC * 128) for _ in range(2)]
        vtb = [Buf() for _ in range(2)]
        q32 = [p32.alloc(TT) for _ in range(2)]
        q32b = [Buf() for _ in range(2)]
        q16 = [p16.alloc(TT) for _ in range(3)]
        q16b = [Buf() for _ in range(3)]
        NW = 4
        E = [p32.alloc(TT) for _ in range(NW)]
        Eb_ = [Buf() for _ in range(NW)]
        SP_ = [p32.alloc(TT) for _ in range(NW)]
        SPb = [Buf() for _ in range(NW)]
        X = [p32.alloc(TT) for _ in range(NW)]
        Xb = [Buf() for _ in range(NW)]
        sacc = [p32.alloc(TT) for _ in range(3)]
        saccb = [Buf() for _ in range(3)]
        W16 = [p16.alloc(TT) for _ in range(NW)]
        W16b = [Buf() for _ in range(NW)]
        ev = [p32.alloc(TT) for _ in range(2)]
        evb = [Buf() for _ in range(2)]
        tasks = []
        ti_ = 0
        for h in range(H):
            for tt in range(NT):
                nk = 4 * tt + 4
                for kc in range(nk - 1, -1, -1):
                    tasks.append(dict(h=h, tt=tt, kc=kc, first=(kc == nk - 1), last=(kc == 0), dj=kc - 4 * tt,
                                      tile=ti_, w=len(tasks) % NW, zb=len(tasks) % 2, cb=2 + len(tasks) % 3))
                ti_ += 1
        sidx = [0]

        def S1(t):
            h, tt, kc, w_ = t["h"], t["tt"], t["kc"], t["w"]
            hk = h % 2
            if t["first"]:
                if tt == 0:
                    P.dma(kT[hk], KT[h * 128:(h + 1) * 128, :], [], [kTb[hk]], kTb[hk])
                    v3 = vt[hk].rearrange("p (k d) -> p k d", k=NKC)
                    vs = VTM.rearrange("(k p) d -> p k d", p=128)
                    for k0 in range(0, NKC, 8):
                        k1 = min(NKC, k0 + 8)
                        P.dma(v3[:, k0:k1, :], vs[:, k0:k1, h * 128:(h + 1) * 128], [], [vtb[hk]], vtb[hk])
                k = t["tile"] % 2
                k3 = t["tile"] % 3
                P.dma(q32[k], PROJ[h * 128:(h + 1) * 128, tt * TT:(tt + 1) * TT], [], [q32b[k]], q32b[k])
                kb.copy("dve", q16[k3], q32[k], [q32b[k]], [q16b[k3]])
            k3 = t["tile"] % 3
            zb = t["zb"]
            kb.mm([(banks[zb], kT[hk][:, kc * 128:(kc + 1) * 128], q16[k3], True, True)], [kTb[hk], q16b[k3]], [bankbuf[zb]])
            kb.act(E[w_], banks[zb], AF.Exp, [bankbuf[zb]], [Eb_[w_]], scale=scale)
            kb.act(SP_[w_], E[w_], AF.Ln, [Eb_[w_], cstb], [SPb[w_]], bias=onesb[:, 0:1])
            if t["dj"] >= 0:
                kb.tt("dve", SP_[w_], SP_[w_], MASK[t["dj"]], ALU.mult, [SPb[w_], cstb], [SPb[w_]])

        def S2(t):
            w_ = t["w"]
            cbk = t["cb"]
            so, sob = sacc[sidx[0] % 3], saccb[sidx[0] % 3]
            sn_, snb = sacc[(sidx[0] + 1) % 3], saccb[(sidx[0] + 1) % 3]
            if t["first"]:
                kb.mm([(banks[cbk], TRI32, SP_[w_], True, True)], [SPb[w_], cstb], [bankbuf[cbk]])
            else:
                kb.mm([(banks[cbk], TRI32, SP_[w_], True, False), (banks[cbk], ONES32, so, False, True)],
                      [SPb[w_], cstb, sob], [bankbuf[cbk]])
            if not t["last"]:
                if t["first"]:
                    kb.copy("dve", sn_, SP_[w_], [SPb[w_]], [snb])
                else:
                    kb.tt("dve", sn_, so, SP_[w_], ALU.add, [SPb[w_], sob], [snb])
                sidx[0] += 1

        def S3(t):
            h, tt, kc, w_ = t["h"], t["tt"], t["kc"], t["w"]
            hk = h % 2
            cbk = t["cb"]
            obank = 6 + (t["tile"] % 2)
            kb.act(X[w_], banks[cbk], AF.Exp, [bankbuf[cbk]], [Xb[w_]], scale=-1.0)
            if t["dj"] >= 0:
                kb.tt("dve", X[w_], X[w_], MASK[t["dj"]], ALU.mult, [Xb[w_], cstb], [Xb[w_]])
            kb.tt("dve", W16[w_], E[w_], X[w_], ALU.mult, [Eb_[w_], Xb[w_]], [W16b[w_]])
            kb.mm([(banks[obank], vt[hk][:, kc * 128:(kc + 1) * 128], W16[w_], t["first"], t["last"])],
                  [vtb[hk], W16b[w_]], [bankbuf[obank]])
            if t["last"]:
                e_ = t["tile"] % 2
                kb.copy("act", ev[e_], banks[obank], [bankbuf[obank]], [evb[e_]])
                store_tile(MIX, h * 128, tt * TT, TT, ev[e_], evb[e_])

        n = len(tasks)
        for i in range(n + 3):
            if i < n:
                S1(tasks[i])
            if 0 <= i - 1 < n:
                S2(tasks[i - 1])
            if 0 <= i - 3 < n:
                S3(tasks[i - 3])

    def dump(nm):
        new_phase()
        src = kb.dr[nm]
        b = Buf()
        rows = src.shape[0]
        for r0 in range(0, rows, 128):
            P.dma(dbg[nm][r0:r0 + 128, :], src[r0:r0 + 128, :], [], [b], b)

    ONES_T = p32.alloc(TT)
    P.op("dve", lambda e: e.memset(ONES_T, 1.0), [], [cstb])
    onesb = p32.alloc(1)
    P.op("dve", lambda e: e.memset(onesb, 1.0), [], [cstb])
    base32 = p32.off

    stages = cfg.stages if hasattr(cfg, "stages") else None

    def want(s):
        return stages is None or s in stages

    if want("cast"):
        cast_weights()
    with nc.allow_low_precision("bf16 matmuls with fp32 accumulation, as the reference tolerance assumes"):
        if want("in0"):
            phase_inproj(0, xT)
        if want("mem0"):
            phase_memattn(0)
        if want("s5"):
            phase_s5()
        if want("glu"):
            phase_glu()
        if want("out0"):
            phase_resid(W["w_out0"], MIX, F32, XC, xT, HA)
        if want("up0"):
            phase_ffn_up(0, HA)
        if want("dn0"):
            phase_resid(W["w_dn0"], HID, BF16, FC, HA, HB)
        if want("kv"):
            phase_kv(HB)
        if want("in1"):
            phase_inproj(1, HB)
        if want("mem1"):
            phase_memattn(1)
        if want("sb"):
            phase_sb()
        if want("out1"):
            phase_resid(W["w_out1"], MIX, F32, XC, HB, HA)
        if want("up1"):
            phase_ffn_up(1, HA)
        if want("dn1"):
            phase_resid(W["w_dn1"], HID, BF16, FC, HA, yT)
        for nm in debug_outs:
            dump(nm)
        P.barrier()
        P.emit()
    es.close()
    return nc


def tile_w(Wm, nblk, col_groups=None):
    Kd, N = Wm.shape
    kc = Kd // 128
    if col_groups is None:
        nt = N // nblk
        x = Wm.reshape(kc, 128, nt, nblk).transpose(2, 1, 0, 3)
    else:
        x = Wm[:, col_groups.reshape(-1)].reshape(kc, 128, col_groups.shape[0], nblk).transpose(2, 1, 0, 3)
        nt = col_groups.shape[0]
    return np.ascontiguousarray(x).reshape(nt * 128, kc * nblk)


def chunkvec(v):
    return np.ascontiguousarray(v.reshape(-1, 128).T)


def host_layout(cfg, inp):
    f = np.float32
    D, KC, FC, NPAIR, SC = cfg.D, cfg.KC, cfg.FC, cfg.NPAIR, cfg.SSM_W // 128
    shared = {}
    gains = [chunkvec(inp["norm_mix_g"][0]), chunkvec(inp["norm_mix_g"][1]), chunkvec(inp["mem_norm_g"][0]),
             chunkvec(inp["mem_norm_g"][1]), chunkvec(inp["norm_ffn_g"][0]), chunkvec(inp["norm_ffn_g"][1]),
             chunkvec(inp["kv_norm_g"])]
    shared["gains"] = np.ascontiguousarray(np.concatenate(gains, axis=1), dtype=f)
    qk = []
    for l in range(2):
        qk += [chunkvec(inp["mem_q_norm_g"][l]), chunkvec(inp["mem_k_norm_g"][l])]
    shared["qkg"] = np.ascontiguousarray(np.concatenate(qk, axis=1), dtype=f)
    cw = inp["ffn_conv_w"]
    cb = inp["ffn_conv_b"]
    cp = np.zeros((128, 2, 2, FC, 4), f)
    for l in range(2):
        for ab in range(2):
            sl = slice(ab * cfg.D_FF, (ab + 1) * cfg.D_FF)
            for i in range(3):
                cp[:, l, ab, :, i] = chunkvec(cw[l, i, sl])
            cp[:, l, ab, :, 3] = chunkvec(cb[l, sl])
    shared["convp"] = cp.reshape(128, -1)
    lam_re, lam_im, ls = inp["s5_lam_re"][0], inp["s5_lam_im"][0], inp["s5_log_step"][0]
    G = cfg.G

    def pairlay(a):
        return np.ascontiguousarray(a.reshape(NPAIR, 2, 64).transpose(1, 2, 0).reshape(128, NPAIR))
    lsb = np.repeat(ls[:, None], 64, axis=1)
    par = np.stack([pairlay(lam_re), pairlay(lam_im), pairlay(lsb)], axis=1)
    shared["s5par"] = np.ascontiguousarray(par.reshape(128, -1), dtype=f)
    rep = np.stack([a.reshape(NPAIR * 128) for a in (lam_re.reshape(NPAIR, 128), lam_im.reshape(NPAIR, 128),
                                                       lsb.reshape(NPAIR, 128))], axis=0)
    shared["s5rep"] = np.ascontiguousarray(np.broadcast_to(rep.reshape(1, -1), (128, 3 * NPAIR * 128)), dtype=f)
    Bp = np.zeros((128, 2, NPAIR, 2, 64), f)
    Cp = np.zeros((128, 2, NPAIR, 128), f)
    for which, (bsrc, csrc) in enumerate(((inp["s5_b_re"][0], inp["s5_c_re"][0]), (inp["s5_b_im"][0], inp["s5_c_im"][0]))):
        for q in range(NPAIR):
            for e in range(2):
                g = 2 * q + e
                r0 = (q % 4) * 32 + e * 16
                Bp[r0:r0 + 16, which, q, e, :] = bsrc[g].T
                Cp[e * 64:(e + 1) * 64, which, q, r0:r0 + 16] = csrc[g].T
    shared["s5B"] = Bp.reshape(128, -1)
    shared["s5C"] = Cp.reshape(128, -1)
    shared["s5D"] = chunkvec(inp["s5_d"][0].reshape(-1)).astype(f)
    tri = (np.arange(128)[:, None] >= np.arange(128)[None, :]).astype(f)
    ones = np.ones((128, 128), f)
    masks = []
    for dj in range(4):
        s = np.arange(128)[:, None] + 128 * dj
        t = np.arange(TT)[None, :]
        masks.append((s < t).astype(f))
    shared["consts"] = np.concatenate([tri, ones] + masks, axis=1)
    for l in range(2):
        shared["w_in%d" % l] = tile_w(inp["w_in"][l], 256)
        shared["w_out%d" % l] = tile_w(inp["w_out"][l], 256)
        shared["w_mkv%d" % l] = tile_w(inp["w_mem_kv"][l], 256)
        cg = np.stack([np.concatenate([np.arange(j * 128, (j + 1) * 128), cfg.D_FF + np.arange(j * 128, (j + 1) * 128)])
                       for j in range(FC)])
        shared["w_up%d" % l] = tile_w(inp["w_ffn_up"][l], 256, cg)
        shared["w_dn%d" % l] = tile_w(inp["w_ffn_down"][l], 128)
    shared["w_glu"] = tile_w(inp["s5_w_glu"][0], 256)
    shared["w_kv"] = tile_w(inp["w_kv_shared"], 256)
    in_maps = []
    for b in range(cfg.B):
        m = dict(shared)
        m["xT"] = np.ascontiguousarray(inp["x"][b].T)
        m["memT"] = np.ascontiguousarray(inp["mem"][b].T)
        in_maps.append(m)
    return in_maps


_CACHE = {}


def run(cfg, inputs, debug_outs=(), trace=False):
    inp = {k: np.asarray(v) for k, v in inputs.items()}
    in_maps = host_layout(cfg, inp)
    key = (id(cfg), tuple(debug_outs))
    nc = build(cfg, debug_outs)
    res = run_bass_kernel_spmd(nc, in_maps, core_ids=list(range(cfg.B)))
    return res


def kernel(**inputs):
    cfg = FULL
    res = run(cfg, inputs)
    out = np.stack([np.ascontiguousarray(res.results[b]["yT"].T) for b in range(cfg.B)], axis=0)
    return out.astype(np.float32)
```

```python
import math
import numpy as np
import ml_dtypes
import concourse.bass as bass
import concourse.mybir as mybir
from concourse.bass_utils import run_bass_kernel_spmd

F32 = mybir.dt.float32
BF16 = mybir.dt.bfloat16
I32 = mybir.dt.int32
AF = mybir.ActivationFunctionType
ALU = mybir.AluOpType

SEM_LIMIT = 16000
TT = 512


class Cfg:
    def __init__(self, D=4096, SEQ=4096, B=2, SSM_W=2048, MEM_HEADS=4, MEM_TOKENS=256, D_FF=11008):
        self.D = D
        self.SEQ = SEQ
        self.B = B
        self.SSM_W = SSM_W
        self.G = SSM_W // 16
        self.NPAIR = self.G // 2
        self.SB_HEADS = SSM_W // 128
        self.MEM_HEADS = MEM_HEADS
        self.MEM_W = MEM_HEADS * 256
        self.MIX_W = SSM_W + self.MEM_W
        self.MEM_TOKENS = MEM_TOKENS
        self.D_FF = D_FF
        self.KC = D // 128
        self.NT = SEQ // TT
        self.FC = D_FF // 128


FULL = Cfg()


class Sem:
    def __init__(self, nc, name):
        self.h = nc.alloc_semaphore(name)
        self.v = 0


class Buf:
    __slots__ = ("name", "w", "r", "dsem")

    def __init__(self, name=""):
        self.name = name
        self.w = []
        self.r = {}
        self.dsem = None


class Stream:
    def __init__(self, P, name, eng):
        self.P = P
        self.name = name
        self.eng = eng
        self.sems = []
        self.sem = None
        self.waited = {}
        self.insts = []

    def cur_sem(self):
        if self.sem is None or self.sem.v >= SEM_LIMIT:
            self.sem = Sem(self.P.nc, "e%s%d" % (self.name, len(self.sems)))
            self.sems.append(self.sem)
        return self.sem


class Prog:
    def __init__(self, nc):
        self.nc = nc
        self.st = {
            "pe": Stream(self, "pe", nc.tensor),
            "act": Stream(self, "act", nc.scalar),
            "dve": Stream(self, "dve", nc.vector),
            "pool": Stream(self, "pool", nc.gpsimd),
            "sp": Stream(self, "sp", nc.sync),
        }
        self.dma_pool = []
        self.dma_live = []
        self.ndsem = 0
        self.rr = 0

    def _deps(self, reads, writes):
        d = {}
        for b in reads:
            for (s, v) in b.w:
                if d.get(s, 0) < v:
                    d[s] = v
        for b in writes:
            for (s, v) in b.w:
                if d.get(s, 0) < v:
                    d[s] = v
            for s, v in b.r.items():
                if d.get(s, 0) < v:
                    d[s] = v
        return d

    def _waits(self, S, d):
        waits = []
        for s, v in d.items():
            if S.name == "pe" and s in S.sems:
                continue
            if S.waited.get(s, 0) >= v:
                continue
            S.waited[s] = v
            waits.append((s, v))
        return waits

    def _mark(self, tok, reads, writes):
        s, v = tok
        for b in reads:
            if b.r.get(s, 0) < v:
                b.r[s] = v
        for b in writes:
            b.w = [tok]
            b.r = {}

    def op(self, st, fn, reads=(), writes=()):
        S = self.st[st]
        waits = self._waits(S, self._deps(reads, writes))
        sem = S.cur_sem()
        sem.v += 1
        tok = (sem, sem.v)
        S.insts.append((waits, fn, sem, 1))
        self._mark(tok, reads, writes)
        return tok

    def _dsem(self, owner):
        if owner.dsem is None or owner.dsem.v >= SEM_LIMIT:
            if self.dma_pool:
                owner.dsem = self.dma_pool.pop()
            else:
                owner.dsem = Sem(self.nc, "d%d" % self.ndsem)
                self.ndsem += 1
            self.dma_live.append(owner.dsem)
        return owner.dsem

    def dma(self, out, in_, reads, writes, owner, st="sp", bg=False):
        S = self.st[st]
        waits = self._waits(S, self._deps(reads, writes))
        if bg:
            if owner.dsem is None:
                owner.dsem = Sem(self.nc, "g%d" % self.ndsem)
                self.ndsem += 1
            sem = owner.dsem
        else:
            sem = self._dsem(owner)
        sem.v += 16
        tok = (sem, sem.v)

        def fn(e, out=out, in_=in_):
            return e.dma_start(out=out, in_=in_)

        S.insts.append((waits, fn, sem, 16))
        self._mark(tok, reads, writes)
        return tok

    def barrier(self):
        toks = {}
        for S in self.st.values():
            if S.sem is not None and S.sem.v > 0:
                toks[S.sem] = S.sem.v
        for s in self.dma_live:
            if s.v > 0:
                toks[s] = s.v
        for S in self.st.values():
            waits = self._waits(S, dict(toks))
            if waits:
                S.insts.append((waits, None, None, 0))
        for s in self.dma_live:
            if s.v < SEM_LIMIT and s not in self.dma_pool:
                self.dma_pool.append(s)
        self.dma_live = []

    def emit(self):
        nc = self.nc
        with nc.Block() as block:
            def run(S):
                def body(e):
                    for (waits, fn, sem, inc) in S.insts:
                        for (s, v) in waits:
                            e.wait_ge(s.h, v)
                        if fn is not None:
                            ins = fn(e)
                            ins.then_inc(sem.h, inc)
                return body
            block.tensor(run(self.st["pe"]))
            block.scalar(run(self.st["act"]))
            block.vector(run(self.st["dve"]))
            block.gpsimd(run(self.st["pool"]))
            block.sync(run(self.st["sp"]))


class Alloc:
    def __init__(self, t, ncols):
        self.t = t
        self.n = ncols
        self.off = 0

    def take(self, units):
        a = self.off
        al = (units + 15) // 16 * 16
        assert a + al <= self.n, ("sbuf pool overflow", a, units, self.n)
        self.off += al
        return a


class Pool2:
    def __init__(self, al, bf):
        self.al = al
        self.bf = bf

    @property
    def off(self):
        return self.al.off

    @off.setter
    def off(self, v):
        self.al.off = v

    def alloc(self, ncols):
        if self.bf:
            units = (ncols + 1) // 2
            a = self.al.take(units)
            return self.al.t[:, a:a + units].bitcast(BF16)[:, 0:ncols]
        a = self.al.take(ncols)
        return self.al.t[:, a:a + ncols]


class K:
    def __init__(self, cfg):
        self.cfg = cfg
        self.nc = bass.Bass("TRN2", target_bir_lowering=False)
        self.P = Prog(self.nc)
        self.dr = {}
        self.cast_rr = 0

    def din(self, name, shape, dt=F32):
        t = self.nc.dram_tensor(name, list(shape), dt, kind="ExternalInput").ap()
        self.dr[name] = t
        return t

    def dscr(self, name, shape, dt=F32):
        t = self.nc.dram_tensor(name, list(shape), dt, kind="Internal").ap()
        self.dr[name] = t
        return t

    def dout(self, name, shape, dt=F32):
        t = self.nc.dram_tensor(name, list(shape), dt, kind="ExternalOutput").ap()
        self.dr[name] = t
        return t

    def act(self, out, in_, func, reads, writes, scale=None, bias=None):
        kw = {}
        if scale is not None:
            kw["scale"] = scale
        if bias is not None:
            kw["bias"] = bias
        return self.P.op("act", lambda e: e.activation(out=out, in_=in_, func=func, **kw), reads, writes)

    def tt(self, st, out, in0, in1, op, reads, writes):
        return self.P.op(st, lambda e: e.tensor_tensor(out=out, in0=in0, in1=in1, op=op), reads, writes)

    def ts(self, st, out, in0, s1, op0, reads, writes, s2=None, op1=None):
        if op1 is None:
            return self.P.op(st, lambda e: e.tensor_scalar(out=out, in0=in0, scalar1=s1, scalar2=None, op0=op0),
                             reads, writes)
        return self.P.op(st, lambda e: e.tensor_scalar(out=out, in0=in0, scalar1=s1, scalar2=s2, op0=op0, op1=op1),
                         reads, writes)

    def stt(self, out, in0, scalar, in1, op0, op1, reads, writes):
        return self.P.op("dve", lambda e: e.scalar_tensor_tensor(out=out, in0=in0, scalar=scalar, in1=in1,
                                                                 op0=op0, op1=op1), reads, writes)

    def copy(self, st, out, in_, reads, writes):
        if st == "act":
            return self.P.op("act", lambda e: e.activation(out=out, in_=in_, func=AF.Copy), reads, writes)
        return self.P.op(st, lambda e: e.tensor_copy(out=out, in_=in_), reads, writes)

    def cast_any(self, out, in_, reads, writes, engines=("act", "dve")):
        st = engines[self.cast_rr % len(engines)]
        self.cast_rr += 1
        return self.copy(st, out, in_, reads, writes)

    def mm(self, items, reads, writes):
        def fn(e):
            ins = None
            for (o, l, r, st, sp) in items:
                ins = e.matmul(o, l, r, start=st, stop=sp)
            return ins
        return self.P.op("pe", fn, reads, writes)


def build(cfg, debug_outs=()):
    kb = K(cfg)
    nc = kb.nc
    P = kb.P
    D, SEQ, KC, NT, FC = cfg.D, cfg.SEQ, cfg.KC, cfg.NT, cfg.FC
    SSM_W, MEM_W, MIX_W = cfg.SSM_W, cfg.MEM_W, cfg.MIX_W
    SC = SSM_W // 128
    MC = MEM_W // 128
    XC = MIX_W // 128
    MT = cfg.MEM_TOKENS
    NPAIR = cfg.NPAIR
    H = cfg.SB_HEADS

    xT = kb.din("xT", [D, SEQ])
    memT = kb.din("memT", [D, MT])
    gains = kb.din("gains", [128, 7 * KC])
    qkg = kb.din("qkg", [128, 8])
    convp = kb.din("convp", [128, 2 * 2 * FC * 4])
    s5rep = kb.din("s5rep", [128, 3 * NPAIR * 128])
    s5par = kb.din("s5par", [128, 3 * NPAIR])
    s5B = kb.din("s5B", [128, 2 * NPAIR * 128])
    s5C = kb.din("s5C", [128, 2 * NPAIR * 128])
    s5D = kb.din("s5D", [128, SC])
    consts = kb.din("consts", [128, 128 + 128 + 4 * TT])

    def wspec(name, Kd, ntiles, nblk):
        kc = Kd // 128
        w32 = kb.din(name, [ntiles * 128, kc * nblk])
        w16 = kb.dscr(name + "_bf", [ntiles * 128, kc * nblk], BF16)
        return dict(name=name, w32=w32, w16=w16, kc=kc, ntiles=ntiles, nblk=nblk)

    W = {}
    for l in range(2):
        W["w_in%d" % l] = wspec("w_in%d" % l, D, MIX_W // 256, 256)
        W["w_out%d" % l] = wspec("w_out%d" % l, MIX_W, D // 256, 256)
        W["w_mkv%d" % l] = wspec("w_mkv%d" % l, D, 2 * MEM_W // 256, 256)
        W["w_up%d" % l] = wspec("w_up%d" % l, D, FC, 256)
        W["w_dn%d" % l] = wspec("w_dn%d" % l, cfg.D_FF, D // 128, 128)
    W["w_glu"] = wspec("w_glu", SSM_W, SSM_W // 256, 256)
    W["w_kv"] = wspec("w_kv", D, 2 * SSM_W // 256, 256)

    HA = kb.dscr("HA", [D, SEQ])
    HB = kb.dscr("HB", [D, SEQ])
    PROJ = kb.dscr("PROJ", [MIX_W, SEQ])
    MIX = kb.dscr("MIX", [MIX_W, SEQ])
    GS5 = kb.dscr("GS5", [SSM_W, SEQ])
    HID = kb.dscr("HID", [cfg.D_FF, SEQ], BF16)
    KT = kb.dscr("KTs", [SSM_W, SEQ], BF16)
    VTM = kb.dscr("VTM", [SEQ, SSM_W], BF16)
    yT = kb.dout("yT", [D, SEQ])
    dbg = {}
    for nm in debug_outs:
        src = kb.dr[nm]
        dbg[nm] = kb.dout("dbg_" + nm, list(src.shape), src.dtype)

    import contextlib
    es = contextlib.ExitStack()
    NALL = 53000
    big = es.enter_context(nc.sbuf_tensor("big", [128, NALL], F32))
    al = Alloc(big, NALL)
    p32 = Pool2(al, False)
    p16 = Pool2(al, True)
    banks = [es.enter_context(nc.psum_tensor("bank%d" % i, [128, 512], F32))[:, :] for i in range(8)]
    bankbuf = [Buf("bank%d" % i) for i in range(8)]

    cst32 = p32.alloc(128 + 128 + 4 * TT)
    cstb = Buf("cst")
    P.dma(cst32, consts, [], [cstb], cstb)
    TRI32 = cst32[:, 0:128]
    ONES32 = cst32[:, 128:256]
    MASK = [cst32[:, 256 + i * TT:256 + (i + 1) * TT] for i in range(4)]
    ones16 = p16.alloc(128)
    kb.copy("dve", ones16, ONES32, [cstb], [cstb])
    gn32 = p32.alloc(7 * KC)
    P.dma(gn32, gains, [], [cstb], cstb)
    qk32 = p32.alloc(8)
    P.dma(qk32, qkg, [], [cstb], cstb)
    base32 = p32.off

    def new_phase():
        P.barrier()
        p32.off = base32
        for b in bankbuf:
            b.w = []
            b.r = {}

    CAST_ORDER = ["w_in0", "w_mkv0", "w_glu", "w_out0", "w_up0", "w_dn0", "w_kv", "w_in1", "w_mkv1", "w_out1",
                  "w_up1", "w_dn1"]

    def cast_weights_dma():
        for name in CAST_ORDER:
            w = W[name]
            b = Buf(name)
            w["buf"] = b
            for ti in range(w["ntiles"]):
                P.dma(w["w16"][ti * 128:(ti + 1) * 128, :], w["w32"][ti * 128:(ti + 1) * 128, :], [], [], b,
                      st="pool", bg=True)
            b.w = [(b.dsem, b.dsem.v)]

    def cast_weights():
        new_phase()
        PIECE = 4096
        NB = 6
        st32 = [p32.alloc(PIECE) for _ in range(NB)]
        st16 = [p16.alloc(PIECE) for _ in range(NB)]
        b32 = [Buf() for _ in range(NB)]
        b16 = [Buf() for _ in range(NB)]
        i = 0
        for name, w in W.items():
            rows = w["ntiles"] * 128
            cols = w["kc"] * w["nblk"]
            for r0 in range(0, rows, 128):
                for c0 in range(0, cols, PIECE):
                    c1 = min(cols, c0 + PIECE)
                    n = c1 - c0
                    k = i % NB
                    i += 1
                    P.dma(st32[k][:, 0:n], w["w32"][r0:r0 + 128, c0:c1], [], [b32[k]], b32[k])
                    kb.cast_any(st16[k][:, 0:n], st32[k][:, 0:n], [b32[k]], [b16[k]], engines=("act", "dve"))
                    P.dma(w["w16"][r0:r0 + 128, c0:c1], st16[k][:, 0:n], [b16[k]], [], b16[k], st="pool")


    def load_act_tile(src, Kc, t0, tw, a32, abuf):
        v = src.rearrange("(k p) t -> p k t", p=128)
        a3 = a32.rearrange("p (k t) -> p k t", k=Kc)
        for k0 in range(0, Kc, 8):
            k1 = min(Kc, k0 + 8)
            P.dma(a3[:, k0:k1, :], v[:, k0:k1, t0:t0 + tw], [], [abuf], abuf)

    def rms_tile(a32, abuf, Kc, tw, gcol, xn, xnbuf, sq, sqbuf, rstd, rbuf, bank, eps_scale):
        items = []
        for kc in range(Kc):
            k2 = kc % 2
            kb.act(sq[k2][:, 0:tw], a32[:, kc * tw:(kc + 1) * tw], AF.Square, [abuf], [sqbuf[k2]])
            kb.mm([(banks[bank][:, 0:tw], ONES32, sq[k2][:, 0:tw], kc == 0, kc == Kc - 1)],
                  [sqbuf[k2], cstb], [bankbuf[bank]])
        kb.act(rstd[:, 0:tw], banks[bank][:, 0:tw], AF.Sqrt, [bankbuf[bank], cstb], [rbuf],
               scale=1.0 / (Kc * 128), bias=epsb[:, 0:1])
        P.op("dve", lambda e: e.reciprocal(out=rstd[:, 0:tw], in_=rstd[:, 0:tw]), [rbuf], [rbuf])
        for kc in range(Kc):
            kb.stt(xn[:, kc * tw:(kc + 1) * tw], a32[:, kc * tw:(kc + 1) * tw], gn32[:, gcol + kc:gcol + kc + 1],
                   rstd[:, 0:tw], ALU.mult, ALU.mult, [abuf, rbuf, cstb], [xnbuf])

    epsb = p32.alloc(1)
    P.op("dve", lambda e: e.memset(epsb, 1e-6), [], [cstb])
    base32 = p32.off

    class WStream:
        def __init__(self, maxcols):
            self.bufs = [p16.alloc(maxcols) for _ in range(3)]
            self.bb = [Buf() for _ in range(3)]
            self.i = 0
            self.q = []

        def prefetch(self, w, ti):
            k = self.i % 3
            self.i += 1
            cols = w["kc"] * w["nblk"]
            P.dma(self.bufs[k][:, 0:cols], w["w16"][ti * 128:(ti + 1) * 128, :], [w["buf"]] if "buf" in w else [],
                  [self.bb[k]], self.bb[k])
            self.q.append((self.bufs[k], self.bb[k]))

        def pop(self):
            return self.q.pop(0)

    def gemm_pass(w, xn, xnbuf, tw, ws, evac, tiles=None, pre=2):
        tiles = list(range(w["ntiles"])) if tiles is None else tiles
        kc_n = w["kc"]
        nb = w["nblk"]
        for j in range(min(pre, len(tiles))):
            ws.prefetch(w, tiles[j])
        bi = 0
        for idx, ti in enumerate(tiles):
            if idx + pre < len(tiles):
                ws.prefetch(w, tiles[idx + pre])
            wt, wb = ws.pop()
            for c in range(nb // 128):
                bank = gemm_pass.bank_rr % 4
                gemm_pass.bank_rr += 1
                items = []
                for kc in range(kc_n):
                    items.append((banks[bank][:, 0:tw], wt[:, kc * nb + c * 128: kc * nb + (c + 1) * 128],
                                  xn[:, kc * tw:(kc + 1) * tw], kc == 0, kc == kc_n - 1))
                kb.mm(items, [wb, xnbuf], [bankbuf[bank]])
                evac(ti, c, bank)
    gemm_pass.bank_rr = 0

    def store_tile(dst, r0, t0, tw, src, sbuf):
        P.dma(dst[r0:r0 + 128, t0:t0 + tw], src, [sbuf], [], sbuf)

    def phase_inproj(l, hsrc):
        new_phase()
        w = W["w_in%d" % l]
        a32 = p32.alloc(KC * TT)
        abuf = Buf()
        xn = p16.alloc(KC * TT)
        xnbuf = Buf()
        sq = [p32.alloc(TT) for _ in range(2)]
        sqb = [Buf() for _ in range(2)]
        rstd = p32.alloc(TT)
        rb = Buf()
        ws = WStream(w["kc"] * w["nblk"])
        ev = [p32.alloc(TT) for _ in range(4)]
        evb = [Buf() for _ in range(4)]
        cnt = [0]
        for tt in range(NT):
            t0 = tt * TT
            load_act_tile(hsrc, KC, t0, TT, a32, abuf)
            rms_tile(a32, abuf, KC, TT, l * KC, xn, xnbuf, sq, sqb, rstd, rb, 7, None)

            def evac(ti, c, bank, t0=t0):
                k = cnt[0] % 4
                cnt[0] += 1
                kb.copy("act" if k % 2 else "dve", ev[k], banks[bank][:, 0:TT], [bankbuf[bank]], [evb[k]])
                store_tile(PROJ, (ti * 2 + c) * 128, t0, TT, ev[k], evb[k])
            gemm_pass(w, xn, xnbuf, TT, ws, evac)

    def phase_memattn(l):
        new_phase()
        w = W["w_mkv%d" % l]
        m32 = p32.alloc(KC * MT)
        mb = Buf()
        mn = p16.alloc(KC * MT)
        mnb = Buf()
        sq = [p32.alloc(TT) for _ in range(2)]
        sqb = [Buf() for _ in range(2)]
        rstd = p32.alloc(TT)
        rb = Buf()
        load_act_tile(memT, KC, 0, MT, m32, mb)
        rms_tile(m32, mb, KC, MT, (2 + l) * KC, mn, mnb, sq, sqb, rstd, rb, 7, None)
        ws = WStream(w["kc"] * w["nblk"])
        k32 = p32.alloc(MC * MT)
        k32b = Buf()
        kn = p16.alloc(MC * MT)
        knb = Buf()
        MCH = MT // 128
        vtm = p16.alloc(MCH * MEM_W)
        vtb = Buf()
        nkt = MEM_W // 256
        ws_i = 0
        for j in range(min(2, w["ntiles"])):
            ws.prefetch(w, j)
        for ti in range(w["ntiles"]):
            if ti + 2 < w["ntiles"]:
                ws.prefetch(w, ti + 2)
            wt, wb = ws.pop()
            if ti < nkt:
                for c in range(2):
                    bank = c
                    items = [(banks[bank][:, 0:MT], wt[:, kc * 256 + c * 128: kc * 256 + (c + 1) * 128],
                              mn[:, kc * MT:(kc + 1) * MT], kc == 0, kc == KC - 1) for kc in range(KC)]
                    kb.mm(items, [wb, mnb], [bankbuf[bank]])
                    ch = ti * 2 + c
                    kb.copy("act", k32[:, ch * MT:(ch + 1) * MT], banks[bank][:, 0:MT], [bankbuf[bank]], [k32b])
            else:
                d0 = (ti - nkt) * 256
                for mc in range(MCH):
                    bank = 2 + mc % 2
                    items = [(banks[bank][:, 0:256], mn[:, kc * MT + mc * 128: kc * MT + (mc + 1) * 128],
                              wt[:, kc * 256:(kc + 1) * 256], kc == 0, kc == KC - 1) for kc in range(KC)]
                    kb.mm(items, [wb, mnb], [bankbuf[bank]])
                    kb.copy("dve", vtm[:, mc * MEM_W + d0: mc * MEM_W + d0 + 256], banks[bank][:, 0:256],
                            [bankbuf[bank]], [vtb])
        for h in range(cfg.MEM_HEADS):
            for dc in range(2):
                ch = 2 * h + dc
                kb.act(sq[dc][:, 0:MT], k32[:, ch * MT:(ch + 1) * MT], AF.Square, [k32b], [sqb[dc]])
                kb.mm([(banks[7][:, 0:MT], ONES32, sq[dc][:, 0:MT], dc == 0, dc == 1)], [sqb[dc], cstb], [bankbuf[7]])
            kb.act(rstd[:, 0:MT], banks[7][:, 0:MT], AF.Sqrt, [bankbuf[7], cstb], [rb], scale=1.0 / 256, bias=epsb[:, 0:1])
            P.op("dve", lambda e: e.reciprocal(out=rstd[:, 0:MT], in_=rstd[:, 0:MT]), [rb], [rb])
            for dc in range(2):
                ch = 2 * h + dc
                kb.stt(kn[:, ch * MT:(ch + 1) * MT], k32[:, ch * MT:(ch + 1) * MT],
                       qk32[:, 4 * l + 2 + dc:4 * l + 3 + dc], rstd[:, 0:MT], ALU.mult, ALU.mult,
                       [k32b, rb, cstb], [knb])
        q32 = p32.alloc(MC * TT)
        qb = Buf()
        qn = p16.alloc(MC * TT)
        qnb = Buf()
        pT = p16.alloc(MCH * TT)
        pTb = Buf()
        rden = p32.alloc(TT)
        rdb = Buf()
        ev = [p32.alloc(TT) for _ in range(2)]
        evb = [Buf() for _ in range(2)]
        cnt = 0
        for tt in range(NT):
            t0 = tt * TT
            v = PROJ.rearrange("(k p) t -> p k t", p=128)
            P.dma(q32.rearrange("p (k t) -> p k t", k=MC), v[:, SC:SC + MC, t0:t0 + TT], [], [qb], qb)
            for h in range(cfg.MEM_HEADS):
                for dc in range(2):
                    ch = 2 * h + dc
                    kb.act(sq[dc], q32[:, ch * TT:(ch + 1) * TT], AF.Square, [qb], [sqb[dc]])
                    kb.mm([(banks[7], ONES32, sq[dc], dc == 0, dc == 1)], [sqb[dc], cstb], [bankbuf[7]])
                kb.act(rstd, banks[7], AF.Sqrt, [bankbuf[7], cstb], [rb], scale=1.0 / 256, bias=epsb[:, 0:1])
                P.op("dve", lambda e: e.reciprocal(out=rstd, in_=rstd), [rb], [rb])
                for dc in range(2):
                    ch = 2 * h + dc
                    kb.stt(qn[:, ch * TT:(ch + 1) * TT], q32[:, ch * TT:(ch + 1) * TT],
                           qk32[:, 4 * l + dc:4 * l + dc + 1], rstd, ALU.mult, ALU.mult, [qb, rb, cstb], [qnb])
                for mc in range(MCH):
                    bank = mc % 2
                    items = [(banks[bank], kn[:, (2 * h + dc) * MT + mc * 128:(2 * h + dc) * MT + (mc + 1) * 128],
                              qn[:, (2 * h + dc) * TT:(2 * h + dc + 1) * TT], dc == 0, dc == 1) for dc in range(2)]
                    kb.mm(items, [knb, qnb], [bankbuf[bank]])
                    kb.act(pT[:, mc * TT:(mc + 1) * TT], banks[bank], AF.Exp, [bankbuf[bank]], [pTb], scale=1.0 / 16.0)
                items = [(banks[2], ones16, pT[:, mc * TT:(mc + 1) * TT], mc == 0, mc == MCH - 1) for mc in range(MCH)]
                kb.mm(items, [pTb, cstb], [bankbuf[2]])
                P.op("dve", lambda e: e.reciprocal(out=rden, in_=banks[2]), [bankbuf[2]], [rdb])
                for dc in range(2):
                    bank = 3 + dc
                    items = [(banks[bank], vtm[:, mc * MEM_W + h * 256 + dc * 128: mc * MEM_W + h * 256 + (dc + 1) * 128],
                              pT[:, mc * TT:(mc + 1) * TT], mc == 0, mc == MCH - 1) for mc in range(MCH)]
                    kb.mm(items, [vtb, pTb], [bankbuf[bank]])
                    k = cnt % 2
                    cnt += 1
                    kb.tt("dve", ev[k], banks[bank], rden, ALU.mult, [bankbuf[bank], rdb], [evb[k]])
                    store_tile(MIX, SSM_W + h * 256 + dc * 128, t0, TT, ev[k], evb[k])

    def phase_s5():
        new_phase()
        NQ = NPAIR * 128
        NP_ = NPAIR
        LRE = p16.alloc(NQ)
        LIM = p16.alloc(NQ)
        CRE = p16.alloc(NQ)
        CIMN = p16.alloc(NQ)
        Lb = Buf()
        par = p32.alloc(3 * NP_)
        pbuf = Buf()
        P.dma(par, s5par, [], [pbuf], pbuf)
        T2 = [p32.alloc(NP_) for _ in range(6)]
        t2b = Buf()
        scr2 = p32.alloc(NP_)
        scr2i = p32.alloc(NP_).bitcast(I32)
        NLEV = 10
        CLx = [p32.alloc(NP_) for _ in range(NLEV - 1)]
        SLx = [p32.alloc(NP_) for _ in range(NLEV - 1)]
        d32 = p32.alloc(SC)
        mark = p32.off

        def prep(lr_in, li_in, ls_in, T, bufs_in, ob):
            step, lr, mag, th, cs, sn = T
            kb.act(step, ls_in, AF.Exp, bufs_in, [ob])
            kb.ts("dve", lr, lr_in, -1e-4, ALU.min, bufs_in, [ob])
            kb.tt("dve", mag, lr, step, ALU.mult, [ob], [ob])
            kb.act(mag, mag, AF.Exp, [ob], [ob])
            kb.tt("dve", th, li_in, step, ALU.mult, bufs_in + [ob], [ob])
            return step, lr, mag, th, cs, sn

        def sincos(th, cs, sn, ob, scr, scri):
            for (dst, shift) in ((sn, 0.0), (cs, math.pi / 2)):
                kb.ts("dve", scr, th, 1.0 / (2 * math.pi), ALU.mult, [ob], [ob], s2=shift / (2 * math.pi) + 0.5, op1=ALU.add)
                kb.copy("dve", scri, scr, [ob], [ob])
                kb.copy("dve", scr, scri, [ob], [ob])
                kb.ts("dve", scr, scr, -2 * math.pi, ALU.mult, [ob], [ob], s2=shift, op1=ALU.add)
                kb.tt("dve", dst, th, scr, ALU.add, [ob], [ob])
                kb.ts("dve", scr, dst, math.pi, ALU.is_gt, [ob], [ob], s2=-2 * math.pi, op1=ALU.mult)
                kb.tt("dve", dst, dst, scr, ALU.add, [ob], [ob])
                kb.ts("dve", scr, dst, -math.pi, ALU.is_lt, [ob], [ob], s2=2 * math.pi, op1=ALU.mult)
                kb.tt("dve", dst, dst, scr, ALU.add, [ob], [ob])
                kb.act(dst, dst, AF.Sin, [ob], [ob])

        PB = min(8, NPAIR)
        BQ = PB * 128
        rep = p32.alloc(3 * BQ)
        rbuf_ = Buf()
        tmp = [p32.alloc(BQ) for _ in range(6)]
        tb = Buf()
        scr = p32.alloc(BQ)
        scri = p32.alloc(BQ).bitcast(I32)
        Bp = p32.alloc(2 * BQ)
        Bb = Buf()
        Cp = p32.alloc(2 * BQ)
        Cb = Buf()
        rep3 = s5rep.rearrange("p (w q) -> p w q", w=3)
        B3 = s5B.rearrange("p (w q) -> p w q", w=2)
        C3 = s5C.rearrange("p (w q) -> p w q", w=2)
        for blk in range(NPAIR // PB):
            q0 = blk * BQ
            P.dma(rep.rearrange("p (w q) -> p w q", w=3), rep3[:, :, q0:q0 + BQ], [], [rbuf_], rbuf_)
            P.dma(Bp.rearrange("p (w q) -> p w q", w=2), B3[:, :, q0:q0 + BQ], [], [Bb], Bb)
            P.dma(Cp.rearrange("p (w q) -> p w q", w=2), C3[:, :, q0:q0 + BQ], [], [Cb], Cb)
            li = rep[:, BQ:2 * BQ]
            step, lr, mag, th, cs, sn = prep(rep[:, 0:BQ], li, rep[:, 2 * BQ:3 * BQ], tmp, [rbuf_], tb)
            sincos(th, cs, sn, tb, scr, scri)
            kb.tt("dve", cs, cs, mag, ALU.mult, [tb], [tb])
            kb.tt("dve", sn, sn, mag, ALU.mult, [tb], [tb])
            kb.tt("dve", step, lr, lr, ALU.mult, [tb], [tb])
            kb.tt("dve", scr, li, li, ALU.mult, [rbuf_, tb], [tb])
            kb.tt("dve", step, step, scr, ALU.add, [tb], [tb])
            P.op("dve", lambda e, a=step: e.reciprocal(out=a, in_=a), [tb], [tb])
            kb.ts("dve", mag, cs, -1.0, ALU.add, [tb], [tb])
            kb.tt("dve", th, mag, lr, ALU.mult, [tb], [tb])
            kb.tt("dve", scr, sn, li, ALU.mult, [rbuf_, tb], [tb])
            kb.tt("dve", th, th, scr, ALU.add, [tb], [tb])
            kb.tt("dve", th, th, step, ALU.mult, [tb], [tb])
            kb.tt("dve", cs, sn, lr, ALU.mult, [tb], [tb])
            kb.tt("dve", scr, mag, li, ALU.mult, [rbuf_, tb], [tb])
            kb.tt("dve", cs, cs, scr, ALU.subtract, [tb], [tb])
            kb.tt("dve", cs, cs, step, ALU.mult, [tb], [tb])
            f_re, f_im = th, cs
            br, bi = Bp[:, 0:BQ], Bp[:, BQ:2 * BQ]
            kb.tt("dve", mag, f_re, br, ALU.mult, [tb, Bb], [tb])
            kb.tt("dve", scr, f_im, bi, ALU.mult, [tb, Bb], [tb])
            kb.tt("dve", LRE[:, q0:q0 + BQ], mag, scr, ALU.subtract, [tb], [Lb])
            kb.tt("dve", mag, f_re, bi, ALU.mult, [tb, Bb], [tb])
            kb.tt("dve", scr, f_im, br, ALU.mult, [tb, Bb], [tb])
            kb.tt("dve", LIM[:, q0:q0 + BQ], mag, scr, ALU.add, [tb], [Lb])
            kb.copy("dve", CRE[:, q0:q0 + BQ], Cp[:, 0:BQ], [Cb], [Lb])
            kb.ts("dve", CIMN[:, q0:q0 + BQ], Cp[:, BQ:2 * BQ], -1.0, ALU.mult, [Cb], [Lb])
        step2, lr2, rho, th2, c0, s0 = prep(par[:, 0:NP_], par[:, NP_:2 * NP_], par[:, 2 * NP_:3 * NP_], T2, [pbuf], t2b)
        sincos(th2, c0, s0, t2b, scr2, scr2i)
        CL = [c0] + CLx
        SL = [s0] + SLx
        for m in range(1, NLEV):
            kb.tt("dve", CL[m], CL[m - 1], CL[m - 1], ALU.mult, [t2b], [t2b])
            kb.tt("dve", scr2, SL[m - 1], SL[m - 1], ALU.mult, [t2b], [t2b])
            kb.tt("dve", CL[m], CL[m], scr2, ALU.subtract, [t2b], [t2b])
            kb.tt("dve", SL[m], CL[m - 1], SL[m - 1], ALU.mult, [t2b], [t2b])
            kb.ts("dve", SL[m], SL[m], 2.0, ALU.mult, [t2b], [t2b])
        P.dma(d32, s5D, [], [t2b], t2b)
        P.barrier()
        p32.off = mark

        NPC = 4
        Ec = [p32.alloc(TT) for _ in range(NPC)]
        Es = [p32.alloc(TT) for _ in range(NPC)]
        Eb = [Buf() for _ in range(NPC)]
        rhoT = [p32.alloc(TT) for _ in range(NPC)]
        init = [p32.alloc(2) for _ in range(NPC)]
        inb = [Buf() for _ in range(NPC)]
        u32 = [p32.alloc(TT) for _ in range(2)]
        u32b = [Buf() for _ in range(2)]
        u16 = [p16.alloc(TT) for _ in range(2)]
        u16b = [Buf() for _ in range(2)]
        NW = NPC
        NTL = 8
        wk = [[p32.alloc(TT) for _ in range(NTL)] for _ in range(NW)]
        wb2 = [[Buf() for _ in range(NTL)] for _ in range(NW)]
        x16 = [[p16.alloc(TT) for _ in range(2)] for _ in range(NW)]
        x16b = [[Buf() for _ in range(2)] for _ in range(NW)]
        tiny = [p32.alloc(4) for _ in range(NW)]
        tinyb = [Buf() for _ in range(NW)]
        gout = [p32.alloc(TT) for _ in range(2)]
        goutb = [Buf() for _ in range(2)]
        uc = 0
        gc = 0
        bc = 0
        for c in range(SC):
            for pi in range(NPC):
                kb.ts("dve", Ec[pi][:, 0:1], ONES32[:, 0:1], 1.0, ALU.mult, [cstb], [Eb[pi]])
                kb.ts("dve", Es[pi][:, 0:1], ONES32[:, 0:1], 0.0, ALU.mult, [cstb], [Eb[pi]])
            for m in range(NLEV - 1):
                s = 1 << m
                for pi in range(NPC):
                    q = c * NPC + pi
                    sm = SL[m][:, q:q + 1]
                    eng = "dve"
                    kb.ts(eng, wk[pi][2][:, 0:s], Es[pi][:, 0:s], sm, ALU.mult, [Eb[pi], t2b], [wb2[pi][2]])
                    kb.ts(eng, wk[pi][3][:, 0:s], Ec[pi][:, 0:s], sm, ALU.mult, [Eb[pi], t2b], [wb2[pi][3]])
                for pi in range(NPC):
                    q = c * NPC + pi
                    cm = CL[m][:, q:q + 1]
                    kb.stt(Ec[pi][:, s:2 * s], Ec[pi][:, 0:s], cm, wk[pi][2][:, 0:s], ALU.mult, ALU.subtract,
                           [Eb[pi], wb2[pi][2], t2b], [Eb[pi]])
                    kb.stt(Es[pi][:, s:2 * s], Es[pi][:, 0:s], cm, wk[pi][3][:, 0:s], ALU.mult, ALU.add,
                           [Eb[pi], wb2[pi][3], t2b], [Eb[pi]])
            for pi in range(NPC):
                q = c * NPC + pi
                kb.act(rhoT[pi], ONES_T, AF.Copy, [cstb, t2b], [Eb[pi]], scale=rho[:, q:q + 1])
                P.op("dve", lambda e, a=init[pi]: e.memset(a, 0.0), [], [inb[pi]])
            for tt in range(NT):
                t0 = tt * TT
                k = uc % 2
                uc += 1
                P.dma(u32[k], PROJ[c * 128:(c + 1) * 128, t0:t0 + TT], [], [u32b[k]], u32b[k])
                kb.copy("act", u16[k], u32[k], [u32b[k]], [u16b[k]])
                ybank = 6 + (gc % 2)
                R = range(NPC)
                for pi in R:
                    q = c * NPC + pi
                    ba = 2 * (bc % 3)
                    bc += 1
                    kb.mm([(banks[ba], LRE[:, q * 128:(q + 1) * 128], u16[k], True, True)], [Lb, u16b[k]], [bankbuf[ba]])
                    kb.mm([(banks[ba + 1], LIM[:, q * 128:(q + 1) * 128], u16[k], True, True)], [Lb, u16b[k]], [bankbuf[ba + 1]])
                    kb.copy("act", wk[pi][0], banks[ba], [bankbuf[ba]], [wb2[pi][0]])
                    kb.copy("act", wk[pi][1], banks[ba + 1], [bankbuf[ba + 1]], [wb2[pi][1]])
                for pi in R:
                    br_, bi_, t1, t2, t3, t4, wr, wim = wk[pi]
                    Bbr, Bbi, B1, B2, B3, B4, Bwr, Bwi = wb2[pi]
                    kb.tt("dve", t1, Ec[pi], br_, ALU.mult, [Eb[pi], Bbr], [B1])
                    kb.tt("dve", t2, Es[pi], bi_, ALU.mult, [Eb[pi], Bbi], [B2])
                    kb.tt("dve", t3, Ec[pi], bi_, ALU.mult, [Eb[pi], Bbi], [B3])
                    kb.tt("dve", t4, Es[pi], br_, ALU.mult, [Eb[pi], Bbr], [B4])
                for pi in R:
                    br_, bi_, t1, t2, t3, t4, wr, wim = wk[pi]
                    Bbr, Bbi, B1, B2, B3, B4, Bwr, Bwi = wb2[pi]
                    kb.tt("dve", t1, t1, t2, ALU.add, [B1, B2], [B1])
                    kb.tt("dve", t3, t3, t4, ALU.subtract, [B3, B4], [B3])
                for pi in R:
                    br_, bi_, t1, t2, t3, t4, wr, wim = wk[pi]
                    Bbr, Bbi, B1, B2, B3, B4, Bwr, Bwi = wb2[pi]
                    P.op("dve", lambda e, o=wr, a=rhoT[pi], b=t1, i0=init[pi][:, 0:1]:
                         e.tensor_tensor_scan(out=o, data0=a, data1=b, initial=i0, op0=ALU.mult, op1=ALU.add),
                         [Eb[pi], B1, inb[pi]], [Bwr])
                    P.op("dve", lambda e, o=wim, a=rhoT[pi], b=t3, i0=init[pi][:, 1:2]:
                         e.tensor_tensor_scan(out=o, data0=a, data1=b, initial=i0, op0=ALU.mult, op1=ALU.add),
                         [Eb[pi], B3, inb[pi]], [Bwi])
                for pi in R:
                    q = c * NPC + pi
                    br_, bi_, t1, t2, t3, t4, wr, wim = wk[pi]
                    Bbr, Bbi, B1, B2, B3, B4, Bwr, Bwi = wb2[pi]
                    c9 = CL[NLEV - 1][:, q:q + 1]
                    s9 = SL[NLEV - 1][:, q:q + 1]
                    tn = tiny[pi]
                    kb.ts("dve", tn[:, 0:1], wim[:, TT - 1:TT], s9, ALU.mult, [Bwi, t2b], [tinyb[pi]])
                    kb.ts("dve", tn[:, 1:2], wr[:, TT - 1:TT], s9, ALU.mult, [Bwr, t2b], [tinyb[pi]])
                    kb.stt(init[pi][:, 0:1], wr[:, TT - 1:TT], c9, tn[:, 0:1], ALU.mult, ALU.subtract, [Bwr, tinyb[pi], t2b], [inb[pi]])
                    kb.stt(init[pi][:, 1:2], wim[:, TT - 1:TT], c9, tn[:, 1:2], ALU.mult, ALU.add, [Bwi, tinyb[pi], t2b], [inb[pi]])
                for pi in R:
                    br_, bi_, t1, t2, t3, t4, wr, wim = wk[pi]
                    Bbr, Bbi, B1, B2, B3, B4, Bwr, Bwi = wb2[pi]
                    kb.tt("dve", t1, Ec[pi], wr, ALU.mult, [Eb[pi], Bwr], [B1])
                    kb.tt("dve", t2, Es[pi], wim, ALU.mult, [Eb[pi], Bwi], [B2])
                    kb.tt("dve", t4, Ec[pi], wim, ALU.mult, [Eb[pi], Bwi], [B4])
                    kb.tt("dve", t3, Es[pi], wr, ALU.mult, [Eb[pi], Bwr], [B3])
                for pi in R:
                    br_, bi_, t1, t2, t3, t4, wr, wim = wk[pi]
                    Bbr, Bbi, B1, B2, B3, B4, Bwr, Bwi = wb2[pi]
                    kb.tt("dve", x16[pi][0], t1, t2, ALU.subtract, [B1, B2], [x16b[pi][0]])
                    kb.tt("dve", x16[pi][1], t3, t4, ALU.add, [B3, B4], [x16b[pi][1]])
                for pi in R:
                    q = c * NPC + pi
                    kb.mm([(banks[ybank], CRE[:, q * 128:(q + 1) * 128], x16[pi][0], pi == 0, False),
                           (banks[ybank], CIMN[:, q * 128:(q + 1) * 128], x16[pi][1], False, pi == NPC - 1)],
                          [Lb, x16b[pi][0], x16b[pi][1]], [bankbuf[ybank]])
                g = gc % 2
                gc += 1
                kb.stt(gout[g], u32[k], d32[:, c:c + 1], banks[ybank], ALU.mult, ALU.add, [u32b[k], t2b, bankbuf[ybank]], [goutb[g]])
                kb.act(gout[g], gout[g], AF.Gelu, [goutb[g]], [goutb[g]])
                store_tile(GS5, c * 128, t0, TT, gout[g], goutb[g])


    ONES_T = None

    def phase_glu():
        new_phase()
        w = W["w_glu"]
        a32 = p32.alloc(SC * TT)
        abuf = Buf()
        xn = p16.alloc(SC * TT)
        xnbuf = Buf()
        ws = WStream(w["kc"] * w["nblk"])
        ev = [p32.alloc(TT) for _ in range(4)]
        evb = [Buf() for _ in range(4)]
        cnt = [0]
        for tt in range(NT):
            t0 = tt * TT
            load_act_tile(GS5, SC, t0, TT, a32, abuf)
            for kc in range(SC):
                kb.cast_any(xn[:, kc * TT:(kc + 1) * TT], a32[:, kc * TT:(kc + 1) * TT], [abuf], [xnbuf], engines=("act", "dve"))

            def evac(ti, c, bank, t0=t0):
                k = cnt[0] % 4
                cnt[0] += 1
                ch = ti * 2 + c
                kb.act(ev[k], banks[bank], AF.Sigmoid, [bankbuf[bank]], [evb[k]])
                kb.tt("dve", ev[k], ev[k], a32[:, ch * TT:(ch + 1) * TT], ALU.mult, [evb[k], abuf], [evb[k]])
                store_tile(MIX, ch * 128, t0, TT, ev[k], evb[k])
            gemm_pass(w, xn, xnbuf, TT, ws, evac)

    def phase_resid(w, src, src_dt, Kc, hold, hnew):
        new_phase()
        if src_dt == F32:
            a32 = p32.alloc(Kc * TT)
            abuf = Buf()
        xn = p16.alloc(Kc * TT)
        xnbuf = Buf()
        ws = WStream(w["kc"] * w["nblk"])
        hin = [p32.alloc(TT) for _ in range(4)]
        hinb = [Buf() for _ in range(4)]
        cnt = [0]
        nchunks = w["ntiles"] * (w["nblk"] // 128)
        for tt in range(NT):
            t0 = tt * TT
            if src_dt == F32:
                load_act_tile(src, Kc, t0, TT, a32, abuf)
                for kc in range(Kc):
                    kb.cast_any(xn[:, kc * TT:(kc + 1) * TT], a32[:, kc * TT:(kc + 1) * TT], [abuf], [xnbuf])
            else:
                v = src.rearrange("(k p) t -> p k t", p=128)
                x3 = xn.rearrange("p (k t) -> p k t", k=Kc)
                for k0 in range(0, Kc, 8):
                    k1 = min(Kc, k0 + 8)
                    P.dma(x3[:, k0:k1, :], v[:, k0:k1, t0:t0 + TT], [], [xnbuf], xnbuf)
            pend = []

            def evac(ti, c, bank, t0=t0):
                k = cnt[0] % 4
                cnt[0] += 1
                ch = ti * (w["nblk"] // 128) + c
                P.dma(hin[k], hold[ch * 128:(ch + 1) * 128, t0:t0 + TT], [], [hinb[k]], hinb[k])
                kb.tt("dve", hin[k], banks[bank], hin[k], ALU.add, [bankbuf[bank], hinb[k]], [hinb[k]])
                store_tile(hnew, ch * 128, t0, TT, hin[k], hinb[k])
            gemm_pass(w, xn, xnbuf, TT, ws, evac)

    def phase_ffn_up(l, hsrc):
        new_phase()
        w = W["w_up%d" % l]
        a32 = p32.alloc(KC * TT)
        abuf = Buf()
        xn = p16.alloc(KC * TT)
        xnbuf = Buf()
        sq = [p32.alloc(TT) for _ in range(2)]
        sqb = [Buf() for _ in range(2)]
        rstd = p32.alloc(TT)
        rb = Buf()
        cp = p32.alloc(2 * FC * 4)
        cpb = Buf()
        P.dma(cp, convp[:, l * 2 * FC * 4:(l + 1) * 2 * FC * 4], [], [cpb], cpb)
        tails = p32.alloc(2 * FC * 2)
        tlb = Buf()
        P.op("dve", lambda e: e.memset(tails, 0.0), [], [tlb])
        ws = WStream(w["kc"] * w["nblk"])
        NB_ = 2
        ua = [[p32.alloc(TT + 2) for _ in range(2)] for _ in range(NB_)]
        uab = [Buf() for _ in range(NB_)]
        cv = [[p32.alloc(TT) for _ in range(2)] for _ in range(NB_)]
        cvb = [Buf() for _ in range(NB_)]
        hid = [p16.alloc(TT) for _ in range(NB_)]
        hidb = [Buf() for _ in range(NB_)]
        cnt = [0]
        for tt in range(NT):
            t0 = tt * TT
            load_act_tile(hsrc, KC, t0, TT, a32, abuf)
            rms_tile(a32, abuf, KC, TT, (4 + l) * KC, xn, xnbuf, sq, sqb, rstd, rb, 7, None)

            def evac(ti, c, bank, t0=t0):
                k = (cnt[0] // 2) % NB_
                cnt[0] += 1
                u = ua[k][c]
                pc = cp[:, (c * FC + ti) * 4:(c * FC + ti) * 4 + 4]
                tl = tails[:, (c * FC + ti) * 2:(c * FC + ti) * 2 + 2]
                kb.copy("act", u[:, 0:2], tl, [tlb], [uab[k]])
                kb.copy("act" if c == 0 else "dve", u[:, 2:TT + 2], banks[bank], [bankbuf[bank]], [uab[k]])
                kb.copy("act", tl, u[:, TT:TT + 2], [uab[k]], [tlb])
                o = cv[k][c]
                kb.act(o, u[:, 2:TT + 2], AF.Identity, [uab[k], cpb], [cvb[k]], scale=pc[:, 2:3], bias=pc[:, 3:4])
                kb.stt(o, u[:, 1:TT + 1], pc[:, 1:2], o, ALU.mult, ALU.add, [uab[k], cpb, cvb[k]], [cvb[k]])
                kb.stt(o, u[:, 0:TT], pc[:, 0:1], o, ALU.mult, ALU.add, [uab[k], cpb, cvb[k]], [cvb[k]])
                if c == 1:
                    kb.act(cv[k][0], cv[k][0], AF.Silu, [cvb[k]], [cvb[k]])
                    kb.tt("dve", hid[k], cv[k][0], cv[k][1], ALU.mult, [cvb[k]], [hidb[k]])
                    store_tile(HID, ti * 128, t0, TT, hid[k], hidb[k])
            gemm_pass(w, xn, xnbuf, TT, ws, evac)

    def phase_kv(hsrc):
        new_phase()
        w = W["w_kv"]
        a32 = p32.alloc(KC * TT)
        abuf = Buf()
        xn = p16.alloc(KC * TT)
        xnbuf = Buf()
        sq = [p32.alloc(TT) for _ in range(2)]
        sqb = [Buf() for _ in range(2)]
        rstd = p32.alloc(TT)
        rb = Buf()
        ws = WStream(w["kc"] * w["nblk"])
        ev = [p16.alloc(TT) for _ in range(4)]
        evb = [Buf() for _ in range(4)]
        cnt = [0]
        nkt = SSM_W // 256
        for tt in range(NT):
            t0 = tt * TT
            load_act_tile(hsrc, KC, t0, TT, a32, abuf)
            rms_tile(a32, abuf, KC, TT, 6 * KC, xn, xnbuf, sq, sqb, rstd, rb, 7, None)

            def evac(ti, c, bank, t0=t0):
                k = cnt[0] % 4
                cnt[0] += 1
                kb.copy("act" if k % 2 else "dve", ev[k], banks[bank], [bankbuf[bank]], [evb[k]])
                store_tile(KT, (ti * 2 + c) * 128, t0, TT, ev[k], evb[k])
            gemm_pass(w, xn, xnbuf, TT, ws, evac, tiles=list(range(nkt)))
            vt = list(range(nkt, 2 * nkt))
            for j in range(min(2, len(vt))):
                ws.prefetch(w, vt[j])
            for idx, ti in enumerate(vt):
                if idx + 2 < len(vt):
                    ws.prefetch(w, vt[idx + 2])
                wt, wb = ws.pop()
                for tc in range(TT // 128):
                    bank = gemm_pass.bank_rr % 4
                    gemm_pass.bank_rr += 1
                    items = [(banks[bank][:, 0:256], xn[:, kc * TT + tc * 128: kc * TT + (tc + 1) * 128],
                              wt[:, kc * 256:(kc + 1) * 256], kc == 0, kc == KC - 1) for kc in range(KC)]
                    kb.mm(items, [wb, xnbuf], [bankbuf[bank]])
                    k = cnt[0] % 4
                    cnt[0] += 1
                    kb.copy("act" if k % 2 else "dve", ev[k][:, 0:256], banks[bank][:, 0:256], [bankbuf[bank]], [evb[k]])
                    P.dma(VTM[t0 + tc * 128: t0 + (tc + 1) * 128, (ti - nkt) * 256:(ti - nkt + 1) * 256], ev[k][:, 0:256],
                          [evb[k]], [], evb[k])

    def phase_sb():
        new_phase()
        NKC = SEQ // 128
        scale = 1.0 / math.sqrt(128.0)
        kT = [p16.alloc(SEQ) for _ in range(2)]
        kTb = [Buf() for _ in range(2)]
        vt = [p16.alloc(NKC * 128) for _ in range(2)]
        vtb = [Buf() for _ in range(2)]
        q32 = [p32.alloc(TT) for _ in range(2)]
        q32b = [Buf() for _ in range(2)]
        q16 = [p16.alloc(TT) for _ in range(3)]
        q16b = [Buf() for _ in range(3)]
        NW = 4
        E = [p32.alloc(TT) for _ in range(NW)]
        Eb_ = [Buf() for _ in range(NW)]
        SP_ = [p32.alloc(TT) for _ in range(NW)]
        SPb = [Buf() for _ in range(NW)]
        X = [p32.alloc(TT) for _ in range(NW)]
        Xb = [Buf() for _ in range(NW)]
        sacc = [p32.alloc(TT) for _ in range(3)]
        saccb = [Buf() for _ in range(3)]
        W16 = [p16.alloc(TT) for _ in range(NW)]
        W16b = [Buf() for _ in range(NW)]
        ev = [p32.alloc(TT) for _ in range(2)]
        evb = [Buf() for _ in range(2)]
        tasks = []
        ti_ = 0
        for h in range(H):
            for tt in range(NT):
                nk = 4 * tt + 4
                for kc in range(nk - 1, -1, -1):
                    tasks.append(dict(h=h, tt=tt, kc=kc, first=(kc == nk - 1), last=(kc == 0), dj=kc - 4 * tt,
                                      tile=ti_, w=len(tasks) % NW, zb=len(tasks) % 2, cb=2 + len(tasks) % 3))
                ti_ += 1
        sidx = [0]

        def S1(t):
            h, tt, kc, w_ = t["h"], t["tt"], t["kc"], t["w"]
            hk = h % 2
            if t["first"]:
                if tt == 0:
                    P.dma(kT[hk], KT[h * 128:(h + 1) * 128, :], [], [kTb[hk]], kTb[hk])
                    v3 = vt[hk].rearrange("p (k d) -> p k d", k=NKC)
                    vs = VTM.rearrange("(k p) d -> p k d", p=128)
                    for k0 in range(0, NKC, 8):
                        k1 = min(NKC, k0 + 8)
                        P.dma(v3[:, k0:k1, :], vs[:, k0:k1, h * 128:(h + 1) * 128], [], [vtb[hk]], vtb[hk])
                k = t["tile"] % 2
                k3 = t["tile"] % 3
                P.dma(q32[k], PROJ[h * 128:(h + 1) * 128, tt * TT:(tt + 1) * TT], [], [q32b[k]], q32b[k])
                kb.copy("dve", q16[k3], q32[k], [q32b[k]], [q16b[k3]])
            k3 = t["tile"] % 3
            zb = t["zb"]
            kb.mm([(banks[zb], kT[hk][:, kc * 128:(kc + 1) * 128], q16[k3], True, True)], [kTb[hk], q16b[k3]], [bankbuf[zb]])
            kb.act(E[w_], banks[zb], AF.Exp, [bankbuf[zb]], [Eb_[w_]], scale=scale)
            kb.act(SP_[w_], E[w_], AF.Ln, [Eb_[w_], cstb], [SPb[w_]], bias=onesb[:, 0:1])
            if t["dj"] >= 0:
                kb.tt("dve", SP_[w_], SP_[w_], MASK[t["dj"]], ALU.mult, [SPb[w_], cstb], [SPb[w_]])

        def S2(t):
            w_ = t["w"]
            cbk = t["cb"]
            so, sob = sacc[sidx[0] % 3], saccb[sidx[0] % 3]
            sn_, snb = sacc[(sidx[0] + 1) % 3], saccb[(sidx[0] + 1) % 3]
            if t["first"]:
                kb.mm([(banks[cbk], TRI32, SP_[w_], True, True)], [SPb[w_], cstb], [bankbuf[cbk]])
            else:
                kb.mm([(banks[cbk], TRI32, SP_[w_], True, False), (banks[cbk], ONES32, so, False, True)],
                      [SPb[w_], cstb, sob], [bankbuf[cbk]])
            if not t["last"]:
                if t["first"]:
                    kb.copy("dve", sn_, SP_[w_], [SPb[w_]], [snb])
                else:
                    kb.tt("dve", sn_, so, SP_[w_], ALU.add, [SPb[w_], sob], [snb])
                sidx[0] += 1

        def S3a(t):
            w_ = t["w"]
            cbk = t["cb"]
            kb.act(X[w_], banks[cbk], AF.Exp, [bankbuf[cbk]], [Xb[w_]], scale=-1.0)
            if t["dj"] >= 0:
                kb.tt("dve", X[w_], X[w_], MASK[t["dj"]], ALU.mult, [Xb[w_], cstb], [Xb[w_]])
            kb.tt("dve", W16[w_], E[w_], X[w_], ALU.mult, [Eb_[w_], Xb[w_]], [W16b[w_]])

        def S3b(t):
            h, tt, kc, w_ = t["h"], t["tt"], t["kc"], t["w"]
            hk = h % 2
            obank = 6 + (t["tile"] % 2)
            kb.mm([(banks[obank], vt[hk][:, kc * 128:(kc + 1) * 128], W16[w_], t["first"], t["last"])],
                  [vtb[hk], W16b[w_]], [bankbuf[obank]])
            if t["last"]:
                e_ = t["tile"] % 2
                kb.copy("act", ev[e_], banks[obank], [bankbuf[obank]], [evb[e_]])
                store_tile(MIX, h * 128, tt * TT, TT, ev[e_], evb[e_])

        n = len(tasks)
        for i in range(n + 3):
            if 0 <= i - 3 < n:
                S3a(tasks[i - 3])
            if i < n:
                S1(tasks[i])
            if 0 <= i - 1 < n:
                S2(tasks[i - 1])
            if 0 <= i - 3 < n:
                S3b(tasks[i - 3])

    def dump(nm):
        new_phase()
        src = kb.dr[nm]
        b = Buf()
        rows = src.shape[0]
        for r0 in range(0, rows, 128):
            P.dma(dbg[nm][r0:r0 + 128, :], src[r0:r0 + 128, :], [], [b], b)

    ONES_T = p32.alloc(TT)
    P.op("dve", lambda e: e.memset(ONES_T, 1.0), [], [cstb])
    onesb = p32.alloc(1)
    P.op("dve", lambda e: e.memset(onesb, 1.0), [], [cstb])
    base32 = p32.off

    stages = cfg.stages if hasattr(cfg, "stages") else None

    def want(s):
        return stages is None or s in stages

    if want("cast"):
        cast_weights_dma()
    with nc.allow_low_precision("bf16 matmuls with fp32 accumulation, as the reference tolerance assumes"):
        if want("in0"):
            phase_inproj(0, xT)
        if want("mem0"):
            phase_memattn(0)
        if want("s5"):
            phase_s5()
        if want("glu"):
            phase_glu()
        if want("out0"):
            phase_resid(W["w_out0"], MIX, F32, XC, xT, HA)
        if want("up0"):
            phase_ffn_up(0, HA)
        if want("dn0"):
            phase_resid(W["w_dn0"], HID, BF16, FC, HA, HB)
        if want("kv"):
            phase_kv(HB)
        if want("in1"):
            phase_inproj(1, HB)
        if want("mem1"):
            phase_memattn(1)
        if want("sb"):
            phase_sb()
        if want("out1"):
            phase_resid(W["w_out1"], MIX, F32, XC, HB, HA)
        if want("up1"):
            phase_ffn_up(1, HA)
        if want("dn1"):
            phase_resid(W["w_dn1"], HID, BF16, FC, HA, yT)
        for nm in debug_outs:
            dump(nm)
        P.barrier()
        P.emit()
    es.close()
    return nc


def tile_w(Wm, nblk, col_groups=None):
    Kd, N = Wm.shape
    kc = Kd // 128
    if col_groups is None:
        nt = N // nblk
        x = Wm.reshape(kc, 128, nt, nblk).transpose(2, 1, 0, 3)
    else:
        x = Wm[:, col_groups.reshape(-1)].reshape(kc, 128, col_groups.shape[0], nblk).transpose(2, 1, 0, 3)
        nt = col_groups.shape[0]
    return np.ascontiguousarray(x).reshape(nt * 128, kc * nblk)


def chunkvec(v):
    return np.ascontiguousarray(v.reshape(-1, 128).T)


def host_layout(cfg, inp):
    f = np.float32
    D, KC, FC, NPAIR, SC = cfg.D, cfg.KC, cfg.FC, cfg.NPAIR, cfg.SSM_W // 128
    shared = {}
    gains = [chunkvec(inp["norm_mix_g"][0]), chunkvec(inp["norm_mix_g"][1]), chunkvec(inp["mem_norm_g"][0]),
             chunkvec(inp["mem_norm_g"][1]), chunkvec(inp["norm_ffn_g"][0]), chunkvec(inp["norm_ffn_g"][1]),
             chunkvec(inp["kv_norm_g"])]
    shared["gains"] = np.ascontiguousarray(np.concatenate(gains, axis=1), dtype=f)
    qk = []
    for l in range(2):
        qk += [chunkvec(inp["mem_q_norm_g"][l]), chunkvec(inp["mem_k_norm_g"][l])]
    shared["qkg"] = np.ascontiguousarray(np.concatenate(qk, axis=1), dtype=f)
    cw = inp["ffn_conv_w"]
    cb = inp["ffn_conv_b"]
    cp = np.zeros((128, 2, 2, FC, 4), f)
    for l in range(2):
        for ab in range(2):
            sl = slice(ab * cfg.D_FF, (ab + 1) * cfg.D_FF)
            for i in range(3):
                cp[:, l, ab, :, i] = chunkvec(cw[l, i, sl])
            cp[:, l, ab, :, 3] = chunkvec(cb[l, sl])
    shared["convp"] = cp.reshape(128, -1)
    lam_re, lam_im, ls = inp["s5_lam_re"][0], inp["s5_lam_im"][0], inp["s5_log_step"][0]
    G = cfg.G

    def pairlay(a):
        return np.ascontiguousarray(a.reshape(NPAIR, 2, 64).transpose(1, 2, 0).reshape(128, NPAIR))
    lsb = np.repeat(ls[:, None], 64, axis=1)
    par = np.stack([pairlay(lam_re), pairlay(lam_im), pairlay(lsb)], axis=1)
    shared["s5par"] = np.ascontiguousarray(par.reshape(128, -1), dtype=f)
    rep = np.stack([a.reshape(NPAIR * 128) for a in (lam_re.reshape(NPAIR, 128), lam_im.reshape(NPAIR, 128),
                                                       lsb.reshape(NPAIR, 128))], axis=0)
    shared["s5rep"] = np.ascontiguousarray(np.broadcast_to(rep.reshape(1, -1), (128, 3 * NPAIR * 128)), dtype=f)
    Bp = np.zeros((128, 2, NPAIR, 2, 64), f)
    Cp = np.zeros((128, 2, NPAIR, 128), f)
    for which, (bsrc, csrc) in enumerate(((inp["s5_b_re"][0], inp["s5_c_re"][0]), (inp["s5_b_im"][0], inp["s5_c_im"][0]))):
        for q in range(NPAIR):
            for e in range(2):
                g = 2 * q + e
                r0 = (q % 4) * 32 + e * 16
                Bp[r0:r0 + 16, which, q, e, :] = bsrc[g].T
                Cp[e * 64:(e + 1) * 64, which, q, r0:r0 + 16] = csrc[g].T
    shared["s5B"] = Bp.reshape(128, -1)
    shared["s5C"] = Cp.reshape(128, -1)
    shared["s5D"] = chunkvec(inp["s5_d"][0].reshape(-1)).astype(f)
    tri = (np.arange(128)[:, None] >= np.arange(128)[None, :]).astype(f)
    ones = np.ones((128, 128), f)
    masks = []
    for dj in range(4):
        s = np.arange(128)[:, None] + 128 * dj
        t = np.arange(TT)[None, :]
        masks.append((s < t).astype(f))
    shared["consts"] = np.concatenate([tri, ones] + masks, axis=1)
    for l in range(2):
        shared["w_in%d" % l] = tile_w(inp["w_in"][l], 256)
        shared["w_out%d" % l] = tile_w(inp["w_out"][l], 256)
        shared["w_mkv%d" % l] = tile_w(inp["w_mem_kv"][l], 256)
        cg = np.stack([np.concatenate([np.arange(j * 128, (j + 1) * 128), cfg.D_FF + np.arange(j * 128, (j + 1) * 128)])
                       for j in range(FC)])
        shared["w_up%d" % l] = tile_w(inp["w_ffn_up"][l], 256, cg)
        shared["w_dn%d" % l] = tile_w(inp["w_ffn_down"][l], 128)
    shared["w_glu"] = tile_w(inp["s5_w_glu"][0], 256)
    shared["w_kv"] = tile_w(inp["w_kv_shared"], 256)
    in_maps = []
    for b in range(cfg.B):
        m = dict(shared)
        m["xT"] = np.ascontiguousarray(inp["x"][b].T)
        m["memT"] = np.ascontiguousarray(inp["mem"][b].T)
        in_maps.append(m)
    return in_maps


_CACHE = {}


def run(cfg, inputs, debug_outs=(), trace=False):
    inp = {k: np.asarray(v) for k, v in inputs.items()}
    in_maps = host_layout(cfg, inp)
    key = (id(cfg), tuple(debug_outs))
    nc = build(cfg, debug_outs)
    res = run_bass_kernel_spmd(nc, in_maps, core_ids=list(range(cfg.B)))
    return res


def kernel(**inputs):
    cfg = FULL
    res = run(cfg, inputs)
    out = np.stack([np.ascontiguousarray(res.results[b]["yT"].T) for b in range(cfg.B)], axis=0)
    return out.astype(np.float32)
```

```python
import math
import numpy as np
import ml_dtypes
import concourse.bass as bass
import concourse.mybir as mybir
from concourse.bass_utils import run_bass_kernel_spmd

F32 = mybir.dt.float32
BF16 = mybir.dt.bfloat16
I32 = mybir.dt.int32
AF = mybir.ActivationFunctionType
ALU = mybir.AluOpType

SEM_LIMIT = 16000
TT = 512


class Cfg:
    def __init__(self, D=4096, SEQ=4096, B=2, SSM_W=2048, MEM_HEADS=4, MEM_TOKENS=256, D_FF=11008):
        self.D = D
        self.SEQ = SEQ
        self.B = B
        self.SSM_W = SSM_W
        self.G = SSM_W // 16
        self.NPAIR = self.G // 2
        self.SB_HEADS = SSM_W // 128
        self.MEM_HEADS = MEM_HEADS
        self.MEM_W = MEM_HEADS * 256
        self.MIX_W = SSM_W + self.MEM_W
        self.MEM_TOKENS = MEM_TOKENS
        self.D_FF = D_FF
        self.KC = D // 128
        self.NT = SEQ // TT
        self.FC = D_FF // 128


FULL = Cfg()


class Sem:
    def __init__(self, nc, name):
        self.h = nc.alloc_semaphore(name)
        self.v = 0


class Buf:
    __slots__ = ("name", "w", "r", "dsem")

    def __init__(self, name=""):
        self.name = name
        self.w = []
        self.r = {}
        self.dsem = None


class Stream:
    def __init__(self, P, name, eng):
        self.P = P
        self.name = name
        self.eng = eng
        self.sems = []
        self.sem = None
        self.waited = {}
        self.insts = []

    def cur_sem(self):
        if self.sem is None or self.sem.v >= SEM_LIMIT:
            self.sem = Sem(self.P.nc, "e%s%d" % (self.name, len(self.sems)))
            self.sems.append(self.sem)
        return self.sem


class Prog:
    def __init__(self, nc):
        self.nc = nc
        self.st = {
            "pe": Stream(self, "pe", nc.tensor),
            "act": Stream(self, "act", nc.scalar),
            "dve": Stream(self, "dve", nc.vector),
            "pool": Stream(self, "pool", nc.gpsimd),
            "sp": Stream(self, "sp", nc.sync),
        }
        self.dma_pool = []
        self.dma_live = []
        self.ndsem = 0
        self.rr = 0

    def _deps(self, reads, writes):
        d = {}
        for b in reads:
            for (s, v) in b.w:
                if d.get(s, 0) < v:
                    d[s] = v
        for b in writes:
            for (s, v) in b.w:
                if d.get(s, 0) < v:
                    d[s] = v
            for s, v in b.r.items():
                if d.get(s, 0) < v:
                    d[s] = v
        return d

    def _waits(self, S, d):
        waits = []
        for s, v in d.items():
            if S.name == "pe" and s in S.sems:
                continue
            if S.waited.get(s, 0) >= v:
                continue
            S.waited[s] = v
            waits.append((s, v))
        return waits

    def _mark(self, tok, reads, writes):
        s, v = tok
        for b in reads:
            if b.r.get(s, 0) < v:
                b.r[s] = v
        for b in writes:
            b.w = [tok]
            b.r = {}

    def op(self, st, fn, reads=(), writes=()):
        S = self.st[st]
        waits = self._waits(S, self._deps(reads, writes))
        sem = S.cur_sem()
        sem.v += 1
        tok = (sem, sem.v)
        S.insts.append((waits, fn, sem, 1))
        self._mark(tok, reads, writes)
        return tok

    def _dsem(self, owner):
        if owner.dsem is None or owner.dsem.v >= SEM_LIMIT:
            if self.dma_pool:
                owner.dsem = self.dma_pool.pop()
            else:
                owner.dsem = Sem(self.nc, "d%d" % self.ndsem)
                self.ndsem += 1
            self.dma_live.append(owner.dsem)
        return owner.dsem

    def dma(self, out, in_, reads, writes, owner, st="sp", bg=False):
        S = self.st[st]
        waits = self._waits(S, self._deps(reads, writes))
        if bg:
            if owner.dsem is None:
                owner.dsem = Sem(self.nc, "g%d" % self.ndsem)
                self.ndsem += 1
            sem = owner.dsem
        else:
            sem = self._dsem(owner)
        sem.v += 16
        tok = (sem, sem.v)

        def fn(e, out=out, in_=in_):
            return e.dma_start(out=out, in_=in_)

        S.insts.append((waits, fn, sem, 16))
        self._mark(tok, reads, writes)
        return tok

    def barrier(self):
        toks = {}
        for S in self.st.values():
            if S.sem is not None and S.sem.v > 0:
                toks[S.sem] = S.sem.v
        for s in self.dma_live:
            if s.v > 0:
                toks[s] = s.v
        for S in self.st.values():
            waits = self._waits(S, dict(toks))
            if waits:
                S.insts.append((waits, None, None, 0))
        for s in self.dma_live:
            if s.v < SEM_LIMIT and s not in self.dma_pool:
                self.dma_pool.append(s)
        self.dma_live = []

    def emit(self):
        nc = self.nc
        with nc.Block() as block:
            def run(S):
                def body(e):
                    for (waits, fn, sem, inc) in S.insts:
                        for (s, v) in waits:
                            e.wait_ge(s.h, v)
                        if fn is not None:
                            ins = fn(e)
                            ins.then_inc(sem.h, inc)
                return body
            block.tensor(run(self.st["pe"]))
            block.scalar(run(self.st["act"]))
            block.vector(run(self.st["dve"]))
            block.gpsimd(run(self.st["pool"]))
            block.sync(run(self.st["sp"]))


class Alloc:
    def __init__(self, t, ncols):
        self.t = t
        self.n = ncols
        self.off = 0

    def take(self, units):
        a = self.off
        al = (units + 15) // 16 * 16
        assert a + al <= self.n, ("sbuf pool overflow", a, units, self.n)
        self.off += al
        return a


class Pool2:
    def __init__(self, al, bf):
        self.al = al
        self.bf = bf

    @property
    def off(self):
        return self.al.off

    @off.setter
    def off(self, v):
        self.al.off = v

    def alloc(self, ncols):
        if self.bf:
            units = (ncols + 1) // 2
            a = self.al.take(units)
            return self.al.t[:, a:a + units].bitcast(BF16)[:, 0:ncols]
        a = self.al.take(ncols)
        return self.al.t[:, a:a + ncols]


class K:
    def __init__(self, cfg):
        self.cfg = cfg
        self.nc = bass.Bass("TRN2", target_bir_lowering=False)
        self.P = Prog(self.nc)
        self.dr = {}
        self.cast_rr = 0

    def din(self, name, shape, dt=F32):
        t = self.nc.dram_tensor(name, list(shape), dt, kind="ExternalInput").ap()
        self.dr[name] = t
        return t

    def dscr(self, name, shape, dt=F32):
        t = self.nc.dram_tensor(name, list(shape), dt, kind="Internal").ap()
        self.dr[name] = t
        return t

    def dout(self, name, shape, dt=F32):
        t = self.nc.dram_tensor(name, list(shape), dt, kind="ExternalOutput").ap()
        self.dr[name] = t
        return t

    def act(self, out, in_, func, reads, writes, scale=None, bias=None):
        kw = {}
        if scale is not None:
            kw["scale"] = scale
        if bias is not None:
            kw["bias"] = bias
        return self.P.op("act", lambda e: e.activation(out=out, in_=in_, func=func, **kw), reads, writes)

    def tt(self, st, out, in0, in1, op, reads, writes):
        return self.P.op(st, lambda e: e.tensor_tensor(out=out, in0=in0, in1=in1, op=op), reads, writes)

    def ts(self, st, out, in0, s1, op0, reads, writes, s2=None, op1=None):
        if op1 is None:
            return self.P.op(st, lambda e: e.tensor_scalar(out=out, in0=in0, scalar1=s1, scalar2=None, op0=op0),
                             reads, writes)
        return self.P.op(st, lambda e: e.tensor_scalar(out=out, in0=in0, scalar1=s1, scalar2=s2, op0=op0, op1=op1),
                         reads, writes)

    def stt(self, out, in0, scalar, in1, op0, op1, reads, writes):
        return self.P.op("dve", lambda e: e.scalar_tensor_tensor(out=out, in0=in0, scalar=scalar, in1=in1,
                                                                 op0=op0, op1=op1), reads, writes)

    def copy(self, st, out, in_, reads, writes):
        if st == "act":
            return self.P.op("act", lambda e: e.activation(out=out, in_=in_, func=AF.Copy), reads, writes)
        return self.P.op(st, lambda e: e.tensor_copy(out=out, in_=in_), reads, writes)

    def cast_any(self, out, in_, reads, writes, engines=("act", "dve")):
        st = engines[self.cast_rr % len(engines)]
        self.cast_rr += 1
        return self.copy(st, out, in_, reads, writes)

    def mm(self, items, reads, writes):
        def fn(e):
            ins = None
            for (o, l, r, st, sp) in items:
                ins = e.matmul(o, l, r, start=st, stop=sp)
            return ins
        return self.P.op("pe", fn, reads, writes)


def build(cfg, debug_outs=()):
    kb = K(cfg)
    nc = kb.nc
    P = kb.P
    D, SEQ, KC, NT, FC = cfg.D, cfg.SEQ, cfg.KC, cfg.NT, cfg.FC
    SSM_W, MEM_W, MIX_W = cfg.SSM_W, cfg.MEM_W, cfg.MIX_W
    SC = SSM_W // 128
    MC = MEM_W // 128
    XC = MIX_W // 128
    MT = cfg.MEM_TOKENS
    NPAIR = cfg.NPAIR
    H = cfg.SB_HEADS

    xT = kb.din("xT", [D, SEQ])
    memT = kb.din("memT", [D, MT])
    gains = kb.din("gains", [128, 7 * KC])
    qkg = kb.din("qkg", [128, 8])
    convp = kb.din("convp", [128, 2 * 2 * FC * 4])
    s5rep = kb.din("s5rep", [128, 3 * NPAIR * 128])
    s5par = kb.din("s5par", [128, 3 * NPAIR])
    s5B = kb.din("s5B", [128, 2 * NPAIR * 128])
    s5C = kb.din("s5C", [128, 2 * NPAIR * 128])
    s5D = kb.din("s5D", [128, SC])
    consts = kb.din("consts", [128, 128 + 128 + 4 * TT])

    def wspec(name, Kd, ntiles, nblk):
        kc = Kd // 128
        w32 = kb.din(name, [ntiles * 128, kc * nblk])
        w16 = kb.dscr(name + "_bf", [ntiles * 128, kc * nblk], BF16)
        return dict(name=name, w32=w32, w16=w16, kc=kc, ntiles=ntiles, nblk=nblk)

    W = {}
    for l in range(2):
        W["w_in%d" % l] = wspec("w_in%d" % l, D, MIX_W // 256, 256)
        W["w_out%d" % l] = wspec("w_out%d" % l, MIX_W, D // 256, 256)
        W["w_mkv%d" % l] = wspec("w_mkv%d" % l, D, 2 * MEM_W // 256, 256)
        W["w_up%d" % l] = wspec("w_up%d" % l, D, FC, 256)
        W["w_dn%d" % l] = wspec("w_dn%d" % l, cfg.D_FF, D // 128, 128)
    W["w_glu"] = wspec("w_glu", SSM_W, SSM_W // 256, 256)
    W["w_kv"] = wspec("w_kv", D, 2 * SSM_W // 256, 256)

    HA = kb.dscr("HA", [D, SEQ])
    HB = kb.dscr("HB", [D, SEQ])
    PROJ = kb.dscr("PROJ", [MIX_W, SEQ])
    MIX = kb.dscr("MIX", [MIX_W, SEQ])
    GS5 = kb.dscr("GS5", [SSM_W, SEQ])
    HID = kb.dscr("HID", [cfg.D_FF, SEQ], BF16)
    KT = kb.dscr("KTs", [SSM_W, SEQ], BF16)
    VTM = kb.dscr("VTM", [SEQ, SSM_W], BF16)
    yT = kb.dout("yT", [D, SEQ])
    dbg = {}
    for nm in debug_outs:
        src = kb.dr[nm]
        dbg[nm] = kb.dout("dbg_" + nm, list(src.shape), src.dtype)

    import contextlib
    es = contextlib.ExitStack()
    NALL = 53000
    big = es.enter_context(nc.sbuf_tensor("big", [128, NALL], F32))
    al = Alloc(big, NALL)
    p32 = Pool2(al, False)
    p16 = Pool2(al, True)
    banks = [es.enter_context(nc.psum_tensor("bank%d" % i, [128, 512], F32))[:, :] for i in range(8)]
    bankbuf = [Buf("bank%d" % i) for i in range(8)]

    cst32 = p32.alloc(128 + 128 + 4 * TT)
    cstb = Buf("cst")
    P.dma(cst32, consts, [], [cstb], cstb)
    TRI32 = cst32[:, 0:128]
    ONES32 = cst32[:, 128:256]
    MASK = [cst32[:, 256 + i * TT:256 + (i + 1) * TT] for i in range(4)]
    ones16 = p16.alloc(128)
    kb.copy("dve", ones16, ONES32, [cstb], [cstb])
    gn32 = p32.alloc(7 * KC)
    P.dma(gn32, gains, [], [cstb], cstb)
    qk32 = p32.alloc(8)
    P.dma(qk32, qkg, [], [cstb], cstb)
    base32 = p32.off

    def new_phase():
        P.barrier()
        p32.off = base32
        for b in bankbuf:
            b.w = []
            b.r = {}

    CAST_ORDER = ["w_in0", "w_mkv0", "w_glu", "w_out0", "w_up0", "w_dn0", "w_kv", "w_in1", "w_mkv1", "w_out1",
                  "w_up1", "w_dn1"]

    def cast_weights_dma():
        for name in CAST_ORDER:
            w = W[name]
            b = Buf(name)
            w["buf"] = b
            gate = [] if name in CAST_ORDER[:2] else [W[CAST_ORDER[0]]["buf"], W[CAST_ORDER[1]]["buf"]]
            for ti in range(w["ntiles"]):
                P.dma(w["w16"][ti * 128:(ti + 1) * 128, :], w["w32"][ti * 128:(ti + 1) * 128, :], gate, [], b,
                      st="pool", bg=True)
            b.w = [(b.dsem, b.dsem.v)]

    def cast_weights():
        new_phase()
        PIECE = 4096
        NB = 6
        st32 = [p32.alloc(PIECE) for _ in range(NB)]
        st16 = [p16.alloc(PIECE) for _ in range(NB)]
        b32 = [Buf() for _ in range(NB)]
        b16 = [Buf() for _ in range(NB)]
        i = 0
        for name, w in W.items():
            rows = w["ntiles"] * 128
            cols = w["kc"] * w["nblk"]
            for r0 in range(0, rows, 128):
                for c0 in range(0, cols, PIECE):
                    c1 = min(cols, c0 + PIECE)
                    n = c1 - c0
                    k = i % NB
                    i += 1
                    P.dma(st32[k][:, 0:n], w["w32"][r0:r0 + 128, c0:c1], [], [b32[k]], b32[k])
                    kb.cast_any(st16[k][:, 0:n], st32[k][:, 0:n], [b32[k]], [b16[k]], engines=("act", "dve"))
                    P.dma(w["w16"][r0:r0 + 128, c0:c1], st16[k][:, 0:n], [b16[k]], [], b16[k], st="pool")


    def load_act_tile(src, Kc, t0, tw, a32, abuf):
        v = src.rearrange("(k p) t -> p k t", p=128)
        a3 = a32.rearrange("p (k t) -> p k t", k=Kc)
        for k0 in range(0, Kc, 8):
            k1 = min(Kc, k0 + 8)
            P.dma(a3[:, k0:k1, :], v[:, k0:k1, t0:t0 + tw], [], [abuf], abuf)

    def rms_tile(a32, abuf, Kc, tw, gcol, xn, xnbuf, sq, sqbuf, rstd, rbuf, bank, eps_scale):
        items = []
        for kc in range(Kc):
            k2 = kc % 2
            kb.act(sq[k2][:, 0:tw], a32[:, kc * tw:(kc + 1) * tw], AF.Square, [abuf], [sqbuf[k2]])
            kb.mm([(banks[bank][:, 0:tw], ONES32, sq[k2][:, 0:tw], kc == 0, kc == Kc - 1)],
                  [sqbuf[k2], cstb], [bankbuf[bank]])
        kb.act(rstd[:, 0:tw], banks[bank][:, 0:tw], AF.Sqrt, [bankbuf[bank], cstb], [rbuf],
               scale=1.0 / (Kc * 128), bias=epsb[:, 0:1])
        P.op("dve", lambda e: e.reciprocal(out=rstd[:, 0:tw], in_=rstd[:, 0:tw]), [rbuf], [rbuf])
        for kc in range(Kc):
            kb.stt(xn[:, kc * tw:(kc + 1) * tw], a32[:, kc * tw:(kc + 1) * tw], gn32[:, gcol + kc:gcol + kc + 1],
                   rstd[:, 0:tw], ALU.mult, ALU.mult, [abuf, rbuf, cstb], [xnbuf])

    epsb = p32.alloc(1)
    P.op("dve", lambda e: e.memset(epsb, 1e-6), [], [cstb])
    base32 = p32.off

    class WStream:
        def __init__(self, maxcols):
            self.bufs = [p16.alloc(maxcols) for _ in range(3)]
            self.bb = [Buf() for _ in range(3)]
            self.i = 0
            self.q = []

        def prefetch(self, w, ti):
            k = self.i % 3
            self.i += 1
            cols = w["kc"] * w["nblk"]
            P.dma(self.bufs[k][:, 0:cols], w["w16"][ti * 128:(ti + 1) * 128, :], [w["buf"]] if "buf" in w else [],
                  [self.bb[k]], self.bb[k])
            self.q.append((self.bufs[k], self.bb[k]))

        def pop(self):
            return self.q.pop(0)

    def gemm_pass(w, xn, xnbuf, tw, ws, evac, tiles=None, pre=2):
        tiles = list(range(w["ntiles"])) if tiles is None else tiles
        kc_n = w["kc"]
        nb = w["nblk"]
        for j in range(min(pre, len(tiles))):
            ws.prefetch(w, tiles[j])
        bi = 0
        for idx, ti in enumerate(tiles):
            if idx + pre < len(tiles):
                ws.prefetch(w, tiles[idx + pre])
            wt, wb = ws.pop()
            for c in range(nb // 128):
                bank = gemm_pass.bank_rr % 4
                gemm_pass.bank_rr += 1
                items = []
                for kc in range(kc_n):
                    items.append((banks[bank][:, 0:tw], wt[:, kc * nb + c * 128: kc * nb + (c + 1) * 128],
                                  xn[:, kc * tw:(kc + 1) * tw], kc == 0, kc == kc_n - 1))
                kb.mm(items, [wb, xnbuf], [bankbuf[bank]])
                evac(ti, c, bank)
    gemm_pass.bank_rr = 0

    def store_tile(dst, r0, t0, tw, src, sbuf):
        P.dma(dst[r0:r0 + 128, t0:t0 + tw], src, [sbuf], [], sbuf)

    def phase_inproj(l, hsrc):
        new_phase()
        w = W["w_in%d" % l]
        a32 = p32.alloc(KC * TT)
        abuf = Buf()
        xn = p16.alloc(KC * TT)
        xnbuf = Buf()
        sq = [p32.alloc(TT) for _ in range(2)]
        sqb = [Buf() for _ in range(2)]
        rstd = p32.alloc(TT)
        rb = Buf()
        ws = WStream(w["kc"] * w["nblk"])
        ev = [p32.alloc(TT) for _ in range(4)]
        evb = [Buf() for _ in range(4)]
        cnt = [0]
        for tt in range(NT):
            t0 = tt * TT
            load_act_tile(hsrc, KC, t0, TT, a32, abuf)
            rms_tile(a32, abuf, KC, TT, l * KC, xn, xnbuf, sq, sqb, rstd, rb, 7, None)

            def evac(ti, c, bank, t0=t0):
                k = cnt[0] % 4
                cnt[0] += 1
                kb.copy("act" if k % 2 else "dve", ev[k], banks[bank][:, 0:TT], [bankbuf[bank]], [evb[k]])
                store_tile(PROJ, (ti * 2 + c) * 128, t0, TT, ev[k], evb[k])
            gemm_pass(w, xn, xnbuf, TT, ws, evac)

    def phase_memattn(l):
        new_phase()
        w = W["w_mkv%d" % l]
        m32 = p32.alloc(KC * MT)
        mb = Buf()
        mn = p16.alloc(KC * MT)
        mnb = Buf()
        sq = [p32.alloc(TT) for _ in range(2)]
        sqb = [Buf() for _ in range(2)]
        rstd = p32.alloc(TT)
        rb = Buf()
        load_act_tile(memT, KC, 0, MT, m32, mb)
        rms_tile(m32, mb, KC, MT, (2 + l) * KC, mn, mnb, sq, sqb, rstd, rb, 7, None)
        ws = WStream(w["kc"] * w["nblk"])
        k32 = p32.alloc(MC * MT)
        k32b = Buf()
        kn = p16.alloc(MC * MT)
        knb = Buf()
        MCH = MT // 128
        vtm = p16.alloc(MCH * MEM_W)
        vtb = Buf()
        nkt = MEM_W // 256
        ws_i = 0
        for j in range(min(2, w["ntiles"])):
            ws.prefetch(w, j)
        for ti in range(w["ntiles"]):
            if ti + 2 < w["ntiles"]:
                ws.prefetch(w, ti + 2)
            wt, wb = ws.pop()
            if ti < nkt:
                for c in range(2):
                    bank = c
                    items = [(banks[bank][:, 0:MT], wt[:, kc * 256 + c * 128: kc * 256 + (c + 1) * 128],
                              mn[:, kc * MT:(kc + 1) * MT], kc == 0, kc == KC - 1) for kc in range(KC)]
                    kb.mm(items, [wb, mnb], [bankbuf[bank]])
                    ch = ti * 2 + c
                    kb.copy("act", k32[:, ch * MT:(ch + 1) * MT], banks[bank][:, 0:MT], [bankbuf[bank]], [k32b])
            else:
                d0 = (ti - nkt) * 256
                for mc in range(MCH):
                    bank = 2 + mc % 2
                    items = [(banks[bank][:, 0:256], mn[:, kc * MT + mc * 128: kc * MT + (mc + 1) * 128],
                              wt[:, kc * 256:(kc + 1) * 256], kc == 0, kc == KC - 1) for kc in range(KC)]
                    kb.mm(items, [wb, mnb], [bankbuf[bank]])
                    kb.copy("dve", vtm[:, mc * MEM_W + d0: mc * MEM_W + d0 + 256], banks[bank][:, 0:256],
                            [bankbuf[bank]], [vtb])
        for h in range(cfg.MEM_HEADS):
            for dc in range(2):
                ch = 2 * h + dc
                kb.act(sq[dc][:, 0:MT], k32[:, ch * MT:(ch + 1) * MT], AF.Square, [k32b], [sqb[dc]])
                kb.mm([(banks[7][:, 0:MT], ONES32, sq[dc][:, 0:MT], dc == 0, dc == 1)], [sqb[dc], cstb], [bankbuf[7]])
            kb.act(rstd[:, 0:MT], banks[7][:, 0:MT], AF.Sqrt, [bankbuf[7], cstb], [rb], scale=1.0 / 256, bias=epsb[:, 0:1])
            P.op("dve", lambda e: e.reciprocal(out=rstd[:, 0:MT], in_=rstd[:, 0:MT]), [rb], [rb])
            for dc in range(2):
                ch = 2 * h + dc
                kb.stt(kn[:, ch * MT:(ch + 1) * MT], k32[:, ch * MT:(ch + 1) * MT],
                       qk32[:, 4 * l + 2 + dc:4 * l + 3 + dc], rstd[:, 0:MT], ALU.mult, ALU.mult,
                       [k32b, rb, cstb], [knb])
        q32 = p32.alloc(MC * TT)
        qb = Buf()
        qn = p16.alloc(MC * TT)
        qnb = Buf()
        pT = p16.alloc(MCH * TT)
        pTb = Buf()
        rden = p32.alloc(TT)
        rdb = Buf()
        ev = [p32.alloc(TT) for _ in range(2)]
        evb = [Buf() for _ in range(2)]
        cnt = 0
        for tt in range(NT):
            t0 = tt * TT
            v = PROJ.rearrange("(k p) t -> p k t", p=128)
            P.dma(q32.rearrange("p (k t) -> p k t", k=MC), v[:, SC:SC + MC, t0:t0 + TT], [], [qb], qb)
            for h in range(cfg.MEM_HEADS):
                for dc in range(2):
                    ch = 2 * h + dc
                    kb.act(sq[dc], q32[:, ch * TT:(ch + 1) * TT], AF.Square, [qb], [sqb[dc]])
                    kb.mm([(banks[7], ONES32, sq[dc], dc == 0, dc == 1)], [sqb[dc], cstb], [bankbuf[7]])
                kb.act(rstd, banks[7], AF.Sqrt, [bankbuf[7], cstb], [rb], scale=1.0 / 256, bias=epsb[:, 0:1])
                P.op("dve", lambda e: e.reciprocal(out=rstd, in_=rstd), [rb], [rb])
                for dc in range(2):
                    ch = 2 * h + dc
                    kb.stt(qn[:, ch * TT:(ch + 1) * TT], q32[:, ch * TT:(ch + 1) * TT],
                           qk32[:, 4 * l + dc:4 * l + dc + 1], rstd, ALU.mult, ALU.mult, [qb, rb, cstb], [qnb])
                for mc in range(MCH):
                    bank = mc % 2
                    items = [(banks[bank], kn[:, (2 * h + dc) * MT + mc * 128:(2 * h + dc) * MT + (mc + 1) * 128],
                              qn[:, (2 * h + dc) * TT:(2 * h + dc + 1) * TT], dc == 0, dc == 1) for dc in range(2)]
                    kb.mm(items, [knb, qnb], [bankbuf[bank]])
                    kb.act(pT[:, mc * TT:(mc + 1) * TT], banks[bank], AF.Exp, [bankbuf[bank]], [pTb], scale=1.0 / 16.0)
                items = [(banks[2], ones16, pT[:, mc * TT:(mc + 1) * TT], mc == 0, mc == MCH - 1) for mc in range(MCH)]
                kb.mm(items, [pTb, cstb], [bankbuf[2]])
                P.op("dve", lambda e: e.reciprocal(out=rden, in_=banks[2]), [bankbuf[2]], [rdb])
                for dc in range(2):
                    bank = 3 + dc
                    items = [(banks[bank], vtm[:, mc * MEM_W + h * 256 + dc * 128: mc * MEM_W + h * 256 + (dc + 1) * 128],
                              pT[:, mc * TT:(mc + 1) * TT], mc == 0, mc == MCH - 1) for mc in range(MCH)]
                    kb.mm(items, [vtb, pTb], [bankbuf[bank]])
                    k = cnt % 2
                    cnt += 1
                    kb.tt("dve", ev[k], banks[bank], rden, ALU.mult, [bankbuf[bank], rdb], [evb[k]])
                    store_tile(MIX, SSM_W + h * 256 + dc * 128, t0, TT, ev[k], evb[k])

    def phase_s5():
        new_phase()
        NQ = NPAIR * 128
        NP_ = NPAIR
        LRE = p16.alloc(NQ)
        LIM = p16.alloc(NQ)
        CRE = p16.alloc(NQ)
        CIMN = p16.alloc(NQ)
        Lb = Buf()
        par = p32.alloc(3 * NP_)
        pbuf = Buf()
        P.dma(par, s5par, [], [pbuf], pbuf)
        T2 = [p32.alloc(NP_) for _ in range(6)]
        t2b = Buf()
        scr2 = p32.alloc(NP_)
        scr2i = p32.alloc(NP_).bitcast(I32)
        NLEV = 10
        CLx = [p32.alloc(NP_) for _ in range(NLEV - 1)]
        SLx = [p32.alloc(NP_) for _ in range(NLEV - 1)]
        d32 = p32.alloc(SC)
        mark = p32.off

        def prep(lr_in, li_in, ls_in, T, bufs_in, ob):
            step, lr, mag, th, cs, sn = T
            kb.act(step, ls_in, AF.Exp, bufs_in, [ob])
            kb.ts("dve", lr, lr_in, -1e-4, ALU.min, bufs_in, [ob])
            kb.tt("dve", mag, lr, step, ALU.mult, [ob], [ob])
            kb.act(mag, mag, AF.Exp, [ob], [ob])
            kb.tt("dve", th, li_in, step, ALU.mult, bufs_in + [ob], [ob])
            return step, lr, mag, th, cs, sn

        def sincos(th, cs, sn, ob, scr, scri):
            for (dst, shift) in ((sn, 0.0), (cs, math.pi / 2)):
                kb.ts("dve", scr, th, 1.0 / (2 * math.pi), ALU.mult, [ob], [ob], s2=shift / (2 * math.pi) + 0.5, op1=ALU.add)
                kb.copy("dve", scri, scr, [ob], [ob])
                kb.copy("dve", scr, scri, [ob], [ob])
                kb.ts("dve", scr, scr, -2 * math.pi, ALU.mult, [ob], [ob], s2=shift, op1=ALU.add)
                kb.tt("dve", dst, th, scr, ALU.add, [ob], [ob])
                kb.ts("dve", scr, dst, math.pi, ALU.is_gt, [ob], [ob], s2=-2 * math.pi, op1=ALU.mult)
                kb.tt("dve", dst, dst, scr, ALU.add, [ob], [ob])
                kb.ts("dve", scr, dst, -math.pi, ALU.is_lt, [ob], [ob], s2=2 * math.pi, op1=ALU.mult)
                kb.tt("dve", dst, dst, scr, ALU.add, [ob], [ob])
                kb.act(dst, dst, AF.Sin, [ob], [ob])

        PB = min(8, NPAIR)
        BQ = PB * 128
        rep = p32.alloc(3 * BQ)
        rbuf_ = Buf()
        tmp = [p32.alloc(BQ) for _ in range(6)]
        tb = Buf()
        scr = p32.alloc(BQ)
        scri = p32.alloc(BQ).bitcast(I32)
        Bp = p32.alloc(2 * BQ)
        Bb = Buf()
        Cp = p32.alloc(2 * BQ)
        Cb = Buf()
        rep3 = s5rep.rearrange("p (w q) -> p w q", w=3)
        B3 = s5B.rearrange("p (w q) -> p w q", w=2)
        C3 = s5C.rearrange("p (w q) -> p w q", w=2)
        for blk in range(NPAIR // PB):
            q0 = blk * BQ
            P.dma(rep.rearrange("p (w q) -> p w q", w=3), rep3[:, :, q0:q0 + BQ], [], [rbuf_], rbuf_)
            P.dma(Bp.rearrange("p (w q) -> p w q", w=2), B3[:, :, q0:q0 + BQ], [], [Bb], Bb)
            P.dma(Cp.rearrange("p (w q) -> p w q", w=2), C3[:, :, q0:q0 + BQ], [], [Cb], Cb)
            li = rep[:, BQ:2 * BQ]
            step, lr, mag, th, cs, sn = prep(rep[:, 0:BQ], li, rep[:, 2 * BQ:3 * BQ], tmp, [rbuf_], tb)
            sincos(th, cs, sn, tb, scr, scri)
            kb.tt("dve", cs, cs, mag, ALU.mult, [tb], [tb])
            kb.tt("dve", sn, sn, mag, ALU.mult, [tb], [tb])
            kb.tt("dve", step, lr, lr, ALU.mult, [tb], [tb])
            kb.tt("dve", scr, li, li, ALU.mult, [rbuf_, tb], [tb])
            kb.tt("dve", step, step, scr, ALU.add, [tb], [tb])
            P.op("dve", lambda e, a=step: e.reciprocal(out=a, in_=a), [tb], [tb])
            kb.ts("dve", mag, cs, -1.0, ALU.add, [tb], [tb])
            kb.tt("dve", th, mag, lr, ALU.mult, [tb], [tb])
            kb.tt("dve", scr, sn, li, ALU.mult, [rbuf_, tb], [tb])
            kb.tt("dve", th, th, scr, ALU.add, [tb], [tb])
            kb.tt("dve", th, th, step, ALU.mult, [tb], [tb])
            kb.tt("dve", cs, sn, lr, ALU.mult, [tb], [tb])
            kb.tt("dve", scr, mag, li, ALU.mult, [rbuf_, tb], [tb])
            kb.tt("dve", cs, cs, scr, ALU.subtract, [tb], [tb])
            kb.tt("dve", cs, cs, step, ALU.mult, [tb], [tb])
            f_re, f_im = th, cs
            br, bi = Bp[:, 0:BQ], Bp[:, BQ:2 * BQ]
            kb.tt("dve", mag, f_re, br, ALU.mult, [tb, Bb], [tb])
            kb.tt("dve", scr, f_im, bi, ALU.mult, [tb, Bb], [tb])
            kb.tt("dve", LRE[:, q0:q0 + BQ], mag, scr, ALU.subtract, [tb], [Lb])
            kb.tt("dve", mag, f_re, bi, ALU.mult, [tb, Bb], [tb])
            kb.tt("dve", scr, f_im, br, ALU.mult, [tb, Bb], [tb])
            kb.tt("dve", LIM[:, q0:q0 + BQ], mag, scr, ALU.add, [tb], [Lb])
            kb.copy("dve", CRE[:, q0:q0 + BQ], Cp[:, 0:BQ], [Cb], [Lb])
            kb.ts("dve", CIMN[:, q0:q0 + BQ], Cp[:, BQ:2 * BQ], -1.0, ALU.mult, [Cb], [Lb])
        step2, lr2, rho, th2, c0, s0 = prep(par[:, 0:NP_], par[:, NP_:2 * NP_], par[:, 2 * NP_:3 * NP_], T2, [pbuf], t2b)
        sincos(th2, c0, s0, t2b, scr2, scr2i)
        CL = [c0] + CLx
        SL = [s0] + SLx
        for m in range(1, NLEV):
            kb.tt("dve", CL[m], CL[m - 1], CL[m - 1], ALU.mult, [t2b], [t2b])
            kb.tt("dve", scr2, SL[m - 1], SL[m - 1], ALU.mult, [t2b], [t2b])
            kb.tt("dve", CL[m], CL[m], scr2, ALU.subtract, [t2b], [t2b])
            kb.tt("dve", SL[m], CL[m - 1], SL[m - 1], ALU.mult, [t2b], [t2b])
            kb.ts("dve", SL[m], SL[m], 2.0, ALU.mult, [t2b], [t2b])
        P.dma(d32, s5D, [], [t2b], t2b)
        P.barrier()
        p32.off = mark

        NPC = 4
        Ec = [p32.alloc(TT) for _ in range(NPC)]
        Es = [p32.alloc(TT) for _ in range(NPC)]
        Eb = [Buf() for _ in range(NPC)]
        rhoT = [p32.alloc(TT) for _ in range(NPC)]
        init = [p32.alloc(2) for _ in range(NPC)]
        inb = [Buf() for _ in range(NPC)]
        u32 = [p32.alloc(TT) for _ in range(2)]
        u32b = [Buf() for _ in range(2)]
        u16 = [p16.alloc(TT) for _ in range(2)]
        u16b = [Buf() for _ in range(2)]
        NW = NPC
        NTL = 8
        wk = [[p32.alloc(TT) for _ in range(NTL)] for _ in range(NW)]
        wb2 = [[Buf() for _ in range(NTL)] for _ in range(NW)]
        x16 = [[p16.alloc(TT) for _ in range(2)] for _ in range(NW)]
        x16b = [[Buf() for _ in range(2)] for _ in range(NW)]
        tiny = [p32.alloc(4) for _ in range(NW)]
        tinyb = [Buf() for _ in range(NW)]
        gout = [p32.alloc(TT) for _ in range(2)]
        goutb = [Buf() for _ in range(2)]
        uc = 0
        gc = 0
        bc = 0
        for c in range(SC):
            for pi in range(NPC):
                kb.ts("dve", Ec[pi][:, 0:1], ONES32[:, 0:1], 1.0, ALU.mult, [cstb], [Eb[pi]])
                kb.ts("dve", Es[pi][:, 0:1], ONES32[:, 0:1], 0.0, ALU.mult, [cstb], [Eb[pi]])
            for m in range(NLEV - 1):
                s = 1 << m
                for pi in range(NPC):
                    q = c * NPC + pi
                    sm = SL[m][:, q:q + 1]
                    eng = "dve"
                    kb.ts(eng, wk[pi][2][:, 0:s], Es[pi][:, 0:s], sm, ALU.mult, [Eb[pi], t2b], [wb2[pi][2]])
                    kb.ts(eng, wk[pi][3][:, 0:s], Ec[pi][:, 0:s], sm, ALU.mult, [Eb[pi], t2b], [wb2[pi][3]])
                for pi in range(NPC):
                    q = c * NPC + pi
                    cm = CL[m][:, q:q + 1]
                    kb.stt(Ec[pi][:, s:2 * s], Ec[pi][:, 0:s], cm, wk[pi][2][:, 0:s], ALU.mult, ALU.subtract,
                           [Eb[pi], wb2[pi][2], t2b], [Eb[pi]])
                    kb.stt(Es[pi][:, s:2 * s], Es[pi][:, 0:s], cm, wk[pi][3][:, 0:s], ALU.mult, ALU.add,
                           [Eb[pi], wb2[pi][3], t2b], [Eb[pi]])
            for pi in range(NPC):
                q = c * NPC + pi
                kb.act(rhoT[pi], ONES_T, AF.Copy, [cstb, t2b], [Eb[pi]], scale=rho[:, q:q + 1])
                P.op("dve", lambda e, a=init[pi]: e.memset(a, 0.0), [], [inb[pi]])
            for tt in range(NT):
                t0 = tt * TT
                k = uc % 2
                uc += 1
                P.dma(u32[k], PROJ[c * 128:(c + 1) * 128, t0:t0 + TT], [], [u32b[k]], u32b[k])
                kb.copy("act", u16[k], u32[k], [u32b[k]], [u16b[k]])
                ybank = 6 + (gc % 2)
                R = range(NPC)
                for pi in R:
                    q = c * NPC + pi
                    ba = 2 * (bc % 3)
                    bc += 1
                    kb.mm([(banks[ba], LRE[:, q * 128:(q + 1) * 128], u16[k], True, True)], [Lb, u16b[k]], [bankbuf[ba]])
                    kb.mm([(banks[ba + 1], LIM[:, q * 128:(q + 1) * 128], u16[k], True, True)], [Lb, u16b[k]], [bankbuf[ba + 1]])
                    kb.copy("act", wk[pi][0], banks[ba], [bankbuf[ba]], [wb2[pi][0]])
                    kb.copy("act", wk[pi][1], banks[ba + 1], [bankbuf[ba + 1]], [wb2[pi][1]])
                for pi in R:
                    br_, bi_, t1, t2, t3, t4, wr, wim = wk[pi]
                    Bbr, Bbi, B1, B2, B3, B4, Bwr, Bwi = wb2[pi]
                    kb.tt("dve", t1, Ec[pi], br_, ALU.mult, [Eb[pi], Bbr], [B1])
                    kb.tt("dve", t2, Es[pi], bi_, ALU.mult, [Eb[pi], Bbi], [B2])
                    kb.tt("dve", t3, Ec[pi], bi_, ALU.mult, [Eb[pi], Bbi], [B3])
                    kb.tt("dve", t4, Es[pi], br_, ALU.mult, [Eb[pi], Bbr], [B4])
                for pi in R:
                    br_, bi_, t1, t2, t3, t4, wr, wim = wk[pi]
                    Bbr, Bbi, B1, B2, B3, B4, Bwr, Bwi = wb2[pi]
                    kb.tt("dve", t1, t1, t2, ALU.add, [B1, B2], [B1])
                    kb.tt("dve", t3, t3, t4, ALU.subtract, [B3, B4], [B3])
                for pi in R:
                    br_, bi_, t1, t2, t3, t4, wr, wim = wk[pi]
                    Bbr, Bbi, B1, B2, B3, B4, Bwr, Bwi = wb2[pi]
                    P.op("dve", lambda e, o=wr, a=rhoT[pi], b=t1, i0=init[pi][:, 0:1]:
                         e.tensor_tensor_scan(out=o, data0=a, data1=b, initial=i0, op0=ALU.mult, op1=ALU.add),
                         [Eb[pi], B1, inb[pi]], [Bwr])
                    P.op("dve", lambda e, o=wim, a=rhoT[pi], b=t3, i0=init[pi][:, 1:2]:
                         e.tensor_tensor_scan(out=o, data0=a, data1=b, initial=i0, op0=ALU.mult, op1=ALU.add),
                         [Eb[pi], B3, inb[pi]], [Bwi])
                for pi in R:
                    q = c * NPC + pi
                    br_, bi_, t1, t2, t3, t4, wr, wim = wk[pi]
                    Bbr, Bbi, B1, B2, B3, B4, Bwr, Bwi = wb2[pi]
                    c9 = CL[NLEV - 1][:, q:q + 1]
                    s9 = SL[NLEV - 1][:, q:q + 1]
                    tn = tiny[pi]
                    kb.ts("dve", tn[:, 0:1], wim[:, TT - 1:TT], s9, ALU.mult, [Bwi, t2b], [tinyb[pi]])
                    kb.ts("dve", tn[:, 1:2], wr[:, TT - 1:TT], s9, ALU.mult, [Bwr, t2b], [tinyb[pi]])
                    kb.stt(init[pi][:, 0:1], wr[:, TT - 1:TT], c9, tn[:, 0:1], ALU.mult, ALU.subtract, [Bwr, tinyb[pi], t2b], [inb[pi]])
                    kb.stt(init[pi][:, 1:2], wim[:, TT - 1:TT], c9, tn[:, 1:2], ALU.mult, ALU.add, [Bwi, tinyb[pi], t2b], [inb[pi]])
                for pi in R:
                    br_, bi_, t1, t2, t3, t4, wr, wim = wk[pi]
                    Bbr, Bbi, B1, B2, B3, B4, Bwr, Bwi = wb2[pi]
                    kb.tt("dve", t1, Ec[pi], wr, ALU.mult, [Eb[pi], Bwr], [B1])
                    kb.tt("dve", t2, Es[pi], wim, ALU.mult, [Eb[pi], Bwi], [B2])
                    kb.tt("dve", t4, Ec[pi], wim, ALU.mult, [Eb[pi], Bwi], [B4])
                    kb.tt("dve", t3, Es[pi], wr, ALU.mult, [Eb[pi], Bwr], [B3])
                for pi in R:
                    br_, bi_, t1, t2, t3, t4, wr, wim = wk[pi]
                    Bbr, Bbi, B1, B2, B3, B4, Bwr, Bwi = wb2[pi]
                    kb.tt("dve", x16[pi][0], t1, t2, ALU.subtract, [B1, B2], [x16b[pi][0]])
                    kb.tt("dve", x16[pi][1], t3, t4, ALU.add, [B3, B4], [x16b[pi][1]])
                for pi in R:
                    q = c * NPC + pi
                    kb.mm([(banks[ybank], CRE[:, q * 128:(q + 1) * 128], x16[pi][0], pi == 0, False),
                           (banks[ybank], CIMN[:, q * 128:(q + 1) * 128], x16[pi][1], False, pi == NPC - 1)],
                          [Lb, x16b[pi][0], x16b[pi][1]], [bankbuf[ybank]])
                g = gc % 2
                gc += 1
                kb.stt(gout[g], u32[k], d32[:, c:c + 1], banks[ybank], ALU.mult, ALU.add, [u32b[k], t2b, bankbuf[ybank]], [goutb[g]])
                kb.act(gout[g], gout[g], AF.Gelu, [goutb[g]], [goutb[g]])
                store_tile(GS5, c * 128, t0, TT, gout[g], goutb[g])


    ONES_T = None

    def phase_glu():
        new_phase()
        w = W["w_glu"]
        a32 = p32.alloc(SC * TT)
        abuf = Buf()
        xn = p16.alloc(SC * TT)
        xnbuf = Buf()
        ws = WStream(w["kc"] * w["nblk"])
        ev = [p32.alloc(TT) for _ in range(4)]
        evb = [Buf() for _ in range(4)]
        cnt = [0]
        for tt in range(NT):
            t0 = tt * TT
            load_act_tile(GS5, SC, t0, TT, a32, abuf)
            for kc in range(SC):
                kb.cast_any(xn[:, kc * TT:(kc + 1) * TT], a32[:, kc * TT:(kc + 1) * TT], [abuf], [xnbuf], engines=("act", "dve"))

            def evac(ti, c, bank, t0=t0):
                k = cnt[0] % 4
                cnt[0] += 1
                ch = ti * 2 + c
                kb.act(ev[k], banks[bank], AF.Sigmoid, [bankbuf[bank]], [evb[k]])
                kb.tt("dve", ev[k], ev[k], a32[:, ch * TT:(ch + 1) * TT], ALU.mult, [evb[k], abuf], [evb[k]])
                store_tile(MIX, ch * 128, t0, TT, ev[k], evb[k])
            gemm_pass(w, xn, xnbuf, TT, ws, evac)

    def phase_resid(w, src, src_dt, Kc, hold, hnew):
        new_phase()
        if src_dt == F32:
            a32 = p32.alloc(Kc * TT)
            abuf = Buf()
        xn = p16.alloc(Kc * TT)
        xnbuf = Buf()
        ws = WStream(w["kc"] * w["nblk"])
        hin = [p32.alloc(TT) for _ in range(4)]
        hinb = [Buf() for _ in range(4)]
        cnt = [0]
        nchunks = w["ntiles"] * (w["nblk"] // 128)
        for tt in range(NT):
            t0 = tt * TT
            if src_dt == F32:
                load_act_tile(src, Kc, t0, TT, a32, abuf)
                for kc in range(Kc):
                    kb.cast_any(xn[:, kc * TT:(kc + 1) * TT], a32[:, kc * TT:(kc + 1) * TT], [abuf], [xnbuf])
            else:
                v = src.rearrange("(k p) t -> p k t", p=128)
                x3 = xn.rearrange("p (k t) -> p k t", k=Kc)
                for k0 in range(0, Kc, 8):
                    k1 = min(Kc, k0 + 8)
                    P.dma(x3[:, k0:k1, :], v[:, k0:k1, t0:t0 + TT], [], [xnbuf], xnbuf)
            pend = []

            def evac(ti, c, bank, t0=t0):
                k = cnt[0] % 4
                cnt[0] += 1
                ch = ti * (w["nblk"] // 128) + c
                P.dma(hin[k], hold[ch * 128:(ch + 1) * 128, t0:t0 + TT], [], [hinb[k]], hinb[k])
                kb.tt("dve", hin[k], banks[bank], hin[k], ALU.add, [bankbuf[bank], hinb[k]], [hinb[k]])
                store_tile(hnew, ch * 128, t0, TT, hin[k], hinb[k])
            gemm_pass(w, xn, xnbuf, TT, ws, evac)

    def phase_ffn_up(l, hsrc):
        new_phase()
        w = W["w_up%d" % l]
        a32 = p32.alloc(KC * TT)
        abuf = Buf()
        xn = p16.alloc(KC * TT)
        xnbuf = Buf()
        sq = [p32.alloc(TT) for _ in range(2)]
        sqb = [Buf() for _ in range(2)]
        rstd = p32.alloc(TT)
        rb = Buf()
        cp = p32.alloc(2 * FC * 4)
        cpb = Buf()
        P.dma(cp, convp[:, l * 2 * FC * 4:(l + 1) * 2 * FC * 4], [], [cpb], cpb)
        tails = p32.alloc(2 * FC * 2)
        tlb = Buf()
        P.op("dve", lambda e: e.memset(tails, 0.0), [], [tlb])
        ws = WStream(w["kc"] * w["nblk"])
        NB_ = 2
        ua = [[p32.alloc(TT + 2) for _ in range(2)] for _ in range(NB_)]
        uab = [Buf() for _ in range(NB_)]
        cv = [[p32.alloc(TT) for _ in range(2)] for _ in range(NB_)]
        cvb = [Buf() for _ in range(NB_)]
        hid = [p16.alloc(TT) for _ in range(NB_)]
        hidb = [Buf() for _ in range(NB_)]
        cnt = [0]
        for tt in range(NT):
            t0 = tt * TT
            load_act_tile(hsrc, KC, t0, TT, a32, abuf)
            rms_tile(a32, abuf, KC, TT, (4 + l) * KC, xn, xnbuf, sq, sqb, rstd, rb, 7, None)

            def evac(ti, c, bank, t0=t0):
                k = (cnt[0] // 2) % NB_
                cnt[0] += 1
                u = ua[k][c]
                pc = cp[:, (c * FC + ti) * 4:(c * FC + ti) * 4 + 4]
                tl = tails[:, (c * FC + ti) * 2:(c * FC + ti) * 2 + 2]
                kb.copy("act", u[:, 0:2], tl, [tlb], [uab[k]])
                kb.copy("act" if c == 0 else "dve", u[:, 2:TT + 2], banks[bank], [bankbuf[bank]], [uab[k]])
                kb.copy("act", tl, u[:, TT:TT + 2], [uab[k]], [tlb])
                o = cv[k][c]
                kb.act(o, u[:, 2:TT + 2], AF.Identity, [uab[k], cpb], [cvb[k]], scale=pc[:, 2:3], bias=pc[:, 3:4])
                kb.stt(o, u[:, 1:TT + 1], pc[:, 1:2], o, ALU.mult, ALU.add, [uab[k], cpb, cvb[k]], [cvb[k]])
                kb.stt(o, u[:, 0:TT], pc[:, 0:1], o, ALU.mult, ALU.add, [uab[k], cpb, cvb[k]], [cvb[k]])
                if c == 1:
                    kb.act(cv[k][0], cv[k][0], AF.Silu, [cvb[k]], [cvb[k]])
                    kb.tt("dve", hid[k], cv[k][0], cv[k][1], ALU.mult, [cvb[k]], [hidb[k]])
                    store_tile(HID, ti * 128, t0, TT, hid[k], hidb[k])
            gemm_pass(w, xn, xnbuf, TT, ws, evac)

    def phase_kv(hsrc):
        new_phase()
        w = W["w_kv"]
        a32 = p32.alloc(KC * TT)
        abuf = Buf()
        xn = p16.alloc(KC * TT)
        xnbuf = Buf()
        sq = [p32.alloc(TT) for _ in range(2)]
        sqb = [Buf() for _ in range(2)]
        rstd = p32.alloc(TT)
        rb = Buf()
        ws = WStream(w["kc"] * w["nblk"])
        ev = [p16.alloc(TT) for _ in range(4)]
        evb = [Buf() for _ in range(4)]
        cnt = [0]
        nkt = SSM_W // 256
        for tt in range(NT):
            t0 = tt * TT
            load_act_tile(hsrc, KC, t0, TT, a32, abuf)
            rms_tile(a32, abuf, KC, TT, 6 * KC, xn, xnbuf, sq, sqb, rstd, rb, 7, None)

            def evac(ti, c, bank, t0=t0):
                k = cnt[0] % 4
                cnt[0] += 1
                kb.copy("act" if k % 2 else "dve", ev[k], banks[bank], [bankbuf[bank]], [evb[k]])
                store_tile(KT, (ti * 2 + c) * 128, t0, TT, ev[k], evb[k])
            gemm_pass(w, xn, xnbuf, TT, ws, evac, tiles=list(range(nkt)))
            vt = list(range(nkt, 2 * nkt))
            for j in range(min(2, len(vt))):
                ws.prefetch(w, vt[j])
            for idx, ti in enumerate(vt):
                if idx + 2 < len(vt):
                    ws.prefetch(w, vt[idx + 2])
                wt, wb = ws.pop()
                for tc in range(TT // 128):
                    bank = gemm_pass.bank_rr % 4
                    gemm_pass.bank_rr += 1
                    items = [(banks[bank][:, 0:256], xn[:, kc * TT + tc * 128: kc * TT + (tc + 1) * 128],
                              wt[:, kc * 256:(kc + 1) * 256], kc == 0, kc == KC - 1) for kc in range(KC)]
                    kb.mm(items, [wb, xnbuf], [bankbuf[bank]])
                    k = cnt[0] % 4
                    cnt[0] += 1
                    kb.copy("act" if k % 2 else "dve", ev[k][:, 0:256], banks[bank][:, 0:256], [bankbuf[bank]], [evb[k]])
                    P.dma(VTM[t0 + tc * 128: t0 + (tc + 1) * 128, (ti - nkt) * 256:(ti - nkt + 1) * 256], ev[k][:, 0:256],
                          [evb[k]], [], evb[k])

    def phase_sb():
        new_phase()
        NKC = SEQ // 128
        scale = 1.0 / math.sqrt(128.0)
        kT = [p16.alloc(SEQ) for _ in range(2)]
        kTb = [Buf() for _ in range(2)]
        vt = [p16.alloc(NKC * 128) for _ in range(2)]
        vtb = [Buf() for _ in range(2)]
        q32 = [p32.alloc(TT) for _ in range(2)]
        q32b = [Buf() for _ in range(2)]
        q16 = [p16.alloc(TT) for _ in range(3)]
        q16b = [Buf() for _ in range(3)]
        NW = 4
        E = [p32.alloc(TT) for _ in range(NW)]
        Eb_ = [Buf() for _ in range(NW)]
        SP_ = [p32.alloc(TT) for _ in range(NW)]
        SPb = [Buf() for _ in range(NW)]
        X = [p32.alloc(TT) for _ in range(NW)]
        Xb = [Buf() for _ in range(NW)]
        sacc = [p32.alloc(TT) for _ in range(3)]
        saccb = [Buf() for _ in range(3)]
        W16 = [p16.alloc(TT) for _ in range(NW)]
        W16b = [Buf() for _ in range(NW)]
        ev = [p32.alloc(TT) for _ in range(2)]
        evb = [Buf() for _ in range(2)]
        tasks = []
        ti_ = 0
        for h in range(H):
            for tt in range(NT):
                nk = 4 * tt + 4
                for kc in range(nk - 1, -1, -1):
                    tasks.append(dict(h=h, tt=tt, kc=kc, first=(kc == nk - 1), last=(kc == 0), dj=kc - 4 * tt,
                                      tile=ti_, w=len(tasks) % NW, zb=len(tasks) % 2, cb=2 + len(tasks) % 3))
                ti_ += 1
        sidx = [0]

        def S1(t):
            h, tt, kc, w_ = t["h"], t["tt"], t["kc"], t["w"]
            hk = h % 2
            if t["first"]:
                if tt == 0:
                    P.dma(kT[hk], KT[h * 128:(h + 1) * 128, :], [], [kTb[hk]], kTb[hk])
                    v3 = vt[hk].rearrange("p (k d) -> p k d", k=NKC)
                    vs = VTM.rearrange("(k p) d -> p k d", p=128)
                    for k0 in range(0, NKC, 8):
                        k1 = min(NKC, k0 + 8)
                        P.dma(v3[:, k0:k1, :], vs[:, k0:k1, h * 128:(h + 1) * 128], [], [vtb[hk]], vtb[hk])
                k = t["tile"] % 2
                k3 = t["tile"] % 3
                P.dma(q32[k], PROJ[h * 128:(h + 1) * 128, tt * TT:(tt + 1) * TT], [], [q32b[k]], q32b[k])
                kb.copy("dve", q16[k3], q32[k], [q32b[k]], [q16b[k3]])
            k3 = t["tile"] % 3
            zb = t["zb"]
            kb.mm([(banks[zb], kT[hk][:, kc * 128:(kc + 1) * 128], q16[k3], True, True)], [kTb[hk], q16b[k3]], [bankbuf[zb]])
            kb.act(E[w_], banks[zb], AF.Exp, [bankbuf[zb]], [Eb_[w_]], scale=scale)
            kb.act(SP_[w_], E[w_], AF.Ln, [Eb_[w_], cstb], [SPb[w_]], bias=onesb[:, 0:1])
            if t["dj"] >= 0:
                kb.tt("dve", SP_[w_], SP_[w_], MASK[t["dj"]], ALU.mult, [SPb[w_], cstb], [SPb[w_]])

        def S2(t):
            w_ = t["w"]
            cbk = t["cb"]
            so, sob = sacc[sidx[0] % 3], saccb[sidx[0] % 3]
            sn_, snb = sacc[(sidx[0] + 1) % 3], saccb[(sidx[0] + 1) % 3]
            if t["first"]:
                kb.mm([(banks[cbk], TRI32, SP_[w_], True, True)], [SPb[w_], cstb], [bankbuf[cbk]])
            else:
                kb.mm([(banks[cbk], TRI32, SP_[w_], True, False), (banks[cbk], ONES32, so, False, True)],
                      [SPb[w_], cstb, sob], [bankbuf[cbk]])
            if not t["last"]:
                if t["first"]:
                    kb.copy("dve", sn_, SP_[w_], [SPb[w_]], [snb])
                else:
                    kb.tt("dve", sn_, so, SP_[w_], ALU.add, [SPb[w_], sob], [snb])
                sidx[0] += 1

        def S3a(t):
            w_ = t["w"]
            cbk = t["cb"]
            kb.act(X[w_], banks[cbk], AF.Exp, [bankbuf[cbk]], [Xb[w_]], scale=-1.0)
            if t["dj"] >= 0:
                kb.tt("dve", X[w_], X[w_], MASK[t["dj"]], ALU.mult, [Xb[w_], cstb], [Xb[w_]])
            kb.tt("dve", W16[w_], E[w_], X[w_], ALU.mult, [Eb_[w_], Xb[w_]], [W16b[w_]])

        def S3b(t):
            h, tt, kc, w_ = t["h"], t["tt"], t["kc"], t["w"]
            hk = h % 2
            obank = 6 + (t["tile"] % 2)
            kb.mm([(banks[obank], vt[hk][:, kc * 128:(kc + 1) * 128], W16[w_], t["first"], t["last"])],
                  [vtb[hk], W16b[w_]], [bankbuf[obank]])
            if t["last"]:
                e_ = t["tile"] % 2
                kb.copy("act", ev[e_], banks[obank], [bankbuf[obank]], [evb[e_]])
                store_tile(MIX, h * 128, tt * TT, TT, ev[e_], evb[e_])

        n = len(tasks)
        for i in range(n + 3):
            if 0 <= i - 3 < n:
                S3a(tasks[i - 3])
            if i < n:
                S1(tasks[i])
            if 0 <= i - 1 < n:
                S2(tasks[i - 1])
            if 0 <= i - 3 < n:
                S3b(tasks[i - 3])

    def dump(nm):
        new_phase()
        src = kb.dr[nm]
        b = Buf()
        rows = src.shape[0]
        for r0 in range(0, rows, 128):
            P.dma(dbg[nm][r0:r0 + 128, :], src[r0:r0 + 128, :], [], [b], b)

    ONES_T = p32.alloc(TT)
    P.op("dve", lambda e: e.memset(ONES_T, 1.0), [], [cstb])
    onesb = p32.alloc(1)
    P.op("dve", lambda e: e.memset(onesb, 1.0), [], [cstb])
    base32 = p32.off

    stages = cfg.stages if hasattr(cfg, "stages") else None

    def want(s):
        return stages is None or s in stages

    if want("cast"):
        cast_weights_dma()
    with nc.allow_low_precision("bf16 matmuls with fp32 accumulation, as the reference tolerance assumes"):
        if want("in0"):
            phase_inproj(0, xT)
        if want("mem0"):
            phase_memattn(0)
        if want("s5"):
            phase_s5()
        if want("glu"):
            phase_glu()
        if want("out0"):
            phase_resid(W["w_out0"], MIX, F32, XC, xT, HA)
        if want("up0"):
            phase_ffn_up(0, HA)
        if want("dn0"):
            phase_resid(W["w_dn0"], HID, BF16, FC, HA, HB)
        if want("kv"):
            phase_kv(HB)
        if want("in1"):
            phase_inproj(1, HB)
        if want("mem1"):
            phase_memattn(1)
        if want("sb"):
            phase_sb()
        if want("out1"):
            phase_resid(W["w_out1"], MIX, F32, XC, HB, HA)
        if want("up1"):
            phase_ffn_up(1, HA)
        if want("dn1"):
            phase_resid(W["w_dn1"], HID, BF16, FC, HA, yT)
        for nm in debug_outs:
            dump(nm)
        P.barrier()
        P.emit()
    es.close()
    return nc


def tile_w(Wm, nblk, col_groups=None):
    Kd, N = Wm.shape
    kc = Kd // 128
    if col_groups is None:
        nt = N // nblk
        x = Wm.reshape(kc, 128, nt, nblk).transpose(2, 1, 0, 3)
    else:
        x = Wm[:, col_groups.reshape(-1)].reshape(kc, 128, col_groups.shape[0], nblk).transpose(2, 1, 0, 3)
        nt = col_groups.shape[0]
    return np.ascontiguousarray(x).reshape(nt * 128, kc * nblk)


def chunkvec(v):
    return np.ascontiguousarray(v.reshape(-1, 128).T)


def host_layout(cfg, inp):
    f = np.float32
    D, KC, FC, NPAIR, SC = cfg.D, cfg.KC, cfg.FC, cfg.NPAIR, cfg.SSM_W // 128
    shared = {}
    gains = [chunkvec(inp["norm_mix_g"][0]), chunkvec(inp["norm_mix_g"][1]), chunkvec(inp["mem_norm_g"][0]),
             chunkvec(inp["mem_norm_g"][1]), chunkvec(inp["norm_ffn_g"][0]), chunkvec(inp["norm_ffn_g"][1]),
             chunkvec(inp["kv_norm_g"])]
    shared["gains"] = np.ascontiguousarray(np.concatenate(gains, axis=1), dtype=f)
    qk = []
    for l in range(2):
        qk += [chunkvec(inp["mem_q_norm_g"][l]), chunkvec(inp["mem_k_norm_g"][l])]
    shared["qkg"] = np.ascontiguousarray(np.concatenate(qk, axis=1), dtype=f)
    cw = inp["ffn_conv_w"]
    cb = inp["ffn_conv_b"]
    cp = np.zeros((128, 2, 2, FC, 4), f)
    for l in range(2):
        for ab in range(2):
            sl = slice(ab * cfg.D_FF, (ab + 1) * cfg.D_FF)
            for i in range(3):
                cp[:, l, ab, :, i] = chunkvec(cw[l, i, sl])
            cp[:, l, ab, :, 3] = chunkvec(cb[l, sl])
    shared["convp"] = cp.reshape(128, -1)
    lam_re, lam_im, ls = inp["s5_lam_re"][0], inp["s5_lam_im"][0], inp["s5_log_step"][0]
    G = cfg.G

    def pairlay(a):
        return np.ascontiguousarray(a.reshape(NPAIR, 2, 64).transpose(1, 2, 0).reshape(128, NPAIR))
    lsb = np.repeat(ls[:, None], 64, axis=1)
    par = np.stack([pairlay(lam_re), pairlay(lam_im), pairlay(lsb)], axis=1)
    shared["s5par"] = np.ascontiguousarray(par.reshape(128, -1), dtype=f)
    rep = np.stack([a.reshape(NPAIR * 128) for a in (lam_re.reshape(NPAIR, 128), lam_im.reshape(NPAIR, 128),
                                                       lsb.reshape(NPAIR, 128))], axis=0)
    shared["s5rep"] = np.ascontiguousarray(np.broadcast_to(rep.reshape(1, -1), (128, 3 * NPAIR * 128)), dtype=f)
    Bp = np.zeros((128, 2, NPAIR, 2, 64), f)
    Cp = np.zeros((128, 2, NPAIR, 128), f)
    for which, (bsrc, csrc) in enumerate(((inp["s5_b_re"][0], inp["s5_c_re"][0]), (inp["s5_b_im"][0], inp["s5_c_im"][0]))):
        for q in range(NPAIR):
            for e in range(2):
                g = 2 * q + e
                r0 = (q % 4) * 32 + e * 16
                Bp[r0:r0 + 16, which, q, e, :] = bsrc[g].T
                Cp[e * 64:(e + 1) * 64, which, q, r0:r0 + 16] = csrc[g].T
    shared["s5B"] = Bp.reshape(128, -1)
    shared["s5C"] = Cp.reshape(128, -1)
    shared["s5D"] = chunkvec(inp["s5_d"][0].reshape(-1)).astype(f)
    tri = (np.arange(128)[:, None] >= np.arange(128)[None, :]).astype(f)
    ones = np.ones((128, 128), f)
    masks = []
    for dj in range(4):
        s = np.arange(128)[:, None] + 128 * dj
        t = np.arange(TT)[None, :]
        masks.append((s < t).astype(f))
    shared["consts"] = np.concatenate([tri, ones] + masks, axis=1)
    for l in range(2):
        shared["w_in%d" % l] = tile_w(inp["w_in"][l], 256)
        shared["w_out%d" % l] = tile_w(inp["w_out"][l], 256)
        shared["w_mkv%d" % l] = tile_w(inp["w_mem_kv"][l], 256)
        cg = np.stack([np.concatenate([np.arange(j * 128, (j + 1) * 128), cfg.D_FF + np.arange(j * 128, (j + 1) * 128)])
                       for j in range(FC)])
        shared["w_up%d" % l] = tile_w(inp["w_ffn_up"][l], 256, cg)
        shared["w_dn%d" % l] = tile_w(inp["w_ffn_down"][l], 128)
    shared["w_glu"] = tile_w(inp["s5_w_glu"][0], 256)
    shared["w_kv"] = tile_w(inp["w_kv_shared"], 256)
    in_maps = []
    for b in range(cfg.B):
        m = dict(shared)
        m["xT"] = np.ascontiguousarray(inp["x"][b].T)
        m["memT"] = np.ascontiguousarray(inp["mem"][b].T)
        in_maps.append(m)
    return in_maps


_CACHE = {}


def run(cfg, inputs, debug_outs=(), trace=False):
    inp = {k: np.asarray(v) for k, v in inputs.items()}
    in_maps = host_layout(cfg, inp)
    key = (id(cfg), tuple(debug_outs))
    nc = build(cfg, debug_outs)
    res = run_bass_kernel_spmd(nc, in_maps, core_ids=list(range(cfg.B)))
    return res


def kernel(**inputs):
    cfg = FULL
    res = run(cfg, inputs)
    out = np.stack([np.ascontiguousarray(res.results[b]["yT"].T) for b in range(cfg.B)], axis=0)
    return out.astype(np.float32)
```
